# Optimizing a Trainium2 kernel written in Bass

```python
import jax, jax.numpy as jnp
from jax import lax
import numpy as np

D_MODEL = 2048
BATCH = 32
SEQ = 256
DEPTH = 2
DEC_BATCH = 4
DEC_SEQ = 1024
PAST_LEN = 256

GRID_W = 64
N_BRANCH = 4
BRANCH_W = D_MODEL // 2
QK_NOPE = 128
QK_ROPE = 64
V_HEAD = 128
N_HEADS = BRANCH_W // V_HEAD
Q_LORA = 512
KV_LORA = 256
ROPE_BASE = 10000.0
Q_BLOCK = 128
SHORT_K = 3
POOL_WINDOWS = (2, 4, 8, 16)
POOL_GROUPS = 4
POOL_GW = BRANCH_W // POOL_GROUPS
CONF_K = 31
EPS = 1e-6
SEG_SIZES = (Q_LORA, KV_LORA + QK_ROPE, BRANCH_W,
             BRANCH_W, BRANCH_W, BRANCH_W, BRANCH_W,
             BRANCH_W, BRANCH_W,
             2 * BRANCH_W, BRANCH_W,
             N_BRANCH * D_MODEL)
IN_COLS = Q_LORA + KV_LORA + QK_ROPE + 10 * BRANCH_W + N_BRANCH * D_MODEL

kernel_name = "hybrid_mla_conv_pool_conformer_diffusion_step"


def rms_norm(x, w):
    xf = x.astype(jnp.float32)
    y = xf * lax.rsqrt(jnp.mean(xf * xf, axis=-1, keepdims=True) + EPS)
    return (y * w.astype(jnp.float32)).astype(x.dtype)


def layer_norm(x, w, b):
    xf = x.astype(jnp.float32)
    mu = jnp.mean(xf, axis=-1, keepdims=True)
    var = jnp.mean(jnp.square(xf - mu), axis=-1, keepdims=True)
    y = (xf - mu) * lax.rsqrt(var + EPS)
    return (y * w.astype(jnp.float32) + b.astype(jnp.float32)).astype(x.dtype)


def depthwise_conv(x, w, b):
    y = lax.conv_general_dilated(x, w[:, None, :].astype(x.dtype), window_strides=(1,), padding='SAME',
                                 dimension_numbers=('NWC', 'WIO', 'NWC'), feature_group_count=x.shape[-1])
    return y + b.astype(x.dtype)


def grid_rope_tables(n_tokens):
    rows = n_tokens // GRID_W
    r, col = jnp.meshgrid(jnp.arange(rows, dtype=jnp.float32), jnp.arange(GRID_W, dtype=jnp.float32), indexing='ij')
    half = QK_ROPE // 2
    inv = ROPE_BASE ** (-jnp.arange(0, half, 2, dtype=jnp.float32) / half)
    ang = jnp.stack([r.reshape(-1)[:, None] * inv, col.reshape(-1)[:, None] * inv], axis=1)
    return jnp.cos(ang), jnp.sin(ang)


def apply_rope(x, cos, sin):
    xr = x.reshape(x.shape[:-1] + (2, 2, QK_ROPE // 4))
    x1, x2 = xr[..., 0, :], xr[..., 1, :]
    cos = cos.astype(x.dtype)
    sin = sin.astype(x.dtype)
    out = jnp.stack([x1 * cos - x2 * sin, x2 * cos + x1 * sin], axis=-2)
    return out.reshape(x.shape)


def block_attention(q, k, v):
    B, Lq, H, Dk = q.shape
    nb = Lq // Q_BLOCK
    scale = Dk ** -0.5
    qb = q.reshape(B, nb, Q_BLOCK, H, Dk).swapaxes(0, 1)

    def one_block(qblk):
        s = jnp.einsum('bqhd,bkhd->bhqk', qblk, k, preferred_element_type=jnp.float32) * scale
        p = jax.nn.softmax(s, axis=-1).astype(v.dtype)
        return jnp.einsum('bhqk,bkhd->bqhd', p, v)

    o = lax.map(one_block, qb)
    return o.swapaxes(0, 1).reshape(B, Lq, H, v.shape[-1])


def multiscale_pool(xp, pool_w, pool_scale):
    B, L, _ = xp.shape
    xf = xp.astype(jnp.float32)
    cs = jnp.concatenate([jnp.zeros((B, 1, BRANCH_W), jnp.float32), jnp.cumsum(xf, axis=1)], axis=1)
    t = jnp.arange(L)
    groups = []
    for g, w in enumerate(POOL_WINDOWS):
        lo = jnp.clip(t - w // 2, 0, L)
        hi = jnp.clip(t - w // 2 + w, 0, L)
        sl = slice(g * POOL_GW, (g + 1) * POOL_GW)
        csg = cs[..., sl]
        mean = (csg[:, hi] - csg[:, lo]) / (hi - lo).astype(jnp.float32)[None, :, None]
        groups.append(mean - xf[..., sl])
    pooled = jnp.stack(groups, axis=2).astype(xp.dtype)
    y = jnp.einsum('blgc,gcd->blgd', pooled, pool_w).reshape(B, L, BRANCH_W)
    return y * pool_scale


def mixer(h, p, ctx_kv, rope):
    B, L, _ = h.shape
    proj = h @ p['w_in']
    split_pts = [int(s) for s in np.cumsum(SEG_SIZES)[:-1]]
    (q_a, kv_a, g_a, b_gate, c_gate, x_conv, g_b, x_pool, g_c, glu_in, g_d, merge_logits) = jnp.split(proj, split_pts, axis=-1)

    q = (rms_norm(q_a, p['q_norm_w']) @ p['w_qb']).reshape(B, L, N_HEADS, QK_NOPE + QK_ROPE)
    q_nope, q_pe = q[..., :QK_NOPE], q[..., QK_NOPE:]
    c_kv = rms_norm(kv_a[..., :KV_LORA], p['kv_norm_w'])
    k_pe = kv_a[..., KV_LORA:]
    if ctx_kv is None:
        new_kv = jnp.concatenate([c_kv, k_pe], axis=-1)
        kv_all = new_kv
    else:
        cos, sin = rope
        q_pe = apply_rope(q_pe, cos[:, None], sin[:, None])
        k_pe = apply_rope(k_pe, cos, sin)
        kv_all = jnp.concatenate([ctx_kv.astype(h.dtype), jnp.concatenate([c_kv, k_pe], axis=-1)], axis=1)
        new_kv = None
    Lk = kv_all.shape[1]
    kv_up = (kv_all[..., :KV_LORA] @ p['w_kvb']).reshape(B, Lk, N_HEADS, QK_NOPE + V_HEAD)
    k_nope, v = kv_up[..., :QK_NOPE], kv_up[..., QK_NOPE:]
    k_rot = jnp.broadcast_to(kv_all[:, :, None, KV_LORA:], (B, Lk, N_HEADS, QK_ROPE))
    k = jnp.concatenate([k_nope, k_rot], axis=-1)
    qf = jnp.concatenate([q_nope, q_pe], axis=-1)
    y_a = block_attention(qf, k, v).reshape(B, L, N_HEADS * V_HEAD) * jax.nn.silu(g_a)

    y_b = b_gate * depthwise_conv(c_gate * x_conv, p['conv3_w'], p['conv3_b']) * jax.nn.silu(g_b)

    y_c = multiscale_pool(x_pool, p['pool_w'], p['pool_scale']) * jax.nn.silu(g_c)

    u = glu_in[..., :BRANCH_W] * jax.nn.sigmoid(glu_in[..., BRANCH_W:])
    u = jax.nn.silu(layer_norm(depthwise_conv(u, p['dw_w'], p['dw_b']), p['cln_w'], p['cln_b']))
    y_d = u * jax.nn.silu(g_d)

    branches = jnp.stack([y_a, y_b, y_c, y_d], axis=2)
    outs = jnp.einsum('blic,icd->blid', branches, p['w_bproj'])
    gates = jax.nn.sigmoid(merge_logits.reshape(B, L, N_BRANCH, D_MODEL))
    merged = jnp.sum(gates * outs, axis=2)
    return merged @ p['w_out'], new_kv


def setup_inputs(seed: int = 0) -> dict:
    key = jax.random.key(seed)
    ks = jax.random.split(key, 26)
    f32 = jnp.float32
    nrm = lambda k, shape, s: jax.random.normal(k, shape, f32) * s
    D, W = D_MODEL, BRANCH_W
    return {
        'x_prompt': nrm(ks[0], (BATCH, SEQ, D), 1.0),
        'x_sample': nrm(ks[1], (DEC_BATCH, DEC_SEQ, D), 1.0),
        'cache_kv': nrm(ks[2], (DEC_BATCH, DEPTH, PAST_LEN, KV_LORA + QK_ROPE), 1.0),
        'c': nrm(ks[3], (DEC_BATCH, D), 1.0),
        'c_ctx': nrm(ks[4], (D,), 1.0),
        'w_ada': nrm(ks[5], (DEPTH, D, 3 * D), 0.5 * D ** -0.5),
        'b_ada': nrm(ks[6], (DEPTH, 3 * D), 0.02),
        'norm_w': 1.0 + nrm(ks[7], (DEPTH, D), 0.05),
        'w_in': nrm(ks[8], (DEPTH, D, IN_COLS), D ** -0.5),
        'q_norm_w': 1.0 + nrm(ks[9], (DEPTH, Q_LORA), 0.05),
        'w_qb': nrm(ks[10], (DEPTH, Q_LORA, N_HEADS * (QK_NOPE + QK_ROPE)), Q_LORA ** -0.5),
        'kv_norm_w': 1.0 + nrm(ks[11], (DEPTH, KV_LORA), 0.05),
        'w_kvb': nrm(ks[12], (DEPTH, KV_LORA, N_HEADS * (QK_NOPE + V_HEAD)), KV_LORA ** -0.5),
        'conv3_w': nrm(ks[13], (DEPTH, SHORT_K, W), SHORT_K ** -0.5),
        'conv3_b': nrm(ks[14], (DEPTH, W), 0.02),
        'pool_w': nrm(ks[15], (DEPTH, POOL_GROUPS, POOL_GW, POOL_GW), POOL_GW ** -0.5),
        'pool_scale': 1.0 + nrm(ks[16], (DEPTH, W), 0.05),
        'dw_w': nrm(ks[17], (DEPTH, CONF_K, W), CONF_K ** -0.5),
        'dw_b': nrm(ks[18], (DEPTH, W), 0.02),
        'cln_w': 1.0 + nrm(ks[19], (DEPTH, W), 0.05),
        'cln_b': nrm(ks[20], (DEPTH, W), 0.02),
        'w_bproj': nrm(ks[21], (DEPTH, N_BRANCH, W, D), W ** -0.5),
        'w_out': nrm(ks[22], (DEPTH, D, D), D ** -0.5),
        'final_norm_w': 1.0 + nrm(ks[23], (D,), 0.05),
    }


def reference(x_prompt, x_sample, cache_kv, c, c_ctx, w_ada, b_ada, norm_w, w_in, q_norm_w, w_qb,
              kv_norm_w, w_kvb, conv3_w, conv3_b, pool_w, pool_scale, dw_w, dw_b, cln_w, cln_b,
              w_bproj, w_out, final_norm_w):
    def layer_params(l):
        return {'w_in': w_in[l], 'q_norm_w': q_norm_w[l], 'w_qb': w_qb[l], 'kv_norm_w': kv_norm_w[l],
                'w_kvb': w_kvb[l], 'conv3_w': conv3_w[l], 'conv3_b': conv3_b[l], 'pool_w': pool_w[l],
                'pool_scale': pool_scale[l], 'dw_w': dw_w[l], 'dw_b': dw_b[l], 'cln_w': cln_w[l],
                'cln_b': cln_b[l], 'w_bproj': w_bproj[l], 'w_out': w_out[l]}

    y_p = x_prompt
    kv_list = []
    for l in range(DEPTH):
        p = layer_params(l)
        shift, scale, gate = jnp.split(jax.nn.silu(c_ctx) @ w_ada[l] + b_ada[l], 3, axis=-1)
        h = rms_norm(y_p, norm_w[l]) * (1 + scale) + shift
        out, kv = mixer(h, p, None, None)
        y_p = y_p + gate * out
        kv_list.append(kv)
    y_prompt = rms_norm(y_p, final_norm_w)
    new_cache_kv = jnp.stack(kv_list, axis=1)

    rope = grid_rope_tables(x_sample.shape[1])
    y_s = x_sample
    for l in range(DEPTH):
        p = layer_params(l)
        mod = (jax.nn.silu(c) @ w_ada[l] + b_ada[l])[:, None, :]
        shift, scale, gate = jnp.split(mod, 3, axis=-1)
        h = rms_norm(y_s, norm_w[l]) * (1 + scale) + shift
        out, _ = mixer(h, p, cache_kv[:, l], rope)
        y_s = y_s + gate * out
    y_sample = rms_norm(y_s, final_norm_w)
    return (y_prompt, y_sample, new_cache_kv)
```

```python
import contextlib
import numpy as np
import concourse.bass as bass
import concourse.mybir as mybir
from concourse.bass_utils import run_bass_kernel_spmd

F32, BF16 = mybir.dt.float32, mybir.dt.bfloat16
AF = mybir.ActivationFunctionType
ALU = mybir.AluOpType
AX = mybir.AxisListType

D = 2048
W = 1024
T = 1024
IN_COLS = 19264
EPS = 1e-6
SEG = dict(qa=0, ckv=512, kpe=768, ga=832, bg=1856, cg=2880, xc=3904, gb=4928, xp=5952, gc=6976,
           glu=8000, gd=10048, mg=11072)
PAD = 16
PV = {}
_o = 0
for _n, _w in (("normw", 16), ("qnw", 4), ("kvnw", 2), ("c3w", 24), ("c3b", 8), ("pscale", 8),
               ("dww", 248), ("dwb", 8), ("clnw", 8), ("clnb", 8)):
    PV[_n] = _o
    _o += _w
NPV = _o


class Ev:
    __slots__ = ("sem", "val")

    def __init__(self, sem, val):
        self.sem, self.val = sem, val


class TObj:
    __slots__ = ("w", "r", "excl")

    def __init__(self, excl=False):
        self.w = {}
        self.r = {}
        self.excl = excl


class View:
    __slots__ = ("ap", "t")

    def __init__(self, ap, t):
        self.ap, self.t = ap, t


class Eng:
    def __init__(self, h, sem, skip_self=False):
        self.h, self.sem, self.count, self.waited, self.skip_self = h, sem, 0, {}, skip_self

    def wait(self, ev):
        if ev.sem is self.sem and self.skip_self:
            return
        k = id(ev.sem)
        if self.waited.get(k, 0) >= ev.val:
            return
        self.h.wait_ge(ev.sem, ev.val)
        self.waited[k] = ev.val


class Queue:
    def __init__(self, eng, sems):
        self.eng, self.sems = eng, sems
        self.cnt = [0] * len(sems)
        self.ev = [None] * len(sems)
        self.n = 0


def _tl(xs):
    out = []
    for x in xs:
        if x is None:
            continue
        if isinstance(x, TObj):
            out.append(x)
        elif isinstance(x, View):
            out.extend(x.t)
        else:
            out.extend(_tl(x))
    return out


class Ctx:
    def __init__(self, nc, es):
        self.nc, self.es = nc, es
        sem = lambda n: es.enter_context(nc.semaphore(n))
        self.pe = Eng(nc.tensor, sem("s_pe"), skip_self=True)
        self.act = Eng(nc.scalar, sem("s_act"))
        self.dve = Eng(nc.vector, sem("s_dve"))
        self.pool = Eng(nc.gpsimd, sem("s_pool"))
        self.sp = Eng(nc.sync, sem("s_sp"))
        self.qs = Queue(self.sp, [sem(f"qs{i}") for i in range(16)])
        self.qg = Queue(self.pool, [sem(f"qg{i}") for i in range(16)])
        self.alt = 0

    def _deps(self, eng, reads, writes):
        need = {}

        def add(ev):
            k = id(ev.sem)
            c = need.get(k)
            if c is None or c.val < ev.val:
                need[k] = ev

        for t in reads:
            for ev in t.w.values():
                add(ev)
            if t.excl:
                for ev in t.r.values():
                    if ev.sem is not eng.sem:
                        add(ev)
        for t in writes:
            for ev in t.w.values():
                add(ev)
            for ev in t.r.values():
                add(ev)
        for ev in need.values():
            eng.wait(ev)

    def _commit(self, ev, reads, writes):
        k = id(ev.sem)
        for t in reads:
            t.r[k] = ev
        for t in writes:
            t.w = {k: ev}
            t.r = {}

    def op(self, eng, reads, writes, emit):
        reads, writes = _tl(reads), _tl(writes)
        self._deps(eng, reads, writes)
        ins = emit()
        eng.count += 1
        ins.then_inc(eng.sem, 1)
        self._commit(Ev(eng.sem, eng.count), reads, writes)

    def dma(self, q, out, in_, reads=(), writes=()):
        reads, writes = _tl(reads), _tl(writes)
        oap, iap = out, in_
        if isinstance(out, View):
            writes = writes + out.t
            oap = out.ap
        if isinstance(in_, View):
            reads = reads + in_.t
            iap = in_.ap
        self._deps(q.eng, reads, writes)
        s = q.n % len(q.sems)
        if q.ev[s] is not None:
            q.eng.wait(q.ev[s])
        q.cnt[s] += 16
        q.eng.h.dma_start(out=oap, in_=iap).then_inc(q.sems[s], 16)
        ev = Ev(q.sems[s], q.cnt[s])
        q.ev[s] = ev
        q.n += 1
        k = id(ev.sem)
        for t in reads:
            t.r[k] = ev
        for t in writes:
            t.w[k] = ev
            t.r = {}

    def mm(self, out, pairs, extra=None):
        reads = [p[0] for p in pairs] + [p[1] for p in pairs]
        n = len(pairs)

        def emit():
            ins = None
            for i, (l, r) in enumerate(pairs):
                ins = self.nc.tensor.matmul(out.ap, lhsT=l.ap, rhs=r.ap, start=(i == 0), stop=(i == n - 1))
            return ins

        self.op(self.pe, reads, [out], emit)

    def tr(self, out, in_, ident):
        self.op(self.pe, [in_, ident], [out], lambda: self.nc.tensor.transpose(out.ap, in_.ap, ident.ap))

    def trs(self, outs_ins, ident):
        def emit():
            ins = None
            for o, i in outs_ins:
                ins = self.nc.tensor.transpose(o.ap, i.ap, ident.ap)
            return ins
        self.op(self.pe, [i for _, i in outs_ins] + [ident], [o for o, _ in outs_ins], emit)

    def actf(self, out, in_, func, bias=None, scale=1.0, accum=None):
        reads = [in_]
        kw = {}
        if isinstance(bias, View):
            reads.append(bias)
            kw["bias"] = bias.ap
        elif bias is not None:
            kw["bias"] = bias
        if isinstance(scale, View):
            reads.append(scale)
            kw["scale"] = scale.ap
        else:
            kw["scale"] = scale
        writes = [out]
        if accum is not None:
            writes.append(accum)
            kw["accum_out"] = accum.ap
        self.op(self.act, reads, writes, lambda: self.nc.scalar.activation(out=out.ap, in_=in_.ap, func=func, **kw))

    def tt(self, out, a, b, op, eng=None):
        eng = eng or self.dve
        self.op(eng, [a, b], [out], lambda: eng.h.tensor_tensor(out=out.ap, in0=a.ap, in1=b.ap, op=op))

    def ts(self, out, a, s1, s2, op0, op1=None, eng=None):
        eng = eng or self.dve
        reads = [a]
        v1 = s1.ap if isinstance(s1, View) else s1
        v2 = s2.ap if isinstance(s2, View) else s2
        if isinstance(s1, View):
            reads.append(s1)
        if isinstance(s2, View):
            reads.append(s2)
        if op1 is None:
            self.op(eng, reads, [out], lambda: eng.h.tensor_scalar(out=out.ap, in0=a.ap, scalar1=v1, scalar2=None, op0=op0))
        else:
            self.op(eng, reads, [out], lambda: eng.h.tensor_scalar(out=out.ap, in0=a.ap, scalar1=v1, scalar2=v2, op0=op0, op1=op1))

    def stt(self, out, a, s, b, op0, op1):
        reads = [a, b]
        sv = s
        if isinstance(s, View):
            reads.append(s)
            sv = s.ap
        self.op(self.dve, reads, [out], lambda: self.nc.vector.scalar_tensor_tensor(out=out.ap, in0=a.ap, scalar=sv, in1=b.ap, op0=op0, op1=op1))

    def copy(self, out, in_, scale=None):
        self.alt ^= 1
        if self.alt:
            self.actf(out, in_, AF.Copy, scale=(1.0 if scale is None else scale))
        elif scale is None:
            self.op(self.dve, [in_], [out], lambda: self.nc.vector.tensor_copy(out=out.ap, in_=in_.ap))
        else:
            self.ts(out, in_, scale, None, ALU.mult)

    def recip(self, out, in_):
        self.op(self.dve, [in_], [out], lambda: self.nc.vector.reciprocal(out=out.ap, in_=in_.ap))

    def memset(self, out, val):
        self.op(self.dve, [], [out], lambda: self.nc.vector.memset(out.ap, val))


class Arena:
    G = 128

    def __init__(self, nc, es, nwords):
        self.ng = nwords // self.G
        self.t = es.enter_context(nc.sbuf_tensor("arena", [128, self.ng * self.G], F32))
        self.gr = [TObj() for _ in range(self.ng)]
        self.free = [True] * self.ng
        self.peak = 0

    def nfree(self):
        return sum(self.free) * self.G * 4

    def alloc(self, nbytes, soft=False):
        n = -(-nbytes // (self.G * 4))
        run = 0
        for i in range(self.ng):
            run = run + 1 if self.free[i] else 0
            if run == n:
                g0 = i - n + 1
                for j in range(g0, i + 1):
                    self.free[j] = False
                self.peak = max(self.peak, self.ng - sum(self.free))
                return Buf(self, g0, n)
        if soft:
            return None
        raise RuntimeError(f"arena full: want {nbytes}, free {self.nfree()}")

    def release(self, b):
        for j in range(b.g0, b.g0 + b.n):
            assert not self.free[j]
            self.free[j] = True


class Buf:
    def __init__(self, arena, g0, n):
        self.a, self.g0, self.n = arena, g0, n

    def free(self):
        self.a.release(self)

    def v(self, dt, w0, w1, pattern=None, parts=128, p0=0, **kw):
        G = self.a.G
        base = self.g0 * G
        if dt == BF16:
            ap = self.a.t[p0:p0 + parts, base:base + self.n * G].bitcast(BF16)[:, w0:w1]
            f0, f1 = w0 // 2, (w1 + 1) // 2
        else:
            ap = self.a.t[p0:p0 + parts, base + w0:base + w1]
            f0, f1 = w0, w1
        if pattern:
            ap = ap.rearrange(pattern, **kw)
        ts = self.a.gr[self.g0 + f0 // G: self.g0 + (f1 - 1) // G + 1]
        return View(ap, ts)


class Chunked:
    def __init__(self, arena, dt, nch, n):
        self.dt, self.nch, self.n = dt, nch, n
        self.b = arena.alloc(nch * n * (2 if dt == BF16 else 4))

    def ch(self, c, t0=0, t1=None, parts=128, pattern=None, p0=0, **kw):
        t1 = self.n if t1 is None else t1
        return self.b.v(self.dt, c * self.n + t0, c * self.n + t1, pattern=pattern, parts=parts, p0=p0, **kw)

    def all(self, c0=0, c1=None, parts=128, p0=0):
        c1 = self.nch if c1 is None else c1
        return self.b.v(self.dt, c0 * self.n, c1 * self.n, pattern="p (c n) -> p c n", parts=parts, p0=p0, c=c1 - c0)

    def free(self):
        self.b.free()


class Psum:
    def __init__(self, nc, es):
        self.t = es.enter_context(nc.psum_tensor("psum", [128, 8 * 512], F32))
        self.banks = [TObj(excl=True) for _ in range(8)]
        self.pos = 0

    def fixed(self, b0, nb=1):
        return PB(self, b0, nb)

    def alloc(self, nb=1):
        if self.pos + nb > 8:
            self.pos = 0
        b0 = self.pos
        self.pos = (self.pos + nb) % 8
        return PB(self, b0, nb)


class PB:
    def __init__(self, ps, b0, nb):
        self.ps, self.b0, self.nb = ps, b0, nb

    def v(self, n0=0, n1=512, parts=128, p0=0, dt=F32, pattern=None, **kw):
        base = self.b0 * 512
        if dt == BF16:
            ap = self.ps.t[p0:p0 + parts, base:base + self.nb * 512].bitcast(BF16)[:, n0:n1]
        else:
            ap = self.ps.t[p0:p0 + parts, base + n0:base + n1]
        if pattern:
            ap = ap.rearrange(pattern, **kw)
        return View(ap, self.ps.banks[self.b0:self.b0 + self.nb])


class Small:
    def __init__(self, nc, es, name, shape, dt):
        self.t = es.enter_context(nc.sbuf_tensor(name, shape, dt))
        self.o = TObj()

    def v(self, *idx):
        ap = self.t[idx] if idx else self.t[:]
        return View(ap, [self.o])


class Stats:
    def __init__(self, nc, es, n=256):
        self.t = es.enter_context(nc.sbuf_tensor("stats", [128, n], F32))
        self.o = [TObj() for _ in range(n)]
        self.i = 0

    def get(self):
        i = self.i
        self.i = (self.i + 1) % len(self.o)
        return View(self.t[:, i:i + 1], [self.o[i]])


class WStream:
    def __init__(self, cx, arena, reserve, spec_of, plan=None):
        self.cx, self.arena, self.reserve, self.spec_of, self.plan = cx, arena, reserve, spec_of, plan
        self.rec = []
        self.issued = []
        self.pos = 0

    def _issue(self, key, soft):
        sap, kch, n = self.spec_of(key)
        nbytes = kch * n * 2
        if soft and self.arena.nfree() - nbytes < self.reserve:
            return False
        b = self.arena.alloc(nbytes, soft=soft)
        if b is None:
            return False
        v = b.v(BF16, 0, kch * n, pattern="p (k n) -> p k n", k=kch)
        self.cx.dma(self.cx.qg, v, sap)
        self.issued.append((b, v))
        return True

    def next(self, key, look=6):
        self.rec.append(key)
        if self.plan is None:
            self._issue(key, soft=False)
        else:
            assert self.plan[self.pos] == key, (self.plan[self.pos], key)
            while len(self.issued) <= self.pos:
                self._issue(self.plan[len(self.issued)], soft=False)
        b, v = self.issued[self.pos]
        self.pos += 1
        if self.plan is not None:
            while len(self.issued) < len(self.plan) and len(self.issued) < self.pos + look:
                if not self._issue(self.plan[len(self.issued)], soft=True):
                    break
        return b, v


def sub(v, ap):
    return View(ap, v.t)


class StopBuild(Exception):
    pass


def build_program(dbg=None, stop=None, plan=None):
    nc = bass.Bass("TRN2", target_bir_lowering=False)
    di = lambda n, s: nc.dram_tensor(n, s, F32, kind="ExternalInput").ap()
    do = lambda n, s: nc.dram_tensor(n, s, F32, kind="ExternalOutput").ap()
    TG = (1024, 512)
    xin = [di("x0", [TG[0], D]), di("x1", [TG[1], D])]
    cache = di("cache", [2, 256, 320])
    condT = di("condT", [128, 32])
    w_ada = di("w_ada", [2, D, 3 * D])
    b_ada = di("b_ada", [2, 3 * D])
    w_in = di("w_in", [2, D, IN_COLS])
    w_qb = di("w_qb", [2, 512, 1536])
    w_kvb = di("w_kvb", [2, 256, 2048])
    pool_w = di("pool_w", [2, 4, 256, 256])
    w_bproj = di("w_bproj", [2, 4, W, D])
    w_out = di("w_out", [2, D, D])
    pvec = di("pvec", [2, 128, NPV])
    fnw = di("fnw", [1, D])
    consts = di("consts", [128, 384])
    rope = di("rope", [4, 64, T])
    invcs = [di("invc0", [4, TG[0]]), di("invc1", [4, TG[1]])]
    kmask = di("kmask", [4, 1280])
    flag = di("flag", [128, 1])
    selrow_h = di("selrow_h", [2, 256])
    qsel = di("qsel", [4, 1024])
    youts = [do("y0", [TG[0], D]), do("y1", [TG[1], D])]
    kv_os = [do("kv_o0", [4, 2, 256, 320]), do("kv_o1", [2, 2, 256, 320])]
    scr = [[nc.dram_tensor(f"scr{l}_{g}", [TG[g], D], F32, kind="Internal").ap() for g in range(2)] for l in range(2)]
    dbg_out = {}

    with contextlib.ExitStack() as es:
        cx = Ctx(nc, es)
        ps = Psum(nc, es)
        st = Stats(nc, es)
        cst = Small(nc, es, "cst", [128, 384], F32)
        identb = Small(nc, es, "identb", [128, 128], BF16)
        pvs = [Small(nc, es, f"pv{l}", [128, NPV], F32) for l in range(2)]
        cond_f = Small(nc, es, "cond_f", [128, 32], F32)
        cond_b = Small(nc, es, "cond_b", [128, 32], BF16)
        modTs = [Small(nc, es, f"modT{l}", [128, 64], F32) for l in range(2)]
        amods = [Small(nc, es, f"amod{l}", [128, 32], F32) for l in range(2)]
        shiftTs = [Small(nc, es, f"shiftT{l}", [128, 32], F32) for l in range(2)]
        gate_d = [nc.dram_tensor(f"gate_scr{l}", [2, D], F32, kind="Internal").ap() for l in range(2)]
        gate_t = [TObj() for _ in range(2)]
        epsT = Small(nc, es, "epsT", [128, 1], F32)
        flagT = Small(nc, es, "flagT", [128, 1], F32)
        nfree = nc.sbuf_bytes_remaining
        arena = Arena(nc, es, (nfree - 2048) // 4)
        def spec_of(key):
            kind, l = key[0], key[1]
            std = lambda src, c0, n: src.rearrange("(k p) n -> p k n", p=128)[:, :, c0:c0 + n]
            if kind == "ada":
                return std(w_ada[l], key[2] * 512, 512), 16, 512
            if kind == "win":
                name = key[3]
                if name == "kpe":
                    return std(w_in[l], SEG["kpe"], 64), 16, 64
                if name == "mg":
                    jp, i = key[4], key[5]
                    return std(w_in[l], SEG["mg"] + i * D + jp * 256, 256), 16, 256
                base = {"glua": SEG["glu"], "glub": SEG["glu"] + 1024}.get(name)
                base = SEG[name] if base is None else base
                return std(w_in[l], base + key[4] * 128, 128), 16, 128
            if kind == "wqb":
                return std(w_qb[l], 0, 1536), 4, 1536
            if kind == "wkvb":
                return std(w_kvb[l], 0, 2048), 2, 2048
            if kind == "poolw":
                return pool_w[l].rearrange("g (k p) d -> p (g k) d", p=128), 8, 256
            if kind == "bp":
                jp, i = key[3], key[4]
                return std(w_bproj[l, i], jp * 256, 256), 8, 256
            if kind == "wout":
                return std(w_out[l], key[3] * 512, 512), 16, 512
            raise KeyError(key)

        ws = WStream(cx, arena, 32 * 1024, spec_of, plan)

        ident = sub(cst.v(), cst.t[:, 0:128])
        ones = sub(cst.v(), cst.t[:, 128:256])
        ident2 = sub(cst.v(), cst.t[0:2, 0:2])

        cx.dma(cx.qs, cst.v(), consts)
        for l in range(2):
            cx.dma(cx.qs, pvs[l].v(), pvec[l])
        cx.dma(cx.qs, cond_f.v(), condT)
        cx.memset(epsT.v(), EPS)
        cx.dma(cx.qs, flagT.v(), flag)
        cx.op(cx.dve, [cst.v()], [identb.v()], lambda: nc.vector.tensor_copy(out=identb.t[:], in_=cst.t[:, 0:128]))
        cx.actf(cond_b.v(), cond_f.v(), AF.Silu)
        selrow = Small(nc, es, "selrow", [2, 256], F32)
        cx.dma(cx.qs, selrow.v(), selrow_h)

        dbgl = []

        def dump(name, view, shape):
            if dbg is not None and name in dbg and view is not None:
                o = nc.dram_tensor("dbg_" + name, shape, view.ap.dtype, kind="ExternalOutput").ap()
                cx.dma(cx.qs, o, view)
                dbg_out[name] = shape
                dbgl.append(name)
            if stop == name:
                raise StopBuild()

        def run_ada(l):
            modT, amod, shiftT = modTs[l], amods[l], shiftTs[l]

            def finish_mod():
                mv = modT.t[:].rearrange("p (c j) -> p j c", j=2)
                for cnd in range(2):
                    tmp = st.get()
                    cx.ts(sub(amod.v(), amod.t[:, cnd * 16:(cnd + 1) * 16]), sub(modT.v(), mv[:, cnd, 16:32]), 1.0, None, ALU.add)
                    cx.tt(sub(amod.v(), amod.t[:, cnd * 16:(cnd + 1) * 16]), sub(amod.v(), amod.t[:, cnd * 16:(cnd + 1) * 16]),
                          sub(pvs[l].v(), pvs[l].t[:, PV["normw"]:PV["normw"] + 16]), ALU.mult)
                    cx.op(cx.dve, [modT.v()], [shiftT.v()],
                          lambda cnd=cnd: nc.vector.tensor_copy(out=shiftT.t[:, cnd * 16:(cnd + 1) * 16], in_=mv[:, cnd, 0:16]))


            for nb in range(12):
                wb, wv = ws.next(("ada", l, nb), look=2)
                pb = ps.alloc()
                cbv = cond_b.v()
                cx.mm(pb.v(0, 512, parts=2),
                      [(sub(cbv, cond_b.t[:, k * 2:k * 2 + 2]), sub(wv, wv.ap[:, k, :])) for k in range(16)])
                wb.free()
                br = arena.alloc(2048)
                brv = br.v(F32, 0, 512, parts=2)
                cx.dma(cx.qs, brv, bass.AP(b_ada.tensor, l * 3 * D + nb * 512, [[0, 2], [1, 512]]))
                if nb < 8:
                    rw = arena.alloc(2048)
                    rwv = rw.v(F32, 0, 512, parts=2)
                    cx.tt(rwv, pb.v(0, 512, parts=2), brv, ALU.add)
                    pt = ps.alloc()
                    for j in range(4):
                        cx.mm(pt.v(j * 2, j * 2 + 2), [(sub(rwv, rwv.ap[:, j * 128:(j + 1) * 128]), ident2)])
                    cx.copy(sub(modT.v(), modT.t[:, nb * 8:nb * 8 + 8]), pt.v(0, 8))
                    rw.free()
                else:
                    cx.tt(brv, pb.v(0, 512, parts=2), brv, ALU.add)
                    cx.dma(cx.qs, gate_d[l][:, (nb - 8) * 512:(nb - 7) * 512], brv, writes=[gate_t[l]])
                br.free()
                if nb == 7:
                    finish_mod()
                if nb < 11:
                    yield
        def phaseN(l, g, xsrc, xsrc_t):
            T = TG[g]
            NT = T // 128
            cnd = g
            amod, shiftT = amods[l], shiftTs[l]
            hT = Chunked(arena, BF16, 16, T)
            xbs = [arena.alloc(8192) for _ in range(3)]
            yield hT

            def front(tt):
                xv = xbs[tt % 3].v(F32, 0, D)
                cx.dma(cx.qs, xv, xsrc[tt * 128:(tt + 1) * 128, :], reads=xsrc_t[tt])
                jb = arena.alloc(4096)
                ss, rs = st.get(), st.get()
                cx.actf(jb.v(BF16, 0, D), xv, AF.Square, accum=ss)
                jb.free()
                cx.actf(rs, ss, AF.Sqrt, bias=epsT.v(), scale=1.0 / D)
                cx.recip(rs, rs)
                cx.actf(xv, xv, AF.Copy, scale=rs)

            def back(tt):
                xv = xbs[tt % 3].v(F32, 0, D)
                for kq in range(4):
                    pb = ps.alloc()
                    cx.trs([(pb.v(j * 128, j * 128 + 128), sub(xv, xv.ap[:, (kq * 4 + j) * 128:(kq * 4 + j + 1) * 128]))
                            for j in range(4)], ident)
                    for j in range(4):
                        k = kq * 4 + j
                        a_ = sub(amod.v(), amod.t[:, cnd * 16 + k:cnd * 16 + k + 1])
                        s_ = sub(shiftT.v(), shiftT.t[:, cnd * 16 + k:cnd * 16 + k + 1])
                        if j % 2 == 0:
                            cx.ts(hT.ch(k, tt * 128, tt * 128 + 128), pb.v(j * 128, j * 128 + 128), a_, s_, ALU.mult, ALU.add)
                        else:
                            cx.actf(hT.ch(k, tt * 128, tt * 128 + 128), pb.v(j * 128, j * 128 + 128), AF.Identity, bias=s_, scale=a_)

            front(0)
            yield None
            for tt in range(NT):
                if tt + 1 < NT:
                    front(tt + 1)
                back(tt)
                yield None
            for b_ in xbs:
                b_.free()
        def run_group(l, g, hT, xsrc, xsrc_t, ydst, ydst_t, kvdst, tick, pre_o, tick_o):
            amod, shiftT = amods[l], shiftTs[l]
            GEN = (g == 0)
            T = TG[g]
            NH, NT = T // 512, T // 128
            cnd = g
            nseq, L = T // 256, 256
            LK = 1280 if GEN else T
            KOFF = 256 if GEN else 0
            pv = pvs[l]
            pcol = lambda name, i: sub(pv.v(), pv.t[:, PV[name] + i:PV[name] + i + 1])

            def proj(wv, M, th, hT):
                pb = ps.alloc()
                cx.mm(pb.v(0, 512, parts=M),
                      [(sub(wv, wv.ap[:, k, 0:M]), hT.ch(k, th * 512, th * 512 + 512)) for k in range(16)])
                return pb

            if l == 0 and g == 0:
                dump("hT", hT.all(), [128, 16, T])
            def rms_feat(src_f32, nch, nfeat, wname, dsts):
                for th in range(NH):
                    pbs = ps.alloc()
                    sqs = []
                    for c in range(nch):
                        sq = arena.alloc(2048)
                        cx.actf(sq.v(F32, 0, 512), src_f32.ch(c, th * 512, th * 512 + 512), AF.Square)
                        sqs.append(sq)
                    cx.mm(pbs.v(), [(ones, sq.v(F32, 0, 512)) for sq in sqs])
                    for sq in sqs:
                        sq.free()
                    rb = arena.alloc(2048)
                    rv = rb.v(F32, 0, 512)
                    cx.actf(rv, pbs.v(), AF.Sqrt, bias=epsT.v(), scale=1.0 / nfeat)
                    cx.recip(rv, rv)
                    for c in range(nch):
                        for d in dsts:
                            d(c, th, src_f32.ch(c, th * 512, th * 512 + 512), rv)
                    rb.free()

            qa = Chunked(arena, F32, 4, T)
            for c in range(4):
                wb, wv = ws.next(("win", l, g, "qa", c))
                for th in range(NH):
                    pb = proj(wv, 128, th, hT)
                    cx.copy(qa.ch(c, th * 512, th * 512 + 512), pb.v())
                wb.free()
            qn = Chunked(arena, BF16, 4, T)
            rms_feat(qa, 4, 512, "qnw", [lambda c, th, a, r: cx.stt(qn.ch(c, th * 512, th * 512 + 512), a, pcol("qnw", c), r, ALU.mult, ALU.mult)])
            qa.free()
            if l == 0 and g == 0:
                dump("qn", qn.all(), [128, 4, T])

            wqb_b, wq = ws.next(("wqb", l, g))
            qTn = Chunked(arena, BF16, 8, T)
            qTp = Chunked(arena, BF16, 8, T)
            cx.memset(qTp.all(p0=64, parts=64), 0.0)
            if GEN:
                cx.dma(cx.qg, qTp.all(p0=64, parts=4), bass.AP(qsel.tensor, 0, [[T, 4], [0, 8], [1, T]]))
            SC = 192.0 ** -0.5
            if GEN:
                rp = arena.alloc(4 * T * 4)
                rpv = [rp.v(F32, i * T, (i + 1) * T, parts=64) for i in range(4)]
                for i in range(4):
                    cx.dma(cx.qs, rpv[i], rope[i])
                wsw_b = arena.alloc(4 * 8 * 64 * 2)
                wsw = wsw_b.v(BF16, 0, 4 * 512, pattern="p (c h d) -> p c h d", c=4, h=8)
                wq4 = wq.ap.rearrange("p c (h d) -> p c h d", h=8)
                for blk, srcb in ((0, 16), (16, 0), (32, 48), (48, 32)):
                    cx.op(cx.dve, [wq], [wsw], lambda blk=blk, srcb=srcb: nc.vector.tensor_copy(
                        out=wsw.ap[:, :, :, blk:blk + 16], in_=wq4[:, :, :, 128 + srcb:128 + srcb + 16]))
            if l == 0 and g == 0:
                dump("wsw", sub(wsw, wsw_b.v(BF16, 0, 2048).ap), [128, 2048])
            for h in range(8):
                if l == 0 and g == 0 and h >= 1:
                    dump("q%dn" % (h - 1), qTn.ch(h - 1), [128, T])
                    dump("q%dp" % (h - 1), qTp.ch(h - 1, parts=64), [64, T])
                for th in range(NH):
                    sl = slice(th * 512, th * 512 + 512)
                    pb = ps.alloc()
                    cx.mm(pb.v(), [(sub(wq, wq.ap[:, c, h * 192:h * 192 + 128]), qn.ch(c, th * 512, th * 512 + 512)) for c in range(4)])
                    cx.copy(qTn.ch(h, th * 512, th * 512 + 512), pb.v(), scale=SC)
                    pb2 = ps.alloc()
                    cx.mm(pb2.v(0, 512, parts=64), [(sub(wq, wq.ap[:, c, h * 192 + 128:h * 192 + 192]), qn.ch(c, th * 512, th * 512 + 512)) for c in range(4)])
                    if not GEN:
                        cx.copy(qTp.ch(h, th * 512, th * 512 + 512, parts=64), pb2.v(0, 512, parts=64), scale=SC)
                    else:
                        pb3 = ps.alloc()
                        cx.mm(pb3.v(0, 512, parts=64), [(sub(wsw, wsw.ap[:, c, h, :]), qn.ch(c, th * 512, th * 512 + 512)) for c in range(4)])
                        t1 = arena.alloc(2048)
                        t2 = arena.alloc(2048)
                        cx.tt(t1.v(F32, 0, 512, parts=64), pb2.v(0, 512, parts=64), sub(rpv[0], rpv[0].ap[:, sl]), ALU.mult)
                        cx.tt(t2.v(F32, 0, 512, parts=64), pb3.v(0, 512, parts=64), sub(rpv[1], rpv[1].ap[:, sl]), ALU.mult)
                        cx.tt(qTp.ch(h, th * 512, th * 512 + 512, parts=64), t1.v(F32, 0, 512, parts=64), t2.v(F32, 0, 512, parts=64), ALU.add)
                        t1.free()
                        t2.free()
            qn.free()
            wqb_b.free()
            if GEN:
                wsw_b.free()
            if l == 0 and g == 0:
                dump("qTn", qTn.all(), [128, 8, T])
                dump("qTp", qTp.all(parts=64), [64, 8, T])

            ckv = Chunked(arena, F32, 2, T)
            for c in range(2):
                wb, wv = ws.next(("win", l, g, "ckv", c))
                for th in range(NH):
                    pb = proj(wv, 128, th, hT)
                    cx.copy(ckv.ch(c, th * 512, th * 512 + 512), pb.v())
                wb.free()
            ckb = Chunked(arena, BF16, 2, LK)
            dsts = [lambda c, th, a, r: cx.stt(ckb.ch(c, KOFF + th * 512, KOFF + th * 512 + 512), a, pcol("kvnw", c), r, ALU.mult, ALU.mult)]
            ckn = Chunked(arena, F32, 2, T)
            dsts.append(lambda c, th, a, r: cx.stt(ckn.ch(c, th * 512, th * 512 + 512), a, pcol("kvnw", c), r, ALU.mult, ALU.mult))
            rms_feat(ckv, 2, 256, "kvnw", dsts)
            ckv.free()
            if l == 0 and g == 0:
                dump("A2a", None, None)
            kpT = Chunked(arena, BF16, 1, LK)
            cx.memset(kpT.all(p0=64, parts=64), 0.0)
            if GEN:
                cx.dma(cx.qg, kpT.ch(0, p0=64, parts=4), kmask)
            wb, wv = ws.next(("win", l, g, "kpe"))
            kpf = Chunked(arena, F32, 1, T)
            if GEN:
                ksw_b = arena.alloc(16 * 64 * 2)
                ksw = ksw_b.v(BF16, 0, 16 * 64, pattern="p (k d) -> p k d", k=16)
                for blk, srcb in ((0, 16), (16, 0), (32, 48), (48, 32)):
                    cx.op(cx.dve, [wv], [ksw], lambda blk=blk, srcb=srcb: nc.vector.tensor_copy(
                        out=ksw.ap[:, :, blk:blk + 16], in_=wv.ap[:, :, srcb:srcb + 16]))
            for th in range(NH):
                sl = slice(th * 512, th * 512 + 512)
                pb = proj(wv, 64, th, hT)
                cx.actf(kpf.ch(0, th * 512, th * 512 + 512, parts=64), pb.v(0, 512, parts=64), AF.Copy)
                if not GEN:
                    cx.op(cx.dve, [pb.v()], [kpT.ch(0, th * 512, th * 512 + 512, parts=64)],
                          lambda th=th, pb=pb: nc.vector.tensor_copy(out=kpT.ch(0, th * 512, th * 512 + 512, parts=64).ap, in_=pb.v(0, 512, parts=64).ap))
                else:
                    pb3 = ps.alloc()
                    cx.mm(pb3.v(0, 512, parts=64), [(sub(ksw, ksw.ap[:, k, :]), hT.ch(k, th * 512, th * 512 + 512)) for k in range(16)])
                    t1 = arena.alloc(2048)
                    t2 = arena.alloc(2048)
                    cx.tt(t1.v(F32, 0, 512, parts=64), pb.v(0, 512, parts=64), sub(rpv[2], rpv[2].ap[:, sl]), ALU.mult)
                    cx.tt(t2.v(F32, 0, 512, parts=64), pb3.v(0, 512, parts=64), sub(rpv[3], rpv[3].ap[:, sl]), ALU.mult)
                    cx.tt(kpT.ch(0, KOFF + th * 512, KOFF + th * 512 + 512, parts=64), t1.v(F32, 0, 512, parts=64), t2.v(F32, 0, 512, parts=64), ALU.add)
                    t1.free()
                    t2.free()
            wb.free()
            if l == 0 and g == 0:
                dump("A2b", None, None)
            if GEN:
                ksw_b.free()
                rp.free()
                for i in range(2):
                    cb_ = arena.alloc(384 * 4)
                    cv = cb_.v(F32, 0, 384)
                    cx.memset(sub(cv, cv.ap[:, 320:384]), 0.0)
                    cx.dma(cx.qs, sub(cv, cv.ap[:, 0:320]), cache[l, i * 128:(i + 1) * 128, :])
                    pb = ps.alloc()
                    cx.trs([(pb.v(0, 128), sub(cv, cv.ap[:, 0:128])), (pb.v(128, 256), sub(cv, cv.ap[:, 128:256])),
                            (pb.v(256, 384), sub(cv, cv.ap[:, 256:384]))], ident)
                    cx.copy(ckb.ch(0, i * 128, i * 128 + 128), pb.v(0, 128))
                    cx.copy(ckb.ch(1, i * 128, i * 128 + 128), pb.v(128, 256))
                    cx.copy(kpT.ch(0, i * 128, i * 128 + 128, parts=64), pb.v(256, 384, parts=64))
                    cb_.free()
            for tt in range(NT):
                pb = ps.alloc()
                cx.trs([(pb.v(0, 128), ckn.ch(0, tt * 128, tt * 128 + 128)),
                        (pb.v(128, 256), ckn.ch(1, tt * 128, tt * 128 + 128))], ident)
                cx.tr(pb.v(256, 320), kpf.ch(0, tt * 128, tt * 128 + 128, parts=64), sub(ident, ident.ap[0:64, 0:64]))
                ob = arena.alloc(320 * 4)
                cx.copy(ob.v(F32, 0, 320), pb.v(0, 320))
                s_, r_ = tt // 2, (tt % 2) * 128
                cx.dma(cx.qs, kvdst[s_, l, r_:r_ + 128, :], ob.v(F32, 0, 320), writes=[kv_t])
                ob.free()
            ckn.free()
            kpf.free()
            if l == 0 and g == 0:
                dump("ckb", ckb.all(), [128, 2, LK])
                dump("kpT", kpT.all(parts=64), [64, 1, LK])

            wkv_b, wkv = ws.next(("wkvb", l, g))
            kTn = Chunked(arena, BF16, 8, LK)
            for h in range(8):
                k0 = 0
                while k0 < LK:
                    n = min(512, LK - k0)
                    pb = ps.alloc()
                    cx.mm(pb.v(0, n), [(sub(wkv, wkv.ap[:, c, h * 256:h * 256 + 128]), ckb.ch(c, k0, k0 + n)) for c in range(2)])
                    cx.copy(kTn.ch(h, k0, k0 + n), pb.v(0, n))
                    k0 += n
            NKT = LK // 128
            Vt = Chunked(arena, BF16, NKT, W)
            wkv4 = wkv.ap.rearrange("p c (h t d) -> p c h t d", h=8, t=2)
            for kt in range(NKT):
                for hv in range(2):
                    pb = ps.alloc()
                    cx.mm(pb.v(0, 512, pattern="p (h d) -> p h d", h=4),
                          [(ckb.ch(c, kt * 128, kt * 128 + 128), sub(wkv, wkv4[:, c, hv * 4:hv * 4 + 4, 1, :])) for c in range(2)])
                    cx.copy(Vt.ch(kt, hv * 512, hv * 512 + 512), pb.v())
            wkv_b.free()
            ckb.free()
            yA = Chunked(arena, BF16, 8, T)
            for c in range(8):
                wb, wv = ws.next(("win", l, g, "ga", c))
                for th in range(NH):
                    pb = proj(wv, 128, th, hT)
                    cx.actf(yA.ch(c, th * 512, th * 512 + 512), pb.v(), AF.Silu)
                wb.free()

            Lk = 1280 if GEN else 256
            nkt = Lk // 128
            QB = 4 if GEN else 2
            units = [(0, 8, 0)] if GEN else [(s * 256, 2, s * 256) for s in range(nseq)]
            NSB = 2
            SBK = 3 if GEN else 1
            Pb = [arena.alloc(Lk * 2) for _ in range(4)]
            PTbs = [Chunked(arena, BF16, nkt, QB * 128) for _ in range(1 if GEN else 2)]
            items = [(qbase, kb0, h, b0, qi) for (qbase, nqb, kb0) in units for h in range(8)
                     for b0 in range(0, nqb, QB) for qi in range(QB)]
            Sbuf = {}

            def stage_s(i):
                qbase, kb0, h, b0, qi = items[i]
                q0 = qbase + (b0 + qi) * 128
                seg = (b0 + qi) // 2
                S = ps.fixed((i % NSB) * SBK, SBK)
                k0 = 0
                while k0 < Lk:
                    n = min(512, Lk - k0)
                    pairs = [(qTn.ch(h, q0, q0 + 128), kTn.ch(h, kb0 + k0, kb0 + k0 + n)),
                             (qTp.ch(h, q0, q0 + 128), kpT.ch(0, kb0 + k0, kb0 + k0 + n))]
                    cx.mm(S.v(k0, k0 + n), pairs)
                    k0 += n
                Sbuf[i] = S

            Pv, Rs = {}, {}

            def stage_x1(i):
                S = Sbuf.pop(i)
                nmx, rsum = st.get(), st.get()
                cx.op(cx.dve, [S.v(0, Lk)], [nmx], lambda S=S, nmx=nmx: nc.vector.tensor_reduce(
                    out=nmx.ap, in_=S.v(0, Lk).ap, axis=AX.X, op=ALU.max, negate=True))
                P = Pb[i % 4].v(BF16, 0, Lk)
                cx.actf(P, S.v(0, Lk), AF.Exp, bias=nmx, accum=rsum)
                Pv[i], Rs[i] = P, rsum

            def stage_nt(i):
                P, rsum = Pv[i], Rs.pop(i)
                cx.recip(rsum, rsum)
                cx.ts(P, P, rsum, None, ALU.mult)
                for k8 in range(0, nkt, 8):
                    n8 = min(8, nkt - k8)
                    pt = ps.fixed(6 if k8 == 0 else 7)
                    cx.trs([(pt.v(j * 128, j * 128 + 128, dt=BF16), sub(P, P.ap[:, (k8 + j) * 128:(k8 + j + 1) * 128])) for j in range(n8)], identb.v())

            def stage_cp(i):
                qbase, kb0, h, b0, qi = items[i]
                Pv.pop(i)
                PTb = PTbs[(i // QB) % len(PTbs)]
                for k8 in range(0, nkt, 8):
                    n8 = min(8, nkt - k8)
                    pt = ps.fixed(6 if k8 == 0 else 7)
                    ptv = pt.v(0, n8 * 128, dt=BF16, pattern="p (k q) -> p k q", k=n8)
                    dstv = PTb.all(k8, k8 + n8)
                    cx.copy(sub(dstv, dstv.ap[:, :, qi * 128:(qi + 1) * 128]), ptv)
                if qi == QB - 1:
                    nq = QB * 128
                    t0 = qbase + b0 * 128
                    po = ps.fixed(7)
                    cx.mm(po.v(0, nq), [(Vt.ch((kb0 // 128) + kt, h * 128, h * 128 + 128), PTb.ch(kt, 0, nq)) for kt in range(nkt)])
                    cx.tt(yA.ch(h, t0, t0 + nq), po.v(0, nq), yA.ch(h, t0, t0 + nq), ALU.mult)

            n_it = len(items)
            stage_s(0)
            for i in range(n_it + 2):
                if i + 1 < n_it:
                    stage_s(i + 1)
                if i < n_it:
                    stage_x1(i)
                if 0 <= i - 2 < n_it:
                    stage_cp(i - 2)
                if 0 <= i - 1 < n_it:
                    stage_nt(i - 1)
            for b_ in Pb + PTbs + [qTn, qTp, kpT, kTn, Vt]:
                b_.free()
            if l == 0 and g == 0:
                dump("yA", yA.all(), [128, 8, T])

            WP = L + 2 * PAD

            def padv(buf, d, th=None):
                v = buf.v(F32, 0, nseq * WP, pattern="p (s l) -> p s l", s=nseq)
                if th is None:
                    return sub(v, v.ap[:, :, PAD + d:PAD + d + L])
                return sub(v, v.ap[:, 2 * th:2 * th + 2, PAD + d:PAD + d + L])

            def hv(view):
                return sub(view, view.ap.rearrange("p (s l) -> p s l", s=2))

            def fullv(view):
                return sub(view, view.ap.rearrange("p (s l) -> p s l", s=nseq))

            def halo(buf):
                if not GEN:
                    return
                v = buf.v(F32, 0, nseq * WP, pattern="p (s l) -> p s l", s=nseq)
                cx.ts(sub(v, v.ap[:, 0:nseq - 1, PAD + L:PAD + L + PAD]), sub(v, v.ap[:, 1:nseq, PAD:PAD + PAD]), flagT.v(), None, ALU.mult)
                cx.ts(sub(v, v.ap[:, 1:nseq, 0:PAD]), sub(v, v.ap[:, 0:nseq - 1, L:L + PAD]), flagT.v(), None, ALU.mult)

            def newpad():
                b = arena.alloc(nseq * WP * 4)
                cx.memset(b.v(F32, 0, nseq * WP), 0.0)
                return b

            yD = Chunked(arena, BF16, 8, T)
            vv = Chunked(arena, F32, 8, T)
            ups = [newpad(), newpad()]
            NPE = 24
            upbs = [arena.alloc(nseq * WP * 2) for _ in range(2)]
            dgs = [arena.alloc(NPE * 128 * 2) for _ in range(2)]

            def d_stage1(c):
                tick()
                up = ups[c % 2]
                wb, wv = ws.next(("win", l, g, "glub", c))
                sgb = arena.alloc(T * 4)
                for th in range(NH):
                    pb = proj(wv, 128, th, hT)
                    cx.actf(sgb.v(F32, th * 512, th * 512 + 512), pb.v(), AF.Sigmoid)
                wb.free()
                wb, wv = ws.next(("win", l, g, "glua", c))
                for th in range(NH):
                    pb = proj(wv, 128, th, hT)
                    cx.tt(padv(up, 0, th), hv(pb.v()), hv(sgb.v(F32, th * 512, th * 512 + 512)), ALU.mult)
                wb.free()
                sgb.free()
                halo(up)
                wb, wv = ws.next(("win", l, g, "gd", c))
                for th in range(NH):
                    pb = proj(wv, 128, th, hT)
                    cx.actf(yD.ch(c, th * 512, th * 512 + 512), pb.v(), AF.Silu)
                wb.free()
                upb = upbs[c % 2]
                cx.actf(upb.v(BF16, 0, nseq * WP), up.v(F32, 0, nseq * WP), AF.Copy)
                dgv = dgs[c % 2].v(BF16, 0, NPE * 128, pattern="p (k j) -> p k j", k=NPE)
                for k in range(NPE):
                    cx.actf(sub(dgv, dgv.ap[:, k, :]), identb.v(), AF.Copy, scale=pcol("dww", c * 31 + k))

            def d_stage2(c):
                up = ups[c % 2]
                dgv = dgs[c % 2].v(BF16, 0, NPE * 128, pattern="p (k j) -> p k j", k=NPE)
                ubv = upbs[c % 2].v(BF16, 0, nseq * WP, pattern="p (s l) -> p s l", s=nseq)
                for th in range(NH):
                    acc = ps.alloc()
                    av = hv(acc.v())
                    cx.mm(av, [(sub(dgv, dgv.ap[:, k, :]), sub(ubv, ubv.ap[:, 2 * th:2 * th + 2, PAD + k - 15:PAD + k - 15 + L]))
                               for k in range(NPE)])
                    for k in range(NPE, 31):
                        cx.stt(av, padv(up, k - 15, th), pcol("dww", c * 31 + k), av, ALU.mult, ALU.add)
                    cx.actf(vv.ch(c, th * 512, th * 512 + 512), acc.v(), AF.Identity, bias=pcol("dwb", c))

            d_stage1(0)
            for c in range(8):
                if c + 1 < 8:
                    d_stage1(c + 1)
                d_stage2(c)
            for u_ in ups + upbs + dgs:
                u_.free()
            for th in range(NH):
                sl = (th * 512, th * 512 + 512)
                p1, p2 = ps.alloc(), ps.alloc()
                cx.mm(p1.v(), [(ones, vv.ch(c, *sl)) for c in range(8)])
                sqs = []
                for c in range(8):
                    sq = arena.alloc(2048)
                    cx.actf(sq.v(F32, 0, 512), vv.ch(c, *sl), AF.Square)
                    sqs.append(sq)
                cx.mm(p2.v(), [(ones, sq.v(F32, 0, 512)) for sq in sqs])
                for sq in sqs:
                    sq.free()
                mb, rb = arena.alloc(2048), arena.alloc(2048)
                mean, rstd = mb.v(F32, 0, 512), rb.v(F32, 0, 512)
                cx.actf(mean, p1.v(), AF.Copy, scale=1.0 / W)
                cx.tt(rstd, mean, mean, ALU.mult)
                cx.stt(rstd, p2.v(), 1.0 / W, rstd, ALU.mult, ALU.subtract)
                cx.actf(rstd, rstd, AF.Sqrt, bias=epsT.v(), scale=1.0)
                cx.recip(rstd, rstd)
                for c in range(8):
                    t1 = arena.alloc(2048)
                    tv1 = t1.v(F32, 0, 512)
                    cx.tt(tv1, vv.ch(c, *sl), mean, ALU.subtract)
                    cx.tt(tv1, tv1, rstd, ALU.mult)
                    cx.actf(tv1, tv1, AF.Silu, bias=pcol("clnb", c), scale=pcol("clnw", c))
                    cx.tt(yD.ch(c, *sl), tv1, yD.ch(c, *sl), ALU.mult)
                    t1.free()
                mb.free()
                rb.free()
            vv.free()
            if l == 0 and g == 0:
                dump("yD", yD.all(), [128, 8, T])

            yB = Chunked(arena, BF16, 8, T)
            cxp = newpad()
            for c in range(8):
                tick()
                bgs = arena.alloc(T * 4)
                gbs = arena.alloc(T * 4)
                for s_ in ("bg", "gb", "cg", "xc"):
                    wb, wv = ws.next(("win", l, g, s_, c))
                    for th in range(NH):
                        pb = proj(wv, 128, th, hT)
                        sl = (th * 512, th * 512 + 512)
                        if s_ == "bg":
                            cx.actf(bgs.v(F32, *sl), pb.v(), AF.Copy)
                        elif s_ == "gb":
                            cx.actf(gbs.v(F32, *sl), pb.v(), AF.Silu)
                        elif s_ == "cg":
                            if th == 0:
                                cgs = arena.alloc(T * 4)
                            cx.actf(cgs.v(F32, *sl), pb.v(), AF.Copy)
                        else:
                            cx.tt(padv(cxp, 0, th), hv(pb.v()), hv(cgs.v(F32, *sl)), ALU.mult)
                    wb.free()
                cgs.free()
                halo(cxp)
                acc = arena.alloc(T * 4)
                av = fullv(acc.v(F32, 0, T))
                cx.ts(av, padv(cxp, -1), pcol("c3w", c * 3 + 0), pcol("c3b", c), ALU.mult, ALU.add)
                cx.stt(av, padv(cxp, 0), pcol("c3w", c * 3 + 1), av, ALU.mult, ALU.add)
                cx.stt(av, padv(cxp, 1), pcol("c3w", c * 3 + 2), av, ALU.mult, ALU.add)
                cx.tt(acc.v(F32, 0, T), acc.v(F32, 0, T), bgs.v(F32, 0, T), ALU.mult)
                cx.tt(yB.ch(c), acc.v(F32, 0, T), gbs.v(F32, 0, T), ALU.mult)
                for b_ in (acc, bgs, gbs):
                    b_.free()
            cxp.free()
            if l == 0 and g == 0:
                dump("yB", yB.all(), [128, 8, T])

            yC = Chunked(arena, BF16, 8, T)
            pw_b, pw = ws.next(("poolw", l, g))
            icb = arena.alloc(4 * T * 4)
            icv = icb.v(F32, 0, 4 * T, pattern="p (w t) -> p w t", w=4)
            cx.dma(cx.qs, icv, bass.AP(invcs[g].tensor, 0, [[0, 128], [T, 4], [1, T]]))
            bufAs = [newpad(), newpad(), newpad()]
            bufB, bufC = newpad(), newpad()
            NW = nseq * WP
            pooleds, gcss = {}, {}

            def c_stage1(c):
                gi, cc = c // 2, c % 2
                tick()
                if cc == 0:
                    pooleds[gi] = Chunked(arena, BF16, 2, T)
                    gcss[gi] = Chunked(arena, F32, 2, T)
                bufA = bufAs[c % 3]
                for s_ in ("xp", "gc"):
                    wb, wv = ws.next(("win", l, g, s_, c))
                    for th in range(NH):
                        pb = proj(wv, 128, th, hT)
                        if s_ == "xp":
                            cx.actf(padv(bufA, 0, th), hv(pb.v()), AF.Copy)
                        else:
                            cx.actf(gcss[gi].ch(cc, th * 512, th * 512 + 512), pb.v(), AF.Silu)
                    wb.free()

            def c_stage2(c):
                gi, cc = c // 2, c % 2
                bufA = bufAs[c % 3]
                halo(bufA)
                fa = lambda b, a0, a1: b.v(F32, a0, a1)
                cx.tt(fa(bufB, 1, NW), fa(bufA, 0, NW - 1), fa(bufA, 1, NW), ALU.add)
                cur, oth, half = bufB, bufC, 1
                lo = 1
                for step in range(gi):
                    lo2 = lo + half
                    cx.tt(fa(oth, lo2, NW - lo2), fa(cur, lo2 - half, NW - lo2 - half), fa(cur, lo2 + half, NW - lo2 + half), ALU.add)
                    cur, oth = oth, cur
                    lo = lo2
                    half *= 2
                tmp = arena.alloc(T * 4)
                tv = fullv(tmp.v(F32, 0, T))
                cx.tt(tv, padv(cur, 0), fullv(sub(icv, icv.ap[:, gi, :])), ALU.mult)
                cx.tt(fullv(pooleds[gi].ch(cc)), tv, padv(bufA, 0), ALU.subtract)
                tmp.free()

            def c_stage3(gi):
                pooled, gcs = pooleds.pop(gi), gcss.pop(gi)
                for dc in range(2):
                    for th in range(NH):
                        pb = ps.alloc()
                        cx.mm(pb.v(), [(sub(pw, pw.ap[:, gi * 2 + kc, dc * 128:(dc + 1) * 128]), pooled.ch(kc, th * 512, th * 512 + 512)) for kc in range(2)])
                        cx.stt(yC.ch(gi * 2 + dc, th * 512, th * 512 + 512), pb.v(), pcol("pscale", gi * 2 + dc), gcs.ch(dc, th * 512, th * 512 + 512), ALU.mult, ALU.mult)
                pooled.free()
                gcs.free()

            c_stage1(0)
            c_stage1(1)
            for c in range(8):
                c_stage2(c)
                if c + 2 < 8:
                    c_stage1(c + 2)
                if c % 2 == 1:
                    c_stage3(c // 2)
            for b_ in bufAs + [bufB, bufC, icb, pw_b]:
                b_.free()
            if l == 0 and g == 0:
                dump("yC", yC.all(), [128, 8, T])

            ys = [yA, yB, yC, yD]
            mg = Chunked(arena, BF16, 16, T)
            for jp in range(8):
                wl = []
                for i in range(4):
                    wl.append((ws.next(("win", l, g, "mg", jp, i), look=6), ws.next(("bp", l, g, jp, i), look=6)))
                for jj in range(2):
                    j = jp * 2 + jj
                    cs = slice(jj * 128, (jj + 1) * 128)
                    for th in range(NH):
                        sl = (th * 512, th * 512 + 512)
                        pbufs = []
                        for i in range(4):
                            (gb_, gw), (bb_, bw) = wl[i]
                            pg = ps.alloc()
                            cx.mm(pg.v(), [(sub(gw, gw.ap[:, k, cs]), hT.ch(k, *sl)) for k in range(16)])
                            po = ps.alloc()
                            cx.mm(po.v(), [(sub(bw, bw.ap[:, c, cs]), ys[i].ch(c, *sl)) for c in range(8)])
                            sg = arena.alloc(2048)
                            cx.actf(sg.v(F32, 0, 512), pg.v(), AF.Sigmoid)
                            cx.tt(sg.v(F32, 0, 512), po.v(), sg.v(F32, 0, 512), ALU.mult)
                            pbufs.append(sg)
                        cx.tt(pbufs[0].v(F32, 0, 512), pbufs[0].v(F32, 0, 512), pbufs[1].v(F32, 0, 512), ALU.add)
                        cx.tt(pbufs[2].v(F32, 0, 512), pbufs[2].v(F32, 0, 512), pbufs[3].v(F32, 0, 512), ALU.add)
                        cx.tt(mg.ch(j, *sl), pbufs[0].v(F32, 0, 512), pbufs[2].v(F32, 0, 512), ALU.add)
                        for b_ in pbufs:
                            b_.free()
                for (gb_, _), (bb_, _) in wl:
                    gb_.free()
                    bb_.free()
            for b_ in (hT, yA, yB, yC, yD):
                b_.free()
            if l == 0 and g == 0:
                dump("mg", mg.all(), [128, 16, T])

            pre_o()
            gbc = arena.alloc(D * 4)
            cx.dma(cx.qs, gbc.v(F32, 0, D), bass.AP(gate_d[l].tensor, cnd * D, [[0, 128], [1, D]]), reads=[gate_t[l]])
            xrs = [[arena.alloc(2048) for _ in range(NT)] for _ in range(2)]
            n_need, n_done = 10, [0]

            def load_res(cb):
                for tt in range(NT):
                    cx.dma(cx.qs, xrs[cb % 2][tt].v(F32, 0, 512), xsrc[tt * 128:(tt + 1) * 128, cb * 512:(cb + 1) * 512], reads=xsrc_t[tt])

            load_res(0)
            for cb in range(4):
                wb, wv = ws.next(("wout", l, g, cb), look=3)
                if cb + 1 < 4:
                    load_res(cb + 1)
                for tt in range(NT):
                    pb = ps.alloc()
                    cx.mm(pb.v(), [(mg.ch(j, tt * 128, tt * 128 + 128), sub(wv, wv.ap[:, j, :])) for j in range(16)])
                    yo = arena.alloc(2048)
                    xr = xrs[cb % 2][tt]
                    cx.tt(yo.v(F32, 0, 512), pb.v(), gbc.v(F32, cb * 512, cb * 512 + 512), ALU.mult)
                    cx.tt(yo.v(F32, 0, 512), yo.v(F32, 0, 512), xr.v(F32, 0, 512), ALU.add)
                    cx.dma(cx.qs, ydst[tt * 128:(tt + 1) * 128, cb * 512:(cb + 1) * 512], yo.v(F32, 0, 512), writes=[ydst_t[tt][cb]])
                    yo.free()
                    step_o = cb * NT + tt + 1
                    while n_done[0] * (4 * NT) < step_o * n_need:
                        tick_o()
                        n_done[0] += 1
                wb.free()
            for r_ in xrs[0] + xrs[1]:
                r_.free()
            gbc.free()
            mg.free()

        def run_final(g, src, src_t, dst, out_t):
            NT = TG[g] // 128
            fb = arena.alloc(D * 4)
            fv = fb.v(F32, 0, D)
            cx.dma(cx.qs, fv, bass.AP(fnw.tensor, 0, [[0, 128], [1, D]]))
            xbs = [arena.alloc(8192) for _ in range(3)]

            def load(tt):
                cx.dma(cx.qs, xbs[tt % 3].v(F32, 0, D), src[tt * 128:(tt + 1) * 128, :], reads=src_t[tt])

            load(0)
            for tt in range(NT):
                if tt + 1 < NT:
                    load(tt + 1)
                xv = xbs[tt % 3].v(F32, 0, D)
                jb = arena.alloc(4096)
                ss = st.get()
                cx.actf(jb.v(BF16, 0, D), xv, AF.Square, accum=ss)
                jb.free()
                cx.actf(ss, ss, AF.Sqrt, bias=epsT.v(), scale=1.0 / D)
                cx.recip(ss, ss)
                cx.stt(xv, xv, ss, fv, ALU.mult, ALU.mult)
                cx.dma(cx.qs, dst[tt * 128:(tt + 1) * 128, :], xv, writes=[out_t])
                yield
            for b_ in xbs + [fb]:
                b_.free()

        kv_t = TObj()
        out_t = TObj()
        scr_t = [[[[TObj() for _ in range(4)] for _ in range(8)] for _ in range(2)] for _ in range(2)]
        none_t = [[] for _ in range(8)]

        def drain(gen):
            for _ in gen:
                pass

        def srcs(l, g):
            return (xin[g], none_t) if l == 0 else (scr[0][g], scr_t[0][g])

        try:
            ada0, ada1 = run_ada(0), run_ada(1)
            for _ in range(8):
                next(ada0)
            order = [(0, 0), (0, 1), (1, 0), (1, 1)]
            gN = phaseN(0, 0, *srcs(0, 0))
            hT = next(gN)
            drain(gN)
            fin0 = None
            for idx, (l, g) in enumerate(order):
                bg_gens = {(0, 0): [ada0], (0, 1): [ada1], (1, 0): [], (1, 1): []}[(l, g)]
                if (l, g) == (1, 1):
                    fin0 = run_final(0, scr[1][0], scr_t[1][0], youts[0], out_t)
                    bg_gens = [fin0]

                def tick(bg_gens=bg_gens):
                    for gen in bg_gens:
                        if next(gen, "done") != "done":
                            return

                nxt = {}

                def pre_o(idx=idx, bg_gens=bg_gens, nxt=nxt):
                    for gen in bg_gens:
                        drain(gen)
                    if idx + 1 < len(order):
                        ln, gn = order[idx + 1]
                        nxt["gen"] = phaseN(ln, gn, *srcs(ln, gn))
                        nxt["hT"] = next(nxt["gen"])

                def tick_o(nxt=nxt):
                    if "gen" in nxt:
                        next(nxt["gen"], None)

                src, src_t = srcs(l, g)
                run_group(l, g, hT, src, src_t, scr[l][g], scr_t[l][g], kv_os[g], tick, pre_o, tick_o)
                if "gen" in nxt:
                    drain(nxt["gen"])
                    hT = nxt["hT"]
            drain(run_final(1, scr[1][1], scr_t[1][1], youts[1], out_t))
        except StopBuild:
            pass
        for t in [kv_t, out_t]:
            for ev in t.w.values():
                cx.sp.wait(ev)
        for q in (cx.qs, cx.qg):
            for ev in q.ev:
                if ev is not None:
                    cx.sp.wait(ev)
        build_program.peak = arena.peak * Arena.G * 4
        build_program.plan = list(ws.rec)
        build_program.counts = dict(pe=cx.pe.count, act=cx.act.count, dve=cx.dve.count, pool=cx.pool.count, qs=cx.qs.n, qg=cx.qg.n, qs_cnt=cx.qs.cnt, qg_cnt=cx.qg.cnt)
    return nc, dbg_out


def _host_consts():
    consts = np.zeros((128, 384), np.float32)
    consts[:, 0:128] = np.eye(128, dtype=np.float32)
    consts[:, 128:256] = 1.0
    return consts


def _rope_tables(identity):
    n = 1024
    sc = np.float32(192.0 ** -0.5)
    if identity:
        one, zero = np.ones((64, n), np.float32), np.zeros((64, n), np.float32)
        return np.stack([one * sc, zero, one, zero]).astype(np.float32)
    pos = np.arange(n)
    r = (pos // 64).astype(np.float32)
    c = (pos % 64).astype(np.float32)
    inv = (np.float32(10000.0) ** (-np.arange(0, 32, 2, dtype=np.float32) / np.float32(32))).astype(np.float32)
    ang = np.stack([r[:, None] * inv, c[:, None] * inv], axis=1).astype(np.float32)
    cos, sin = np.cos(ang).astype(np.float32), np.sin(ang).astype(np.float32)
    cosT = np.zeros((64, n), np.float32)
    sinT = np.zeros((64, n), np.float32)
    for a in range(2):
        for hf in range(2):
            rows = slice(a * 32 + hf * 16, a * 32 + hf * 16 + 16)
            cosT[rows] = cos[:, a, :].T
            sinT[rows] = (-sin[:, a, :].T) if hf == 0 else sin[:, a, :].T
    return np.stack([cosT * sc, sinT * sc, cosT, sinT]).astype(np.float32)


def _invc(L, n):
    out = np.zeros((4, n), np.float32)
    t = np.arange(L)
    for wi, w in enumerate((2, 4, 8, 16)):
        lo = np.clip(t - w // 2, 0, L)
        hi = np.clip(t - w // 2 + w, 0, L)
        v = (1.0 / (hi - lo).astype(np.float32)).astype(np.float32)
        out[wi] = np.tile(v, n // L)
    return out


def _kmask(separate):
    m = np.zeros((4, 1280), np.float32)
    if separate:
        m[:] = -30000.0
        for s in range(4):
            m[s, 256 + s * 256:256 + (s + 1) * 256] = 0.0
    return m


def _pvec(norm_w, q_norm_w, kv_norm_w, conv3_w, conv3_b, pool_scale, dw_w, dw_b, cln_w, cln_b):
    out = np.zeros((2, 128, NPV), np.float32)
    fm = lambda v: np.ascontiguousarray(v.reshape(-1, 128).T)
    for l in range(2):
        o = out[l]
        o[:, PV["normw"]:PV["normw"] + 16] = fm(norm_w[l])
        o[:, PV["qnw"]:PV["qnw"] + 4] = fm(q_norm_w[l])
        o[:, PV["kvnw"]:PV["kvnw"] + 2] = fm(kv_norm_w[l])
        o[:, PV["c3w"]:PV["c3w"] + 24] = conv3_w[l].reshape(3, 8, 128).transpose(2, 1, 0).reshape(128, 24)
        o[:, PV["c3b"]:PV["c3b"] + 8] = fm(conv3_b[l])
        o[:, PV["pscale"]:PV["pscale"] + 8] = fm(pool_scale[l])
        o[:, PV["dww"]:PV["dww"] + 248] = dw_w[l].reshape(31, 8, 128).transpose(2, 1, 0).reshape(128, 248)
        o[:, PV["dwb"]:PV["dwb"] + 8] = fm(dw_b[l])
        o[:, PV["clnw"]:PV["clnw"] + 8] = fm(cln_w[l])
        o[:, PV["clnb"]:PV["clnb"] + 8] = fm(cln_b[l])
    return out


_CACHE = {}


def kernel(x_prompt, x_sample, cache_kv, c, c_ctx, w_ada, b_ada, norm_w, w_in, q_norm_w, w_qb, kv_norm_w, w_kvb,
           conv3_w, conv3_b, pool_w, pool_scale, dw_w, dw_b, cln_w, cln_b, w_bproj, w_out, final_norm_w, _dbg=None, _stop=None, _ncores=8):
    A = lambda v: np.ascontiguousarray(np.asarray(v, dtype=np.float32))
    x_prompt, x_sample, cache_kv, c, c_ctx = map(A, (x_prompt, x_sample, cache_kv, c, c_ctx))
    key = (tuple(sorted(_dbg)) if _dbg else None, _stop)
    if key not in _CACHE:
        build_program(_dbg, _stop)
        _CACHE[key] = build_program(_dbg, _stop, build_program.plan)
    nc, dbg_out = _CACHE[key]
    consts = _host_consts()
    sel = np.zeros((2, 256), np.float32)
    sel[0, 0:128] = 1.0
    sel[1, 128:256] = 1.0
    qsel = np.zeros((4, 1024), np.float32)
    for r_ in range(4):
        qsel[r_, r_ * 256:(r_ + 1) * 256] = 1.0
    shared = dict(w_ada=A(w_ada), b_ada=A(b_ada), w_in=A(w_in), w_qb=A(w_qb), w_kvb=A(w_kvb), pool_w=A(pool_w),
                  w_bproj=A(w_bproj), w_out=A(w_out),
                  pvec=_pvec(*map(A, (norm_w, q_norm_w, kv_norm_w, conv3_w, conv3_b, pool_scale, dw_w, dw_b, cln_w, cln_b))),
                  fnw=A(final_norm_w).reshape(1, D), consts=consts, selrow_h=sel, invc1=_invc(256, 512), qsel=qsel)
    rope_s, rope_i = _rope_tables(False), _rope_tables(True)
    invc_s, invc_p = _invc(1024, 1024), _invc(256, 1024)
    km_s, km_p = _kmask(False), _kmask(True)
    g0_prompts = {i: list(range(8 + 4 * (i - 4), 12 + 4 * (i - 4))) for i in range(4, 8)}
    g1_prompts = {i: ([2 * i, 2 * i + 1] if i < 4 else [24 + 2 * (i - 4), 25 + 2 * (i - 4)]) for i in range(8)}
    in_maps = []
    for i in range(_ncores):
        m = dict(shared)
        if i < 4:
            cond0 = c[i]
            m["x0"] = np.ascontiguousarray(x_sample[i])
            m["cache"] = np.ascontiguousarray(cache_kv[i])
            m["rope"], m["invc0"], m["kmask"] = rope_s, invc_s, km_s
            m["flag"] = np.ones((128, 1), np.float32)
        else:
            cond0 = c_ctx
            m["x0"] = np.ascontiguousarray(x_prompt[g0_prompts[i]].reshape(1024, D))
            m["cache"] = np.zeros((2, 256, 320), np.float32)
            m["rope"], m["invc0"], m["kmask"] = rope_i, invc_p, km_p
            m["flag"] = np.zeros((128, 1), np.float32)
        m["x1"] = np.ascontiguousarray(x_prompt[g1_prompts[i]].reshape(512, D))
        cond = np.stack([cond0, c_ctx], axis=1)
        m["condT"] = np.ascontiguousarray(cond.reshape(16, 128, 2).transpose(1, 0, 2).reshape(128, 32))
        in_maps.append(m)
    res = run_bass_kernel_spmd(nc, in_maps, core_ids=list(range(_ncores)))
    R = res.results
    if _dbg:
        kernel.dbg = {n: [R[i]["dbg_" + n] for i in range(_ncores)] for n in dbg_out}
        if _stop:
            return None
    if _ncores < 8:
        return R
    y_prompt = np.zeros((32, 256, D), np.float32)
    y_sample = np.zeros((4, 1024, D), np.float32)
    new_kv = np.zeros((32, 2, 256, 320), np.float32)
    for i in range(8):
        y_prompt[g1_prompts[i]] = R[i]["y1"].reshape(2, 256, D)
        new_kv[g1_prompts[i]] = R[i]["kv_o1"]
        if i < 4:
            y_sample[i] = R[i]["y0"]
        else:
            y_prompt[g0_prompts[i]] = R[i]["y0"].reshape(4, 256, D)
            new_kv[g0_prompts[i]] = R[i]["kv_o0"]
    return (y_prompt.astype(np.float32), y_sample.astype(np.float32), new_kv.astype(np.float32))
```

```python
import contextlib
import numpy as np
import concourse.bass as bass
import concourse.mybir as mybir
from concourse.bass_utils import run_bass_kernel_spmd

F32, BF16 = mybir.dt.float32, mybir.dt.bfloat16
AF = mybir.ActivationFunctionType
ALU = mybir.AluOpType
AX = mybir.AxisListType

D = 2048
W = 1024
T = 1024
IN_COLS = 19264
EPS = 1e-6
SEG = dict(qa=0, ckv=512, kpe=768, ga=832, bg=1856, cg=2880, xc=3904, gb=4928, xp=5952, gc=6976,
           glu=8000, gd=10048, mg=11072)
PAD = 16
PV = {}
_o = 0
for _n, _w in (("normw", 16), ("qnw", 4), ("kvnw", 2), ("c3w", 24), ("c3b", 8), ("pscale", 8),
               ("dww", 248), ("dwb", 8), ("clnw", 8), ("clnb", 8)):
    PV[_n] = _o
    _o += _w
NPV = _o


class Ev:
    __slots__ = ("sem", "val")

    def __init__(self, sem, val):
        self.sem, self.val = sem, val


class TObj:
    __slots__ = ("w", "r", "excl")

    def __init__(self, excl=False):
        self.w = {}
        self.r = {}
        self.excl = excl


class View:
    __slots__ = ("ap", "t")

    def __init__(self, ap, t):
        self.ap, self.t = ap, t


class Eng:
    def __init__(self, h, sem, skip_self=False):
        self.h, self.sem, self.count, self.waited, self.skip_self = h, sem, 0, {}, skip_self

    def wait(self, ev):
        if ev.sem is self.sem and self.skip_self:
            return
        k = id(ev.sem)
        if self.waited.get(k, 0) >= ev.val:
            return
        self.h.wait_ge(ev.sem, ev.val)
        self.waited[k] = ev.val


class Queue:
    def __init__(self, eng, sems):
        self.eng, self.sems = eng, sems
        self.cnt = [0] * len(sems)
        self.ev = [None] * len(sems)
        self.n = 0


def _tl(xs):
    out = []
    for x in xs:
        if x is None:
            continue
        if isinstance(x, TObj):
            out.append(x)
        elif isinstance(x, View):
            out.extend(x.t)
        else:
            out.extend(_tl(x))
    return out


class Ctx:
    def __init__(self, nc, es):
        self.nc, self.es = nc, es
        sem = lambda n: es.enter_context(nc.semaphore(n))
        self.pe = Eng(nc.tensor, sem("s_pe"), skip_self=True)
        self.act = Eng(nc.scalar, sem("s_act"))
        self.dve = Eng(nc.vector, sem("s_dve"))
        self.pool = Eng(nc.gpsimd, sem("s_pool"))
        self.sp = Eng(nc.sync, sem("s_sp"))
        self.qs = Queue(self.sp, [sem(f"qs{i}") for i in range(16)])
        self.qg = Queue(self.pool, [sem(f"qg{i}") for i in range(16)])
        self.qa = Queue(self.act, [sem(f"qa{i}") for i in range(16)])
        self.alt = 0

    def _deps(self, eng, reads, writes):
        need = {}

        def add(ev):
            k = id(ev.sem)
            c = need.get(k)
            if c is None or c.val < ev.val:
                need[k] = ev

        for t in reads:
            for ev in t.w.values():
                add(ev)
            if t.excl:
                for ev in t.r.values():
                    if ev.sem is not eng.sem:
                        add(ev)
        for t in writes:
            for ev in t.w.values():
                add(ev)
            for ev in t.r.values():
                add(ev)
        for ev in need.values():
            eng.wait(ev)

    def _commit(self, ev, reads, writes):
        k = id(ev.sem)
        for t in reads:
            t.r[k] = ev
        for t in writes:
            t.w = {k: ev}
            t.r = {}

    def op(self, eng, reads, writes, emit):
        reads, writes = _tl(reads), _tl(writes)
        self._deps(eng, reads, writes)
        ins = emit()
        eng.count += 1
        ins.then_inc(eng.sem, 1)
        self._commit(Ev(eng.sem, eng.count), reads, writes)

    def dma(self, q, out, in_, reads=(), writes=()):
        reads, writes = _tl(reads), _tl(writes)
        oap, iap = out, in_
        if isinstance(out, View):
            writes = writes + out.t
            oap = out.ap
        if isinstance(in_, View):
            reads = reads + in_.t
            iap = in_.ap
        self._deps(q.eng, reads, writes)
        s = q.n % len(q.sems)
        if q.ev[s] is not None:
            q.eng.wait(q.ev[s])
        q.cnt[s] += 16
        q.eng.h.dma_start(out=oap, in_=iap).then_inc(q.sems[s], 16)
        ev = Ev(q.sems[s], q.cnt[s])
        q.ev[s] = ev
        q.n += 1
        k = id(ev.sem)
        for t in reads:
            t.r[k] = ev
        for t in writes:
            t.w[k] = ev
            t.r = {}

    def mm(self, out, pairs, extra=None):
        reads = [p[0] for p in pairs] + [p[1] for p in pairs]
        n = len(pairs)

        def emit():
            ins = None
            for i, (l, r) in enumerate(pairs):
                ins = self.nc.tensor.matmul(out.ap, lhsT=l.ap, rhs=r.ap, start=(i == 0), stop=(i == n - 1))
            return ins

        self.op(self.pe, reads, [out], emit)

    def tr(self, out, in_, ident):
        self.op(self.pe, [in_, ident], [out], lambda: self.nc.tensor.transpose(out.ap, in_.ap, ident.ap))

    def trs(self, outs_ins, ident):
        def emit():
            ins = None
            for o, i in outs_ins:
                ins = self.nc.tensor.transpose(o.ap, i.ap, ident.ap)
            return ins
        self.op(self.pe, [i for _, i in outs_ins] + [ident], [o for o, _ in outs_ins], emit)

    def actf(self, out, in_, func, bias=None, scale=1.0, accum=None):
        reads = [in_]
        kw = {}
        if isinstance(bias, View):
            reads.append(bias)
            kw["bias"] = bias.ap
        elif bias is not None:
            kw["bias"] = bias
        if isinstance(scale, View):
            reads.append(scale)
            kw["scale"] = scale.ap
        else:
            kw["scale"] = scale
        writes = [out]
        if accum is not None:
            writes.append(accum)
            kw["accum_out"] = accum.ap
        self.op(self.act, reads, writes, lambda: self.nc.scalar.activation(out=out.ap, in_=in_.ap, func=func, **kw))

    def tt(self, out, a, b, op, eng=None):
        eng = eng or self.dve
        self.op(eng, [a, b], [out], lambda: eng.h.tensor_tensor(out=out.ap, in0=a.ap, in1=b.ap, op=op))

    def ts(self, out, a, s1, s2, op0, op1=None, eng=None):
        eng = eng or self.dve
        reads = [a]
        v1 = s1.ap if isinstance(s1, View) else s1
        v2 = s2.ap if isinstance(s2, View) else s2
        if isinstance(s1, View):
            reads.append(s1)
        if isinstance(s2, View):
            reads.append(s2)
        if op1 is None:
            self.op(eng, reads, [out], lambda: eng.h.tensor_scalar(out=out.ap, in0=a.ap, scalar1=v1, scalar2=None, op0=op0))
        else:
            self.op(eng, reads, [out], lambda: eng.h.tensor_scalar(out=out.ap, in0=a.ap, scalar1=v1, scalar2=v2, op0=op0, op1=op1))

    def stt(self, out, a, s, b, op0, op1):
        reads = [a, b]
        sv = s
        if isinstance(s, View):
            reads.append(s)
            sv = s.ap
        self.op(self.dve, reads, [out], lambda: self.nc.vector.scalar_tensor_tensor(out=out.ap, in0=a.ap, scalar=sv, in1=b.ap, op0=op0, op1=op1))

    def copy(self, out, in_, scale=None):
        self.alt ^= 1
        if self.alt:
            self.actf(out, in_, AF.Copy, scale=(1.0 if scale is None else scale))
        elif scale is None:
            self.op(self.dve, [in_], [out], lambda: self.nc.vector.tensor_copy(out=out.ap, in_=in_.ap))
        else:
            self.ts(out, in_, scale, None, ALU.mult)

    def recip(self, out, in_):
        self.op(self.dve, [in_], [out], lambda: self.nc.vector.reciprocal(out=out.ap, in_=in_.ap))

    def memset(self, out, val):
        self.op(self.dve, [], [out], lambda: self.nc.vector.memset(out.ap, val))


class Arena:
    G = 128

    def __init__(self, nc, es, nwords):
        self.ng = nwords // self.G
        self.t = es.enter_context(nc.sbuf_tensor("arena", [128, self.ng * self.G], F32))
        self.gr = [TObj() for _ in range(self.ng)]
        self.free = [True] * self.ng
        self.peak = 0

    def nfree(self):
        return sum(self.free) * self.G * 4

    def alloc(self, nbytes, soft=False):
        n = -(-nbytes // (self.G * 4))
        run = 0
        for i in range(self.ng):
            run = run + 1 if self.free[i] else 0
            if run == n:
                g0 = i - n + 1
                for j in range(g0, i + 1):
                    self.free[j] = False
                self.peak = max(self.peak, self.ng - sum(self.free))
                return Buf(self, g0, n)
        if soft:
            return None
        raise RuntimeError(f"arena full: want {nbytes}, free {self.nfree()}")

    def release(self, b):
        for j in range(b.g0, b.g0 + b.n):
            assert not self.free[j]
            self.free[j] = True


class Buf:
    def __init__(self, arena, g0, n):
        self.a, self.g0, self.n = arena, g0, n

    def free(self):
        self.a.release(self)

    def v(self, dt, w0, w1, pattern=None, parts=128, p0=0, **kw):
        G = self.a.G
        base = self.g0 * G
        if dt == BF16:
            ap = self.a.t[p0:p0 + parts, base:base + self.n * G].bitcast(BF16)[:, w0:w1]
            f0, f1 = w0 // 2, (w1 + 1) // 2
        else:
            ap = self.a.t[p0:p0 + parts, base + w0:base + w1]
            f0, f1 = w0, w1
        if pattern:
            ap = ap.rearrange(pattern, **kw)
        ts = self.a.gr[self.g0 + f0 // G: self.g0 + (f1 - 1) // G + 1]
        return View(ap, ts)


class Chunked:
    def __init__(self, arena, dt, nch, n):
        self.dt, self.nch, self.n = dt, nch, n
        self.b = arena.alloc(nch * n * (2 if dt == BF16 else 4))

    def ch(self, c, t0=0, t1=None, parts=128, pattern=None, p0=0, **kw):
        t1 = self.n if t1 is None else t1
        return self.b.v(self.dt, c * self.n + t0, c * self.n + t1, pattern=pattern, parts=parts, p0=p0, **kw)

    def all(self, c0=0, c1=None, parts=128, p0=0):
        c1 = self.nch if c1 is None else c1
        return self.b.v(self.dt, c0 * self.n, c1 * self.n, pattern="p (c n) -> p c n", parts=parts, p0=p0, c=c1 - c0)

    def free(self):
        self.b.free()


class Psum:
    def __init__(self, nc, es):
        self.t = es.enter_context(nc.psum_tensor("psum", [128, 8 * 512], F32))
        self.banks = [TObj(excl=True) for _ in range(8)]
        self.pos = 0

    def fixed(self, b0, nb=1):
        return PB(self, b0, nb)

    def alloc(self, nb=1):
        if self.pos + nb > 8:
            self.pos = 0
        b0 = self.pos
        self.pos = (self.pos + nb) % 8
        return PB(self, b0, nb)


class PB:
    def __init__(self, ps, b0, nb):
        self.ps, self.b0, self.nb = ps, b0, nb

    def v(self, n0=0, n1=512, parts=128, p0=0, dt=F32, pattern=None, **kw):
        base = self.b0 * 512
        if dt == BF16:
            ap = self.ps.t[p0:p0 + parts, base:base + self.nb * 512].bitcast(BF16)[:, n0:n1]
        else:
            ap = self.ps.t[p0:p0 + parts, base + n0:base + n1]
        if pattern:
            ap = ap.rearrange(pattern, **kw)
        return View(ap, self.ps.banks[self.b0:self.b0 + self.nb])


class Small:
    def __init__(self, nc, es, name, shape, dt):
        self.t = es.enter_context(nc.sbuf_tensor(name, shape, dt))
        self.o = TObj()

    def v(self, *idx):
        ap = self.t[idx] if idx else self.t[:]
        return View(ap, [self.o])


class Stats:
    def __init__(self, nc, es, n=256):
        self.t = es.enter_context(nc.sbuf_tensor("stats", [128, n], F32))
        self.o = [TObj() for _ in range(n)]
        self.i = 0

    def get(self):
        i = self.i
        self.i = (self.i + 1) % len(self.o)
        return View(self.t[:, i:i + 1], [self.o[i]])


class WStream:
    def __init__(self, cx, arena, reserve, spec_of, plan=None):
        self.cx, self.arena, self.reserve, self.spec_of, self.plan = cx, arena, reserve, spec_of, plan
        self.rec = []
        self.issued = []
        self.pos = 0

    def _issue(self, key, soft):
        sap, kch, n = self.spec_of(key)
        nbytes = kch * n * 2
        if soft and self.arena.nfree() - nbytes < self.reserve:
            return False
        b = self.arena.alloc(nbytes, soft=soft)
        if b is None:
            return False
        v = b.v(BF16, 0, kch * n, pattern="p (k n) -> p k n", k=kch)
        self.cx.dma(self.cx.qg, v, sap)
        self.issued.append((b, v))
        return True

    def next(self, key, look=6):
        self.rec.append(key)
        if self.plan is None:
            self._issue(key, soft=False)
        else:
            assert self.plan[self.pos] == key, (self.plan[self.pos], key)
            while len(self.issued) <= self.pos:
                self._issue(self.plan[len(self.issued)], soft=False)
        b, v = self.issued[self.pos]
        self.pos += 1
        if self.plan is not None:
            while len(self.issued) < len(self.plan) and len(self.issued) < self.pos + look:
                if not self._issue(self.plan[len(self.issued)], soft=True):
                    break
        return b, v


def sub(v, ap):
    return View(ap, v.t)


class StopBuild(Exception):
    pass


def build_program(dbg=None, stop=None, plan=None):
    nc = bass.Bass("TRN2", target_bir_lowering=False)
    di = lambda n, s: nc.dram_tensor(n, s, F32, kind="ExternalInput").ap()
    do = lambda n, s: nc.dram_tensor(n, s, F32, kind="ExternalOutput").ap()
    TG = (1024, 512)
    xin = [di("x0", [TG[0], D]), di("x1", [TG[1], D])]
    cache = di("cache", [2, 256, 320])
    condT = di("condT", [128, 32])
    w_ada = di("w_ada", [2, D, 3 * D])
    b_ada = di("b_ada", [2, 3 * D])
    w_in = di("w_in", [2, D, IN_COLS])
    w_qb = di("w_qb", [2, 512, 1536])
    w_kvb = di("w_kvb", [2, 256, 2048])
    pool_w = di("pool_w", [2, 4, 256, 256])
    w_bproj = di("w_bproj", [2, 4, W, D])
    w_out = di("w_out", [2, D, D])
    pvec = di("pvec", [2, 128, NPV])
    fnw = di("fnw", [1, D])
    consts = di("consts", [128, 384])
    rope = di("rope", [4, 64, T])
    invcs = [di("invc0", [4, TG[0]]), di("invc1", [4, TG[1]])]
    kmask = di("kmask", [4, 1280])
    flag = di("flag", [128, 1])
    selrow_h = di("selrow_h", [2, 256])
    qsel = di("qsel", [4, 1024])
    youts = [do("y0", [TG[0], D]), do("y1", [TG[1], D])]
    kv_os = [do("kv_o0", [4, 2, 256, 320]), do("kv_o1", [2, 2, 256, 320])]
    scr = [[nc.dram_tensor(f"scr{l}_{g}", [TG[g], D], F32, kind="Internal").ap() for g in range(2)] for l in range(2)]
    dbg_out = {}

    with contextlib.ExitStack() as es:
        cx = Ctx(nc, es)
        ps = Psum(nc, es)
        st = Stats(nc, es)
        cst = Small(nc, es, "cst", [128, 384], F32)
        identb = Small(nc, es, "identb", [128, 128], BF16)
        pvs = [Small(nc, es, f"pv{l}", [128, NPV], F32) for l in range(2)]
        cond_f = Small(nc, es, "cond_f", [128, 32], F32)
        cond_b = Small(nc, es, "cond_b", [128, 32], BF16)
        modTs = [Small(nc, es, f"modT{l}", [128, 64], F32) for l in range(2)]
        amods = [Small(nc, es, f"amod{l}", [128, 32], F32) for l in range(2)]
        shiftTs = [Small(nc, es, f"shiftT{l}", [128, 32], F32) for l in range(2)]
        gate_d = [nc.dram_tensor(f"gate_scr{l}", [2, D], F32, kind="Internal").ap() for l in range(2)]
        gate_t = [TObj() for _ in range(2)]
        epsT = Small(nc, es, "epsT", [128, 1], F32)
        flagT = Small(nc, es, "flagT", [128, 1], F32)
        nfree = nc.sbuf_bytes_remaining
        arena = Arena(nc, es, (nfree - 2048) // 4)
        def spec_of(key):
            kind, l = key[0], key[1]
            std = lambda src, c0, n: src.rearrange("(k p) n -> p k n", p=128)[:, :, c0:c0 + n]
            if kind == "ada":
                return std(w_ada[l], key[2] * 512, 512), 16, 512
            if kind == "win":
                name = key[3]
                if name == "kpe":
                    return std(w_in[l], SEG["kpe"], 64), 16, 64
                if name == "mg":
                    j, i = key[4], key[5]
                    return std(w_in[l], SEG["mg"] + i * D + j * 128, 128), 16, 128
                base = {"glua": SEG["glu"], "glub": SEG["glu"] + 1024}.get(name)
                base = SEG[name] if base is None else base
                return std(w_in[l], base + key[4] * 128, 128), 16, 128
            if kind == "wqb":
                return std(w_qb[l], 0, 1536), 4, 1536
            if kind == "wkvb":
                return std(w_kvb[l], 0, 2048), 2, 2048
            if kind == "poolw":
                return pool_w[l].rearrange("g (k p) d -> p (g k) d", p=128), 8, 256
            if kind == "bp":
                j, i = key[3], key[4]
                return std(w_bproj[l, i], j * 128, 128), 8, 128
            if kind == "wout":
                return std(w_out[l], key[3] * 512, 512), 16, 512
            raise KeyError(key)

        ws = WStream(cx, arena, 32 * 1024, spec_of, plan)

        ident = sub(cst.v(), cst.t[:, 0:128])
        ones = sub(cst.v(), cst.t[:, 128:256])
        ident2 = sub(cst.v(), cst.t[0:2, 0:2])

        cx.dma(cx.qs, cst.v(), consts)
        for l in range(2):
            cx.dma(cx.qs, pvs[l].v(), pvec[l])
        cx.dma(cx.qs, cond_f.v(), condT)
        cx.memset(epsT.v(), EPS)
        cx.dma(cx.qs, flagT.v(), flag)
        cx.op(cx.dve, [cst.v()], [identb.v()], lambda: nc.vector.tensor_copy(out=identb.t[:], in_=cst.t[:, 0:128]))
        cx.actf(cond_b.v(), cond_f.v(), AF.Silu)
        selrow = Small(nc, es, "selrow", [2, 256], F32)
        cx.dma(cx.qs, selrow.v(), selrow_h)

        dbgl = []

        def dump(name, view, shape):
            if dbg is not None and name in dbg and view is not None:
                o = nc.dram_tensor("dbg_" + name, shape, view.ap.dtype, kind="ExternalOutput").ap()
                cx.dma(cx.qs, o, view)
                dbg_out[name] = shape
                dbgl.append(name)
            if stop == name:
                raise StopBuild()

        def run_ada(l):
            modT, amod, shiftT = modTs[l], amods[l], shiftTs[l]

            def finish_mod():
                mv = modT.t[:].rearrange("p (c j) -> p j c", j=2)
                for cnd in range(2):
                    tmp = st.get()
                    cx.ts(sub(amod.v(), amod.t[:, cnd * 16:(cnd + 1) * 16]), sub(modT.v(), mv[:, cnd, 16:32]), 1.0, None, ALU.add)
                    cx.tt(sub(amod.v(), amod.t[:, cnd * 16:(cnd + 1) * 16]), sub(amod.v(), amod.t[:, cnd * 16:(cnd + 1) * 16]),
                          sub(pvs[l].v(), pvs[l].t[:, PV["normw"]:PV["normw"] + 16]), ALU.mult)
                    cx.op(cx.dve, [modT.v()], [shiftT.v()],
                          lambda cnd=cnd: nc.vector.tensor_copy(out=shiftT.t[:, cnd * 16:(cnd + 1) * 16], in_=mv[:, cnd, 0:16]))


            for nb in range(12):
                wb, wv = ws.next(("ada", l, nb), look=2)
                pb = ps.alloc()
                cbv = cond_b.v()
                cx.mm(pb.v(0, 512, parts=2),
                      [(sub(cbv, cond_b.t[:, k * 2:k * 2 + 2]), sub(wv, wv.ap[:, k, :])) for k in range(16)])
                wb.free()
                br = arena.alloc(2048)
                brv = br.v(F32, 0, 512, parts=2)
                cx.dma(cx.qs, brv, bass.AP(b_ada.tensor, l * 3 * D + nb * 512, [[0, 2], [1, 512]]))
                if nb < 8:
                    rw = arena.alloc(2048)
                    rwv = rw.v(F32, 0, 512, parts=2)
                    cx.tt(rwv, pb.v(0, 512, parts=2), brv, ALU.add)
                    pt = ps.alloc()
                    for j in range(4):
                        cx.mm(pt.v(j * 2, j * 2 + 2), [(sub(rwv, rwv.ap[:, j * 128:(j + 1) * 128]), ident2)])
                    cx.copy(sub(modT.v(), modT.t[:, nb * 8:nb * 8 + 8]), pt.v(0, 8))
                    rw.free()
                else:
                    cx.tt(brv, pb.v(0, 512, parts=2), brv, ALU.add)
                    cx.dma(cx.qs, gate_d[l][:, (nb - 8) * 512:(nb - 7) * 512], brv, writes=[gate_t[l]])
                br.free()
                if nb == 7:
                    finish_mod()
                if nb < 11:
                    yield
        def phaseN(l, g, xsrc, xsrc_t):
            T = TG[g]
            NT = T // 128
            cnd = g
            amod, shiftT = amods[l], shiftTs[l]
            hT = Chunked(arena, BF16, 16, T)
            xbs = [arena.alloc(8192) for _ in range(4)]
            yield hT

            def load(tt):
                cx.dma(cx.qa, xbs[tt % 4].v(F32, 0, D), xsrc[tt * 128:(tt + 1) * 128, :], reads=xsrc_t[tt])

            def front(tt):
                xv = xbs[tt % 4].v(F32, 0, D)
                jb = arena.alloc(4096)
                ss, rs = st.get(), st.get()
                cx.actf(jb.v(BF16, 0, D), xv, AF.Square, accum=ss)
                jb.free()
                cx.actf(rs, ss, AF.Sqrt, bias=epsT.v(), scale=1.0 / D)
                cx.recip(rs, rs)
                cx.actf(xv, xv, AF.Copy, scale=rs)

            def back(tt):
                xv = xbs[tt % 4].v(F32, 0, D)
                for kq in range(4):
                    pb = ps.alloc()
                    cx.trs([(pb.v(j * 128, j * 128 + 128), sub(xv, xv.ap[:, (kq * 4 + j) * 128:(kq * 4 + j + 1) * 128]))
                            for j in range(4)], ident)
                    for j in range(4):
                        k = kq * 4 + j
                        a_ = sub(amod.v(), amod.t[:, cnd * 16 + k:cnd * 16 + k + 1])
                        s_ = sub(shiftT.v(), shiftT.t[:, cnd * 16 + k:cnd * 16 + k + 1])
                        if j % 2 == 0:
                            cx.ts(hT.ch(k, tt * 128, tt * 128 + 128), pb.v(j * 128, j * 128 + 128), a_, s_, ALU.mult, ALU.add)
                        else:
                            cx.actf(hT.ch(k, tt * 128, tt * 128 + 128), pb.v(j * 128, j * 128 + 128), AF.Identity, bias=s_, scale=a_)

            load(0)
            if NT > 1:
                load(1)
            front(0)
            yield None
            for tt in range(NT):
                if tt + 2 < NT:
                    load(tt + 2)
                if tt + 1 < NT:
                    front(tt + 1)
                back(tt)
                yield None
            for b_ in xbs:
                b_.free()
        def run_group(l, g, hT, xsrc, xsrc_t, ydst, ydst_t, kvdst, tick, pre_o, tick_o):
            amod, shiftT = amods[l], shiftTs[l]
            GEN = (g == 0)
            T = TG[g]
            NH, NT = T // 512, T // 128
            cnd = g
            nseq, L = T // 256, 256
            LK = 1280 if GEN else T
            KOFF = 256 if GEN else 0
            pv = pvs[l]
            pcol = lambda name, i: sub(pv.v(), pv.t[:, PV[name] + i:PV[name] + i + 1])

            def proj(wv, M, th, hT):
                pb = ps.alloc()
                cx.mm(pb.v(0, 512, parts=M),
                      [(sub(wv, wv.ap[:, k, 0:M]), hT.ch(k, th * 512, th * 512 + 512)) for k in range(16)])
                return pb

            if l == 0 and g == 0:
                dump("hT", hT.all(), [128, 16, T])
            def rms_feat(src_f32, nch, nfeat, wname, dsts):
                for th in range(NH):
                    pbs = ps.alloc()
                    sqs = []
                    for c in range(nch):
                        sq = arena.alloc(2048)
                        cx.actf(sq.v(F32, 0, 512), src_f32.ch(c, th * 512, th * 512 + 512), AF.Square)
                        sqs.append(sq)
                    cx.mm(pbs.v(), [(ones, sq.v(F32, 0, 512)) for sq in sqs])
                    for sq in sqs:
                        sq.free()
                    rb = arena.alloc(2048)
                    rv = rb.v(F32, 0, 512)
                    cx.actf(rv, pbs.v(), AF.Sqrt, bias=epsT.v(), scale=1.0 / nfeat)
                    cx.recip(rv, rv)
                    for c in range(nch):
                        for d in dsts:
                            d(c, th, src_f32.ch(c, th * 512, th * 512 + 512), rv)
                    rb.free()

            qa = Chunked(arena, F32, 4, T)
            for c in range(4):
                wb, wv = ws.next(("win", l, g, "qa", c))
                for th in range(NH):
                    pb = proj(wv, 128, th, hT)
                    cx.copy(qa.ch(c, th * 512, th * 512 + 512), pb.v())
                wb.free()
            qn = Chunked(arena, BF16, 4, T)
            rms_feat(qa, 4, 512, "qnw", [lambda c, th, a, r: cx.stt(qn.ch(c, th * 512, th * 512 + 512), a, pcol("qnw", c), r, ALU.mult, ALU.mult)])
            qa.free()
            if l == 0 and g == 0:
                dump("qn", qn.all(), [128, 4, T])

            wqb_b, wq = ws.next(("wqb", l, g))
            qTn = Chunked(arena, BF16, 8, T)
            qTp = Chunked(arena, BF16, 8, T)
            cx.memset(qTp.all(p0=64, parts=64), 0.0)
            if GEN:
                cx.dma(cx.qg, qTp.all(p0=64, parts=4), bass.AP(qsel.tensor, 0, [[T, 4], [0, 8], [1, T]]))
            SC = 192.0 ** -0.5
            if GEN:
                rp = arena.alloc(4 * T * 4)
                rpv = [rp.v(F32, i * T, (i + 1) * T, parts=64) for i in range(4)]
                for i in range(4):
                    cx.dma(cx.qs, rpv[i], rope[i])
                wsw_b = arena.alloc(4 * 8 * 64 * 2)
                wsw = wsw_b.v(BF16, 0, 4 * 512, pattern="p (c h d) -> p c h d", c=4, h=8)
                wq4 = wq.ap.rearrange("p c (h d) -> p c h d", h=8)
                for blk, srcb in ((0, 16), (16, 0), (32, 48), (48, 32)):
                    cx.op(cx.dve, [wq], [wsw], lambda blk=blk, srcb=srcb: nc.vector.tensor_copy(
                        out=wsw.ap[:, :, :, blk:blk + 16], in_=wq4[:, :, :, 128 + srcb:128 + srcb + 16]))
            if l == 0 and g == 0:
                dump("wsw", sub(wsw, wsw_b.v(BF16, 0, 2048).ap), [128, 2048])
            for h in range(8):
                if l == 0 and g == 0 and h >= 1:
                    dump("q%dn" % (h - 1), qTn.ch(h - 1), [128, T])
                    dump("q%dp" % (h - 1), qTp.ch(h - 1, parts=64), [64, T])
                for th in range(NH):
                    sl = slice(th * 512, th * 512 + 512)
                    pb = ps.alloc()
                    cx.mm(pb.v(), [(sub(wq, wq.ap[:, c, h * 192:h * 192 + 128]), qn.ch(c, th * 512, th * 512 + 512)) for c in range(4)])
                    cx.copy(qTn.ch(h, th * 512, th * 512 + 512), pb.v(), scale=SC)
                    pb2 = ps.alloc()
                    cx.mm(pb2.v(0, 512, parts=64), [(sub(wq, wq.ap[:, c, h * 192 + 128:h * 192 + 192]), qn.ch(c, th * 512, th * 512 + 512)) for c in range(4)])
                    if not GEN:
                        cx.copy(qTp.ch(h, th * 512, th * 512 + 512, parts=64), pb2.v(0, 512, parts=64), scale=SC)
                    else:
                        pb3 = ps.alloc()
                        cx.mm(pb3.v(0, 512, parts=64), [(sub(wsw, wsw.ap[:, c, h, :]), qn.ch(c, th * 512, th * 512 + 512)) for c in range(4)])
                        t1 = arena.alloc(2048)
                        t2 = arena.alloc(2048)
                        cx.tt(t1.v(F32, 0, 512, parts=64), pb2.v(0, 512, parts=64), sub(rpv[0], rpv[0].ap[:, sl]), ALU.mult)
                        cx.tt(t2.v(F32, 0, 512, parts=64), pb3.v(0, 512, parts=64), sub(rpv[1], rpv[1].ap[:, sl]), ALU.mult)
                        cx.tt(qTp.ch(h, th * 512, th * 512 + 512, parts=64), t1.v(F32, 0, 512, parts=64), t2.v(F32, 0, 512, parts=64), ALU.add)
                        t1.free()
                        t2.free()
            qn.free()
            wqb_b.free()
            if GEN:
                wsw_b.free()
            if l == 0 and g == 0:
                dump("qTn", qTn.all(), [128, 8, T])
                dump("qTp", qTp.all(parts=64), [64, 8, T])

            ckv = Chunked(arena, F32, 2, T)
            for c in range(2):
                wb, wv = ws.next(("win", l, g, "ckv", c))
                for th in range(NH):
                    pb = proj(wv, 128, th, hT)
                    cx.copy(ckv.ch(c, th * 512, th * 512 + 512), pb.v())
                wb.free()
            ckb = Chunked(arena, BF16, 2, LK)
            dsts = [lambda c, th, a, r: cx.stt(ckb.ch(c, KOFF + th * 512, KOFF + th * 512 + 512), a, pcol("kvnw", c), r, ALU.mult, ALU.mult)]
            ckn = Chunked(arena, F32, 2, T)
            dsts.append(lambda c, th, a, r: cx.stt(ckn.ch(c, th * 512, th * 512 + 512), a, pcol("kvnw", c), r, ALU.mult, ALU.mult))
            rms_feat(ckv, 2, 256, "kvnw", dsts)
            ckv.free()
            if l == 0 and g == 0:
                dump("A2a", None, None)
            kpT = Chunked(arena, BF16, 1, LK)
            cx.memset(kpT.all(p0=64, parts=64), 0.0)
            if GEN:
                cx.dma(cx.qg, kpT.ch(0, p0=64, parts=4), kmask)
            wb, wv = ws.next(("win", l, g, "kpe"))
            kpf = Chunked(arena, F32, 1, T)
            if GEN:
                ksw_b = arena.alloc(16 * 64 * 2)
                ksw = ksw_b.v(BF16, 0, 16 * 64, pattern="p (k d) -> p k d", k=16)
                for blk, srcb in ((0, 16), (16, 0), (32, 48), (48, 32)):
                    cx.op(cx.dve, [wv], [ksw], lambda blk=blk, srcb=srcb: nc.vector.tensor_copy(
                        out=ksw.ap[:, :, blk:blk + 16], in_=wv.ap[:, :, srcb:srcb + 16]))
            for th in range(NH):
                sl = slice(th * 512, th * 512 + 512)
                pb = proj(wv, 64, th, hT)
                cx.actf(kpf.ch(0, th * 512, th * 512 + 512, parts=64), pb.v(0, 512, parts=64), AF.Copy)
                if not GEN:
                    cx.op(cx.dve, [pb.v()], [kpT.ch(0, th * 512, th * 512 + 512, parts=64)],
                          lambda th=th, pb=pb: nc.vector.tensor_copy(out=kpT.ch(0, th * 512, th * 512 + 512, parts=64).ap, in_=pb.v(0, 512, parts=64).ap))
                else:
                    pb3 = ps.alloc()
                    cx.mm(pb3.v(0, 512, parts=64), [(sub(ksw, ksw.ap[:, k, :]), hT.ch(k, th * 512, th * 512 + 512)) for k in range(16)])
                    t1 = arena.alloc(2048)
                    t2 = arena.alloc(2048)
                    cx.tt(t1.v(F32, 0, 512, parts=64), pb.v(0, 512, parts=64), sub(rpv[2], rpv[2].ap[:, sl]), ALU.mult)
                    cx.tt(t2.v(F32, 0, 512, parts=64), pb3.v(0, 512, parts=64), sub(rpv[3], rpv[3].ap[:, sl]), ALU.mult)
                    cx.tt(kpT.ch(0, KOFF + th * 512, KOFF + th * 512 + 512, parts=64), t1.v(F32, 0, 512, parts=64), t2.v(F32, 0, 512, parts=64), ALU.add)
                    t1.free()
                    t2.free()
            wb.free()
            if l == 0 and g == 0:
                dump("A2b", None, None)
            if GEN:
                ksw_b.free()
                rp.free()
                for i in range(2):
                    cb_ = arena.alloc(384 * 4)
                    cv = cb_.v(F32, 0, 384)
                    cx.memset(sub(cv, cv.ap[:, 320:384]), 0.0)
                    cx.dma(cx.qs, sub(cv, cv.ap[:, 0:320]), cache[l, i * 128:(i + 1) * 128, :])
                    pb = ps.alloc()
                    cx.trs([(pb.v(0, 128), sub(cv, cv.ap[:, 0:128])), (pb.v(128, 256), sub(cv, cv.ap[:, 128:256])),
                            (pb.v(256, 384), sub(cv, cv.ap[:, 256:384]))], ident)
                    cx.copy(ckb.ch(0, i * 128, i * 128 + 128), pb.v(0, 128))
                    cx.copy(ckb.ch(1, i * 128, i * 128 + 128), pb.v(128, 256))
                    cx.copy(kpT.ch(0, i * 128, i * 128 + 128, parts=64), pb.v(256, 384, parts=64))
                    cb_.free()
            for tt in range(NT):
                pb = ps.alloc()
                cx.trs([(pb.v(0, 128), ckn.ch(0, tt * 128, tt * 128 + 128)),
                        (pb.v(128, 256), ckn.ch(1, tt * 128, tt * 128 + 128))], ident)
                cx.tr(pb.v(256, 320), kpf.ch(0, tt * 128, tt * 128 + 128, parts=64), sub(ident, ident.ap[0:64, 0:64]))
                ob = arena.alloc(320 * 4)
                cx.copy(ob.v(F32, 0, 320), pb.v(0, 320))
                s_, r_ = tt // 2, (tt % 2) * 128
                cx.dma(cx.qs, kvdst[s_, l, r_:r_ + 128, :], ob.v(F32, 0, 320), writes=[kv_t])
                ob.free()
            ckn.free()
            kpf.free()
            if l == 0 and g == 0:
                dump("ckb", ckb.all(), [128, 2, LK])
                dump("kpT", kpT.all(parts=64), [64, 1, LK])

            wkv_b, wkv = ws.next(("wkvb", l, g))
            kTn = Chunked(arena, BF16, 8, LK)
            for h in range(8):
                k0 = 0
                while k0 < LK:
                    n = min(512, LK - k0)
                    pb = ps.alloc()
                    cx.mm(pb.v(0, n), [(sub(wkv, wkv.ap[:, c, h * 256:h * 256 + 128]), ckb.ch(c, k0, k0 + n)) for c in range(2)])
                    cx.copy(kTn.ch(h, k0, k0 + n), pb.v(0, n))
                    k0 += n
            NKT = LK // 128
            Vt = Chunked(arena, BF16, NKT, W)
            wkv4 = wkv.ap.rearrange("p c (h t d) -> p c h t d", h=8, t=2)
            for kt in range(NKT):
                for hv in range(2):
                    pb = ps.alloc()
                    cx.mm(pb.v(0, 512, pattern="p (h d) -> p h d", h=4),
                          [(ckb.ch(c, kt * 128, kt * 128 + 128), sub(wkv, wkv4[:, c, hv * 4:hv * 4 + 4, 1, :])) for c in range(2)])
                    cx.copy(Vt.ch(kt, hv * 512, hv * 512 + 512), pb.v())
            wkv_b.free()
            ckb.free()
            yA = Chunked(arena, BF16, 8, T)
            for c in range(8):
                wb, wv = ws.next(("win", l, g, "ga", c))
                for th in range(NH):
                    pb = proj(wv, 128, th, hT)
                    cx.actf(yA.ch(c, th * 512, th * 512 + 512), pb.v(), AF.Silu)
                wb.free()

            Lk = 1280 if GEN else 256
            nkt = Lk // 128
            QB = 4 if GEN else 2
            units = [(0, 8, 0)] if GEN else [(s * 256, 2, s * 256) for s in range(nseq)]
            NSB = 2
            SBK = 3 if GEN else 1
            Pb = [arena.alloc(Lk * 2) for _ in range(4)]
            PTbs = [Chunked(arena, BF16, nkt, QB * 128) for _ in range(1 if GEN else 2)]
            items = [(qbase, kb0, h, b0, qi) for (qbase, nqb, kb0) in units for h in range(8)
                     for b0 in range(0, nqb, QB) for qi in range(QB)]
            Sbuf = {}

            def stage_s(i):
                qbase, kb0, h, b0, qi = items[i]
                q0 = qbase + (b0 + qi) * 128
                seg = (b0 + qi) // 2
                S = ps.fixed((i % NSB) * SBK, SBK)
                k0 = 0
                while k0 < Lk:
                    n = min(512, Lk - k0)
                    pairs = [(qTn.ch(h, q0, q0 + 128), kTn.ch(h, kb0 + k0, kb0 + k0 + n)),
                             (qTp.ch(h, q0, q0 + 128), kpT.ch(0, kb0 + k0, kb0 + k0 + n))]
                    cx.mm(S.v(k0, k0 + n), pairs)
                    k0 += n
                Sbuf[i] = S

            Pv, Rs = {}, {}

            def stage_x1(i):
                S = Sbuf.pop(i)
                nmx, rsum = st.get(), st.get()
                cx.op(cx.dve, [S.v(0, Lk)], [nmx], lambda S=S, nmx=nmx: nc.vector.tensor_reduce(
                    out=nmx.ap, in_=S.v(0, Lk).ap, axis=AX.X, op=ALU.max, negate=True))
                P = Pb[i % 4].v(BF16, 0, Lk)
                cx.actf(P, S.v(0, Lk), AF.Exp, bias=nmx, accum=rsum)
                Pv[i], Rs[i] = P, rsum

            def stage_nt(i):
                P, rsum = Pv[i], Rs.pop(i)
                cx.recip(rsum, rsum)
                cx.ts(P, P, rsum, None, ALU.mult)
                for k8 in range(0, nkt, 8):
                    n8 = min(8, nkt - k8)
                    pt = ps.fixed(6 if k8 == 0 else 7)
                    cx.trs([(pt.v(j * 128, j * 128 + 128, dt=BF16), sub(P, P.ap[:, (k8 + j) * 128:(k8 + j + 1) * 128])) for j in range(n8)], identb.v())

            def stage_cp(i):
                qbase, kb0, h, b0, qi = items[i]
                Pv.pop(i)
                PTb = PTbs[(i // QB) % len(PTbs)]
                for k8 in range(0, nkt, 8):
                    n8 = min(8, nkt - k8)
                    pt = ps.fixed(6 if k8 == 0 else 7)
                    ptv = pt.v(0, n8 * 128, dt=BF16, pattern="p (k q) -> p k q", k=n8)
                    dstv = PTb.all(k8, k8 + n8)
                    cx.copy(sub(dstv, dstv.ap[:, :, qi * 128:(qi + 1) * 128]), ptv)
                if qi == QB - 1:
                    nq = QB * 128
                    t0 = qbase + b0 * 128
                    po = ps.fixed(7)
                    cx.mm(po.v(0, nq), [(Vt.ch((kb0 // 128) + kt, h * 128, h * 128 + 128), PTb.ch(kt, 0, nq)) for kt in range(nkt)])
                    cx.tt(yA.ch(h, t0, t0 + nq), po.v(0, nq), yA.ch(h, t0, t0 + nq), ALU.mult)

            n_it = len(items)
            stage_s(0)
            for i in range(n_it + 2):
                if i + 1 < n_it:
                    stage_s(i + 1)
                if i < n_it:
                    stage_x1(i)
                if 0 <= i - 2 < n_it:
                    stage_cp(i - 2)
                if 0 <= i - 1 < n_it:
                    stage_nt(i - 1)
            for b_ in Pb + PTbs + [qTn, qTp, kpT, kTn, Vt]:
                b_.free()
            if l == 0 and g == 0:
                dump("yA", yA.all(), [128, 8, T])

            WP = L + 2 * PAD

            def padv(buf, d, th=None):
                v = buf.v(F32, 0, nseq * WP, pattern="p (s l) -> p s l", s=nseq)
                if th is None:
                    return sub(v, v.ap[:, :, PAD + d:PAD + d + L])
                return sub(v, v.ap[:, 2 * th:2 * th + 2, PAD + d:PAD + d + L])

            def hv(view):
                return sub(view, view.ap.rearrange("p (s l) -> p s l", s=2))

            def fullv(view):
                return sub(view, view.ap.rearrange("p (s l) -> p s l", s=nseq))

            def halo(buf):
                if not GEN:
                    return
                v = buf.v(F32, 0, nseq * WP, pattern="p (s l) -> p s l", s=nseq)
                cx.ts(sub(v, v.ap[:, 0:nseq - 1, PAD + L:PAD + L + PAD]), sub(v, v.ap[:, 1:nseq, PAD:PAD + PAD]), flagT.v(), None, ALU.mult)
                cx.ts(sub(v, v.ap[:, 1:nseq, 0:PAD]), sub(v, v.ap[:, 0:nseq - 1, L:L + PAD]), flagT.v(), None, ALU.mult)

            def newpad():
                b = arena.alloc(nseq * WP * 4)
                cx.memset(b.v(F32, 0, nseq * WP), 0.0)
                return b

            yD = Chunked(arena, BF16, 8, T)
            vv = Chunked(arena, F32, 8, T)
            ups = [newpad(), newpad()]
            NPE = 24
            upbs = [arena.alloc(nseq * WP * 2) for _ in range(2)]
            dgs = [arena.alloc(NPE * 128 * 2) for _ in range(2)]

            def d_stage1(c):
                tick()
                up = ups[c % 2]
                wb, wv = ws.next(("win", l, g, "glub", c))
                sgb = arena.alloc(T * 4)
                for th in range(NH):
                    pb = proj(wv, 128, th, hT)
                    cx.actf(sgb.v(F32, th * 512, th * 512 + 512), pb.v(), AF.Sigmoid)
                wb.free()
                wb, wv = ws.next(("win", l, g, "glua", c))
                for th in range(NH):
                    pb = proj(wv, 128, th, hT)
                    cx.tt(padv(up, 0, th), hv(pb.v()), hv(sgb.v(F32, th * 512, th * 512 + 512)), ALU.mult)
                wb.free()
                sgb.free()
                halo(up)
                wb, wv = ws.next(("win", l, g, "gd", c))
                for th in range(NH):
                    pb = proj(wv, 128, th, hT)
                    cx.actf(yD.ch(c, th * 512, th * 512 + 512), pb.v(), AF.Silu)
                wb.free()
                upb = upbs[c % 2]
                cx.actf(upb.v(BF16, 0, nseq * WP), up.v(F32, 0, nseq * WP), AF.Copy)
                dgv = dgs[c % 2].v(BF16, 0, NPE * 128, pattern="p (k j) -> p k j", k=NPE)
                for k in range(NPE):
                    cx.actf(sub(dgv, dgv.ap[:, k, :]), identb.v(), AF.Copy, scale=pcol("dww", c * 31 + k))

            def d_stage2(c):
                up = ups[c % 2]
                dgv = dgs[c % 2].v(BF16, 0, NPE * 128, pattern="p (k j) -> p k j", k=NPE)
                ubv = upbs[c % 2].v(BF16, 0, nseq * WP, pattern="p (s l) -> p s l", s=nseq)
                for th in range(NH):
                    acc = ps.alloc()
                    av = hv(acc.v())
                    cx.mm(av, [(sub(dgv, dgv.ap[:, k, :]), sub(ubv, ubv.ap[:, 2 * th:2 * th + 2, PAD + k - 15:PAD + k - 15 + L]))
                               for k in range(NPE)])
                    for k in range(NPE, 31):
                        cx.stt(av, padv(up, k - 15, th), pcol("dww", c * 31 + k), av, ALU.mult, ALU.add)
                    cx.actf(vv.ch(c, th * 512, th * 512 + 512), acc.v(), AF.Identity, bias=pcol("dwb", c))

            d_stage1(0)
            for c in range(8):
                if c + 1 < 8:
                    d_stage1(c + 1)
                d_stage2(c)
            for u_ in ups + upbs + dgs:
                u_.free()
            for th in range(NH):
                sl = (th * 512, th * 512 + 512)
                p1, p2 = ps.alloc(), ps.alloc()
                cx.mm(p1.v(), [(ones, vv.ch(c, *sl)) for c in range(8)])
                sqs = []
                for c in range(8):
                    sq = arena.alloc(2048)
                    cx.actf(sq.v(F32, 0, 512), vv.ch(c, *sl), AF.Square)
                    sqs.append(sq)
                cx.mm(p2.v(), [(ones, sq.v(F32, 0, 512)) for sq in sqs])
                for sq in sqs:
                    sq.free()
                mb, rb = arena.alloc(2048), arena.alloc(2048)
                mean, rstd = mb.v(F32, 0, 512), rb.v(F32, 0, 512)
                cx.actf(mean, p1.v(), AF.Copy, scale=1.0 / W)
                cx.tt(rstd, mean, mean, ALU.mult)
                cx.stt(rstd, p2.v(), 1.0 / W, rstd, ALU.mult, ALU.subtract)
                cx.actf(rstd, rstd, AF.Sqrt, bias=epsT.v(), scale=1.0)
                cx.recip(rstd, rstd)
                for c in range(8):
                    t1 = arena.alloc(2048)
                    tv1 = t1.v(F32, 0, 512)
                    cx.tt(tv1, vv.ch(c, *sl), mean, ALU.subtract)
                    cx.tt(tv1, tv1, rstd, ALU.mult)
                    cx.actf(tv1, tv1, AF.Silu, bias=pcol("clnb", c), scale=pcol("clnw", c))
                    cx.tt(yD.ch(c, *sl), tv1, yD.ch(c, *sl), ALU.mult)
                    t1.free()
                mb.free()
                rb.free()
            vv.free()
            if l == 0 and g == 0:
                dump("yD", yD.all(), [128, 8, T])

            yB = Chunked(arena, BF16, 8, T)
            cxp = newpad()
            for c in range(8):
                tick()
                bgs = arena.alloc(T * 4)
                gbs = arena.alloc(T * 4)
                for s_ in ("bg", "gb", "cg", "xc"):
                    wb, wv = ws.next(("win", l, g, s_, c))
                    for th in range(NH):
                        pb = proj(wv, 128, th, hT)
                        sl = (th * 512, th * 512 + 512)
                        if s_ == "bg":
                            cx.actf(bgs.v(F32, *sl), pb.v(), AF.Copy)
                        elif s_ == "gb":
                            cx.actf(gbs.v(F32, *sl), pb.v(), AF.Silu)
                        elif s_ == "cg":
                            if th == 0:
                                cgs = arena.alloc(T * 4)
                            cx.actf(cgs.v(F32, *sl), pb.v(), AF.Copy)
                        else:
                            cx.tt(padv(cxp, 0, th), hv(pb.v()), hv(cgs.v(F32, *sl)), ALU.mult)
                    wb.free()
                cgs.free()
                halo(cxp)
                acc = arena.alloc(T * 4)
                av = fullv(acc.v(F32, 0, T))
                cx.ts(av, padv(cxp, -1), pcol("c3w", c * 3 + 0), pcol("c3b", c), ALU.mult, ALU.add)
                cx.stt(av, padv(cxp, 0), pcol("c3w", c * 3 + 1), av, ALU.mult, ALU.add)
                cx.stt(av, padv(cxp, 1), pcol("c3w", c * 3 + 2), av, ALU.mult, ALU.add)
                cx.tt(acc.v(F32, 0, T), acc.v(F32, 0, T), bgs.v(F32, 0, T), ALU.mult)
                cx.tt(yB.ch(c), acc.v(F32, 0, T), gbs.v(F32, 0, T), ALU.mult)
                for b_ in (acc, bgs, gbs):
                    b_.free()
            cxp.free()
            if l == 0 and g == 0:
                dump("yB", yB.all(), [128, 8, T])

            yC = Chunked(arena, BF16, 8, T)
            pw_b, pw = ws.next(("poolw", l, g))
            icb = arena.alloc(4 * T * 4)
            icv = icb.v(F32, 0, 4 * T, pattern="p (w t) -> p w t", w=4)
            cx.dma(cx.qs, icv, bass.AP(invcs[g].tensor, 0, [[0, 128], [T, 4], [1, T]]))
            bufAs = [newpad(), newpad(), newpad()]
            bufB, bufC = newpad(), newpad()
            NW = nseq * WP
            pooleds, gcss = {}, {}

            def c_stage1(c):
                gi, cc = c // 2, c % 2
                tick()
                if cc == 0:
                    pooleds[gi] = Chunked(arena, BF16, 2, T)
                    gcss[gi] = Chunked(arena, F32, 2, T)
                bufA = bufAs[c % 3]
                for s_ in ("xp", "gc"):
                    wb, wv = ws.next(("win", l, g, s_, c))
                    for th in range(NH):
                        pb = proj(wv, 128, th, hT)
                        if s_ == "xp":
                            cx.actf(padv(bufA, 0, th), hv(pb.v()), AF.Copy)
                        else:
                            cx.actf(gcss[gi].ch(cc, th * 512, th * 512 + 512), pb.v(), AF.Silu)
                    wb.free()

            def c_stage2(c):
                gi, cc = c // 2, c % 2
                bufA = bufAs[c % 3]
                halo(bufA)
                fa = lambda b, a0, a1: b.v(F32, a0, a1)
                cx.tt(fa(bufB, 1, NW), fa(bufA, 0, NW - 1), fa(bufA, 1, NW), ALU.add)
                cur, oth, half = bufB, bufC, 1
                lo = 1
                for step in range(gi):
                    lo2 = lo + half
                    cx.tt(fa(oth, lo2, NW - lo2), fa(cur, lo2 - half, NW - lo2 - half), fa(cur, lo2 + half, NW - lo2 + half), ALU.add)
                    cur, oth = oth, cur
                    lo = lo2
                    half *= 2
                tmp = arena.alloc(T * 4)
                tv = fullv(tmp.v(F32, 0, T))
                cx.tt(tv, padv(cur, 0), fullv(sub(icv, icv.ap[:, gi, :])), ALU.mult)
                cx.tt(fullv(pooleds[gi].ch(cc)), tv, padv(bufA, 0), ALU.subtract)
                tmp.free()

            def c_stage3(gi):
                pooled, gcs = pooleds.pop(gi), gcss.pop(gi)
                for dc in range(2):
                    for th in range(NH):
                        pb = ps.alloc()
                        cx.mm(pb.v(), [(sub(pw, pw.ap[:, gi * 2 + kc, dc * 128:(dc + 1) * 128]), pooled.ch(kc, th * 512, th * 512 + 512)) for kc in range(2)])
                        cx.stt(yC.ch(gi * 2 + dc, th * 512, th * 512 + 512), pb.v(), pcol("pscale", gi * 2 + dc), gcs.ch(dc, th * 512, th * 512 + 512), ALU.mult, ALU.mult)
                pooled.free()
                gcs.free()

            c_stage1(0)
            c_stage1(1)
            for c in range(8):
                c_stage2(c)
                if c + 2 < 8:
                    c_stage1(c + 2)
                if c % 2 == 1:
                    c_stage3(c // 2)
            for b_ in bufAs + [bufB, bufC, icb, pw_b]:
                b_.free()
            if l == 0 and g == 0:
                dump("yC", yC.all(), [128, 8, T])

            ys = [yA, yB, yC, yD]
            mg = Chunked(arena, BF16, 16, T)
            for j in range(16):
                wl = []
                for i in range(4):
                    wl.append((ws.next(("win", l, g, "mg", j, i), look=8), ws.next(("bp", l, g, j, i), look=8)))
                for th in range(NH):
                    sl = (th * 512, th * 512 + 512)
                    pbufs = []
                    for i in range(4):
                        (gb_, gw), (bb_, bw) = wl[i]
                        pg = proj(gw, 128, th, hT)
                        po = ps.alloc()
                        cx.mm(po.v(), [(sub(bw, bw.ap[:, c, :]), ys[i].ch(c, *sl)) for c in range(8)])
                        sg = arena.alloc(2048)
                        cx.actf(sg.v(F32, 0, 512), pg.v(), AF.Sigmoid)
                        cx.tt(sg.v(F32, 0, 512), po.v(), sg.v(F32, 0, 512), ALU.mult)
                        pbufs.append(sg)
                    cx.tt(pbufs[0].v(F32, 0, 512), pbufs[0].v(F32, 0, 512), pbufs[1].v(F32, 0, 512), ALU.add)
                    cx.tt(pbufs[2].v(F32, 0, 512), pbufs[2].v(F32, 0, 512), pbufs[3].v(F32, 0, 512), ALU.add)
                    cx.tt(mg.ch(j, *sl), pbufs[0].v(F32, 0, 512), pbufs[2].v(F32, 0, 512), ALU.add)
                    for b_ in pbufs:
                        b_.free()
                for (gb_, _), (bb_, _) in wl:
                    gb_.free()
                    bb_.free()
            for b_ in (hT, yA, yB, yC, yD):
                b_.free()
            if l == 0 and g == 0:
                dump("mg", mg.all(), [128, 16, T])

            pre_o()
            gbc = arena.alloc(D * 4)
            cx.dma(cx.qs, gbc.v(F32, 0, D), bass.AP(gate_d[l].tensor, cnd * D, [[0, 128], [1, D]]), reads=[gate_t[l]])
            xrs = [[arena.alloc(2048) for _ in range(NT)] for _ in range(2)]
            n_need, n_done = 10, [0]

            def load_res(cb):
                for tt in range(NT):
                    cx.dma(cx.qa, xrs[cb % 2][tt].v(F32, 0, 512), xsrc[tt * 128:(tt + 1) * 128, cb * 512:(cb + 1) * 512], reads=xsrc_t[tt])

            load_res(0)
            for cb in range(4):
                wb, wv = ws.next(("wout", l, g, cb), look=3)
                if cb + 1 < 4:
                    load_res(cb + 1)
                for tt in range(NT):
                    pb = ps.alloc()
                    cx.mm(pb.v(), [(mg.ch(j, tt * 128, tt * 128 + 128), sub(wv, wv.ap[:, j, :])) for j in range(16)])
                    yo = arena.alloc(2048)
                    xr = xrs[cb % 2][tt]
                    cx.tt(yo.v(F32, 0, 512), pb.v(), gbc.v(F32, cb * 512, cb * 512 + 512), ALU.mult)
                    cx.tt(yo.v(F32, 0, 512), yo.v(F32, 0, 512), xr.v(F32, 0, 512), ALU.add)
                    cx.dma(cx.qs, ydst[tt * 128:(tt + 1) * 128, cb * 512:(cb + 1) * 512], yo.v(F32, 0, 512), writes=[ydst_t[tt][cb]])
                    yo.free()
                    step_o = cb * NT + tt + 1
                    while n_done[0] * (4 * NT) < step_o * n_need:
                        tick_o()
                        n_done[0] += 1
                wb.free()
            for r_ in xrs[0] + xrs[1]:
                r_.free()
            gbc.free()
            mg.free()

        def run_final(g, src, src_t, dst, out_t):
            NT = TG[g] // 128
            fb = arena.alloc(D * 4)
            fv = fb.v(F32, 0, D)
            cx.dma(cx.qs, fv, bass.AP(fnw.tensor, 0, [[0, 128], [1, D]]))
            xbs = [arena.alloc(8192) for _ in range(3)]

            def load(tt):
                cx.dma(cx.qa, xbs[tt % 3].v(F32, 0, D), src[tt * 128:(tt + 1) * 128, :], reads=src_t[tt])

            load(0)
            for tt in range(NT):
                if tt + 1 < NT:
                    load(tt + 1)
                xv = xbs[tt % 3].v(F32, 0, D)
                jb = arena.alloc(4096)
                ss = st.get()
                cx.actf(jb.v(BF16, 0, D), xv, AF.Square, accum=ss)
                jb.free()
                cx.actf(ss, ss, AF.Sqrt, bias=epsT.v(), scale=1.0 / D)
                cx.recip(ss, ss)
                cx.stt(xv, xv, ss, fv, ALU.mult, ALU.mult)
                cx.dma(cx.qs, dst[tt * 128:(tt + 1) * 128, :], xv, writes=[out_t])
                yield
            for b_ in xbs + [fb]:
                b_.free()

        kv_t = TObj()
        out_t = TObj()
        scr_t = [[[[TObj() for _ in range(4)] for _ in range(8)] for _ in range(2)] for _ in range(2)]
        none_t = [[] for _ in range(8)]

        def drain(gen):
            for _ in gen:
                pass

        def srcs(l, g):
            return (xin[g], none_t) if l == 0 else (scr[0][g], scr_t[0][g])

        try:
            ada0, ada1 = run_ada(0), run_ada(1)
            for _ in range(8):
                next(ada0)
            order = [(0, 0), (0, 1), (1, 0), (1, 1)]
            gN = phaseN(0, 0, *srcs(0, 0))
            hT = next(gN)
            drain(gN)
            fin0 = None
            for idx, (l, g) in enumerate(order):
                bg_gens = {(0, 0): [ada0], (0, 1): [ada1], (1, 0): [], (1, 1): []}[(l, g)]
                if (l, g) == (1, 1):
                    fin0 = run_final(0, scr[1][0], scr_t[1][0], youts[0], out_t)
                    bg_gens = [fin0]

                def tick(bg_gens=bg_gens):
                    for gen in bg_gens:
                        if next(gen, "done") != "done":
                            return

                nxt = {}

                def pre_o(idx=idx, bg_gens=bg_gens, nxt=nxt):
                    for gen in bg_gens:
                        drain(gen)
                    if idx + 1 < len(order):
                        ln, gn = order[idx + 1]
                        nxt["gen"] = phaseN(ln, gn, *srcs(ln, gn))
                        nxt["hT"] = next(nxt["gen"])

                def tick_o(nxt=nxt):
                    if "gen" in nxt:
                        next(nxt["gen"], None)

                src, src_t = srcs(l, g)
                run_group(l, g, hT, src, src_t, scr[l][g], scr_t[l][g], kv_os[g], tick, pre_o, tick_o)
                if "gen" in nxt:
                    drain(nxt["gen"])
                    hT = nxt["hT"]
            drain(run_final(1, scr[1][1], scr_t[1][1], youts[1], out_t))
        except StopBuild:
            pass
        for t in [kv_t, out_t]:
            for ev in t.w.values():
                cx.sp.wait(ev)
        for q in (cx.qs, cx.qg, cx.qa):
            for ev in q.ev:
                if ev is not None:
                    cx.sp.wait(ev)
        build_program.peak = arena.peak * Arena.G * 4
        build_program.plan = list(ws.rec)
        build_program.counts = dict(pe=cx.pe.count, act=cx.act.count, dve=cx.dve.count, pool=cx.pool.count, qs=cx.qs.n, qg=cx.qg.n, qs_cnt=cx.qs.cnt, qg_cnt=cx.qg.cnt)
    return nc, dbg_out


def _host_consts():
    consts = np.zeros((128, 384), np.float32)
    consts[:, 0:128] = np.eye(128, dtype=np.float32)
    consts[:, 128:256] = 1.0
    return consts


def _rope_tables(identity):
    n = 1024
    sc = np.float32(192.0 ** -0.5)
    if identity:
        one, zero = np.ones((64, n), np.float32), np.zeros((64, n), np.float32)
        return np.stack([one * sc, zero, one, zero]).astype(np.float32)
    pos = np.arange(n)
    r = (pos // 64).astype(np.float32)
    c = (pos % 64).astype(np.float32)
    inv = (np.float32(10000.0) ** (-np.arange(0, 32, 2, dtype=np.float32) / np.float32(32))).astype(np.float32)
    ang = np.stack([r[:, None] * inv, c[:, None] * inv], axis=1).astype(np.float32)
    cos, sin = np.cos(ang).astype(np.float32), np.sin(ang).astype(np.float32)
    cosT = np.zeros((64, n), np.float32)
    sinT = np.zeros((64, n), np.float32)
    for a in range(2):
        for hf in range(2):
            rows = slice(a * 32 + hf * 16, a * 32 + hf * 16 + 16)
            cosT[rows] = cos[:, a, :].T
            sinT[rows] = (-sin[:, a, :].T) if hf == 0 else sin[:, a, :].T
    return np.stack([cosT * sc, sinT * sc, cosT, sinT]).astype(np.float32)


def _invc(L, n):
    out = np.zeros((4, n), np.float32)
    t = np.arange(L)
    for wi, w in enumerate((2, 4, 8, 16)):
        lo = np.clip(t - w // 2, 0, L)
        hi = np.clip(t - w // 2 + w, 0, L)
        v = (1.0 / (hi - lo).astype(np.float32)).astype(np.float32)
        out[wi] = np.tile(v, n // L)
    return out


def _kmask(separate):
    m = np.zeros((4, 1280), np.float32)
    if separate:
        m[:] = -30000.0
        for s in range(4):
            m[s, 256 + s * 256:256 + (s + 1) * 256] = 0.0
    return m


def _pvec(norm_w, q_norm_w, kv_norm_w, conv3_w, conv3_b, pool_scale, dw_w, dw_b, cln_w, cln_b):
    out = np.zeros((2, 128, NPV), np.float32)
    fm = lambda v: np.ascontiguousarray(v.reshape(-1, 128).T)
    for l in range(2):
        o = out[l]
        o[:, PV["normw"]:PV["normw"] + 16] = fm(norm_w[l])
        o[:, PV["qnw"]:PV["qnw"] + 4] = fm(q_norm_w[l])
        o[:, PV["kvnw"]:PV["kvnw"] + 2] = fm(kv_norm_w[l])
        o[:, PV["c3w"]:PV["c3w"] + 24] = conv3_w[l].reshape(3, 8, 128).transpose(2, 1, 0).reshape(128, 24)
        o[:, PV["c3b"]:PV["c3b"] + 8] = fm(conv3_b[l])
        o[:, PV["pscale"]:PV["pscale"] + 8] = fm(pool_scale[l])
        o[:, PV["dww"]:PV["dww"] + 248] = dw_w[l].reshape(31, 8, 128).transpose(2, 1, 0).reshape(128, 248)
        o[:, PV["dwb"]:PV["dwb"] + 8] = fm(dw_b[l])
        o[:, PV["clnw"]:PV["clnw"] + 8] = fm(cln_w[l])
        o[:, PV["clnb"]:PV["clnb"] + 8] = fm(cln_b[l])
    return out


_CACHE = {}


def kernel(x_prompt, x_sample, cache_kv, c, c_ctx, w_ada, b_ada, norm_w, w_in, q_norm_w, w_qb, kv_norm_w, w_kvb,
           conv3_w, conv3_b, pool_w, pool_scale, dw_w, dw_b, cln_w, cln_b, w_bproj, w_out, final_norm_w, _dbg=None, _stop=None, _ncores=8):
    A = lambda v: np.ascontiguousarray(np.asarray(v, dtype=np.float32))
    x_prompt, x_sample, cache_kv, c, c_ctx = map(A, (x_prompt, x_sample, cache_kv, c, c_ctx))
    key = (tuple(sorted(_dbg)) if _dbg else None, _stop)
    if key not in _CACHE:
        build_program(_dbg, _stop)
        _CACHE[key] = build_program(_dbg, _stop, build_program.plan)
    nc, dbg_out = _CACHE[key]
    consts = _host_consts()
    sel = np.zeros((2, 256), np.float32)
    sel[0, 0:128] = 1.0
    sel[1, 128:256] = 1.0
    qsel = np.zeros((4, 1024), np.float32)
    for r_ in range(4):
        qsel[r_, r_ * 256:(r_ + 1) * 256] = 1.0
    shared = dict(w_ada=A(w_ada), b_ada=A(b_ada), w_in=A(w_in), w_qb=A(w_qb), w_kvb=A(w_kvb), pool_w=A(pool_w),
                  w_bproj=A(w_bproj), w_out=A(w_out),
                  pvec=_pvec(*map(A, (norm_w, q_norm_w, kv_norm_w, conv3_w, conv3_b, pool_scale, dw_w, dw_b, cln_w, cln_b))),
                  fnw=A(final_norm_w).reshape(1, D), consts=consts, selrow_h=sel, invc1=_invc(256, 512), qsel=qsel)
    rope_s, rope_i = _rope_tables(False), _rope_tables(True)
    invc_s, invc_p = _invc(1024, 1024), _invc(256, 1024)
    km_s, km_p = _kmask(False), _kmask(True)
    g0_prompts = {i: list(range(8 + 4 * (i - 4), 12 + 4 * (i - 4))) for i in range(4, 8)}
    g1_prompts = {i: ([2 * i, 2 * i + 1] if i < 4 else [24 + 2 * (i - 4), 25 + 2 * (i - 4)]) for i in range(8)}
    in_maps = []
    for i in range(_ncores):
        m = dict(shared)
        if i < 4:
            cond0 = c[i]
            m["x0"] = np.ascontiguousarray(x_sample[i])
            m["cache"] = np.ascontiguousarray(cache_kv[i])
            m["rope"], m["invc0"], m["kmask"] = rope_s, invc_s, km_s
            m["flag"] = np.ones((128, 1), np.float32)
        else:
            cond0 = c_ctx
            m["x0"] = np.ascontiguousarray(x_prompt[g0_prompts[i]].reshape(1024, D))
            m["cache"] = np.zeros((2, 256, 320), np.float32)
            m["rope"], m["invc0"], m["kmask"] = rope_i, invc_p, km_p
            m["flag"] = np.zeros((128, 1), np.float32)
        m["x1"] = np.ascontiguousarray(x_prompt[g1_prompts[i]].reshape(512, D))
        cond = np.stack([cond0, c_ctx], axis=1)
        m["condT"] = np.ascontiguousarray(cond.reshape(16, 128, 2).transpose(1, 0, 2).reshape(128, 32))
        in_maps.append(m)
    res = run_bass_kernel_spmd(nc, in_maps, core_ids=list(range(_ncores)))
    R = res.results
    if _dbg:
        kernel.dbg = {n: [R[i]["dbg_" + n] for i in range(_ncores)] for n in dbg_out}
        if _stop:
            return None
    if _ncores < 8:
        return R
    y_prompt = np.zeros((32, 256, D), np.float32)
    y_sample = np.zeros((4, 1024, D), np.float32)
    new_kv = np.zeros((32, 2, 256, 320), np.float32)
    for i in range(8):
        y_prompt[g1_prompts[i]] = R[i]["y1"].reshape(2, 256, D)
        new_kv[g1_prompts[i]] = R[i]["kv_o1"]
        if i < 4:
            y_sample[i] = R[i]["y0"]
        else:
            y_prompt[g0_prompts[i]] = R[i]["y0"].reshape(4, 256, D)
            new_kv[g0_prompts[i]] = R[i]["kv_o0"]
    return (y_prompt.astype(np.float32), y_sample.astype(np.float32), new_kv.astype(np.float32))
```

```python
import contextlib
import numpy as np
import concourse.bass as bass
import concourse.mybir as mybir
from concourse.bass_utils import run_bass_kernel_spmd

F32, BF16 = mybir.dt.float32, mybir.dt.bfloat16
AF = mybir.ActivationFunctionType
ALU = mybir.AluOpType
AX = mybir.AxisListType

D = 2048
W = 1024
T = 1024
IN_COLS = 19264
EPS = 1e-6
SEG = dict(qa=0, ckv=512, kpe=768, ga=832, bg=1856, cg=2880, xc=3904, gb=4928, xp=5952, gc=6976,
           glu=8000, gd=10048, mg=11072)
PAD = 16
PV = {}
_o = 0
for _n, _w in (("normw", 16), ("qnw", 4), ("kvnw", 2), ("c3w", 24), ("c3b", 8), ("pscale", 8),
               ("dww", 248), ("dwb", 8), ("clnw", 8), ("clnb", 8)):
    PV[_n] = _o
    _o += _w
NPV = _o


class Ev:
    __slots__ = ("sem", "val")

    def __init__(self, sem, val):
        self.sem, self.val = sem, val


class TObj:
    __slots__ = ("w", "r", "excl")

    def __init__(self, excl=False):
        self.w = {}
        self.r = {}
        self.excl = excl


class View:
    __slots__ = ("ap", "t")

    def __init__(self, ap, t):
        self.ap, self.t = ap, t


class Eng:
    def __init__(self, h, sem, skip_self=False):
        self.h, self.sem, self.count, self.waited, self.skip_self = h, sem, 0, {}, skip_self

    def wait(self, ev):
        if ev.sem is self.sem and self.skip_self:
            return
        k = id(ev.sem)
        if self.waited.get(k, 0) >= ev.val:
            return
        self.h.wait_ge(ev.sem, ev.val)
        self.waited[k] = ev.val


class Queue:
    def __init__(self, eng, sems):
        self.eng, self.sems = eng, sems
        self.cnt = [0] * len(sems)
        self.ev = [None] * len(sems)
        self.n = 0


def _tl(xs):
    out = []
    for x in xs:
        if x is None:
            continue
        if isinstance(x, TObj):
            out.append(x)
        elif isinstance(x, View):
            out.extend(x.t)
        else:
            out.extend(_tl(x))
    return out


class Ctx:
    def __init__(self, nc, es):
        self.nc, self.es = nc, es
        sem = lambda n: es.enter_context(nc.semaphore(n))
        self.pe = Eng(nc.tensor, sem("s_pe"), skip_self=True)
        self.act = Eng(nc.scalar, sem("s_act"))
        self.dve = Eng(nc.vector, sem("s_dve"))
        self.pool = Eng(nc.gpsimd, sem("s_pool"))
        self.sp = Eng(nc.sync, sem("s_sp"))
        self.qs = Queue(self.sp, [sem(f"qs{i}") for i in range(16)])
        self.qg = Queue(self.pool, [sem(f"qg{i}") for i in range(16)])
        self.alt = 0

    def _deps(self, eng, reads, writes):
        need = {}

        def add(ev):
            k = id(ev.sem)
            c = need.get(k)
            if c is None or c.val < ev.val:
                need[k] = ev

        for t in reads:
            for ev in t.w.values():
                add(ev)
            if t.excl:
                for ev in t.r.values():
                    if ev.sem is not eng.sem:
                        add(ev)
        for t in writes:
            for ev in t.w.values():
                add(ev)
            for ev in t.r.values():
                add(ev)
        for ev in need.values():
            eng.wait(ev)

    def _commit(self, ev, reads, writes):
        k = id(ev.sem)
        for t in reads:
            t.r[k] = ev
        for t in writes:
            t.w = {k: ev}
            t.r = {}

    def op(self, eng, reads, writes, emit):
        reads, writes = _tl(reads), _tl(writes)
        self._deps(eng, reads, writes)
        ins = emit()
        eng.count += 1
        ins.then_inc(eng.sem, 1)
        self._commit(Ev(eng.sem, eng.count), reads, writes)

    def dma(self, q, out, in_, reads=(), writes=()):
        reads, writes = _tl(reads), _tl(writes)
        oap, iap = out, in_
        if isinstance(out, View):
            writes = writes + out.t
            oap = out.ap
        if isinstance(in_, View):
            reads = reads + in_.t
            iap = in_.ap
        self._deps(q.eng, reads, writes)
        s = q.n % len(q.sems)
        if q.ev[s] is not None:
            q.eng.wait(q.ev[s])
        q.cnt[s] += 16
        q.eng.h.dma_start(out=oap, in_=iap).then_inc(q.sems[s], 16)
        ev = Ev(q.sems[s], q.cnt[s])
        q.ev[s] = ev
        q.n += 1
        k = id(ev.sem)
        for t in reads:
            t.r[k] = ev
        for t in writes:
            t.w[k] = ev
            t.r = {}

    def mm(self, out, pairs, extra=None):
        reads = [p[0] for p in pairs] + [p[1] for p in pairs]
        n = len(pairs)

        def emit():
            ins = None
            for i, (l, r) in enumerate(pairs):
                ins = self.nc.tensor.matmul(out.ap, lhsT=l.ap, rhs=r.ap, start=(i == 0), stop=(i == n - 1))
            return ins

        self.op(self.pe, reads, [out], emit)

    def tr(self, out, in_, ident):
        self.op(self.pe, [in_, ident], [out], lambda: self.nc.tensor.transpose(out.ap, in_.ap, ident.ap))

    def trs(self, outs_ins, ident):
        def emit():
            ins = None
            for o, i in outs_ins:
                ins = self.nc.tensor.transpose(o.ap, i.ap, ident.ap)
            return ins
        self.op(self.pe, [i for _, i in outs_ins] + [ident], [o for o, _ in outs_ins], emit)

    def actf(self, out, in_, func, bias=None, scale=1.0, accum=None):
        reads = [in_]
        kw = {}
        if isinstance(bias, View):
            reads.append(bias)
            kw["bias"] = bias.ap
        elif bias is not None:
            kw["bias"] = bias
        if isinstance(scale, View):
            reads.append(scale)
            kw["scale"] = scale.ap
        else:
            kw["scale"] = scale
        writes = [out]
        if accum is not None:
            writes.append(accum)
            kw["accum_out"] = accum.ap
        self.op(self.act, reads, writes, lambda: self.nc.scalar.activation(out=out.ap, in_=in_.ap, func=func, **kw))

    def tt(self, out, a, b, op, eng=None):
        eng = eng or self.dve
        self.op(eng, [a, b], [out], lambda: eng.h.tensor_tensor(out=out.ap, in0=a.ap, in1=b.ap, op=op))

    def ts(self, out, a, s1, s2, op0, op1=None, eng=None):
        eng = eng or self.dve
        reads = [a]
        v1 = s1.ap if isinstance(s1, View) else s1
        v2 = s2.ap if isinstance(s2, View) else s2
        if isinstance(s1, View):
            reads.append(s1)
        if isinstance(s2, View):
            reads.append(s2)
        if op1 is None:
            self.op(eng, reads, [out], lambda: eng.h.tensor_scalar(out=out.ap, in0=a.ap, scalar1=v1, scalar2=None, op0=op0))
        else:
            self.op(eng, reads, [out], lambda: eng.h.tensor_scalar(out=out.ap, in0=a.ap, scalar1=v1, scalar2=v2, op0=op0, op1=op1))

    def stt(self, out, a, s, b, op0, op1):
        reads = [a, b]
        sv = s
        if isinstance(s, View):
            reads.append(s)
            sv = s.ap
        self.op(self.dve, reads, [out], lambda: self.nc.vector.scalar_tensor_tensor(out=out.ap, in0=a.ap, scalar=sv, in1=b.ap, op0=op0, op1=op1))

    def copy(self, out, in_, scale=None):
        self.alt ^= 1
        if self.alt:
            self.actf(out, in_, AF.Copy, scale=(1.0 if scale is None else scale))
        elif scale is None:
            self.op(self.dve, [in_], [out], lambda: self.nc.vector.tensor_copy(out=out.ap, in_=in_.ap))
        else:
            self.ts(out, in_, scale, None, ALU.mult)

    def recip(self, out, in_):
        self.op(self.dve, [in_], [out], lambda: self.nc.vector.reciprocal(out=out.ap, in_=in_.ap))

    def memset(self, out, val):
        self.op(self.dve, [], [out], lambda: self.nc.vector.memset(out.ap, val))


class Arena:
    G = 128

    def __init__(self, nc, es, nwords):
        self.ng = nwords // self.G
        self.t = es.enter_context(nc.sbuf_tensor("arena", [128, self.ng * self.G], F32))
        self.gr = [TObj() for _ in range(self.ng)]
        self.free = [True] * self.ng
        self.peak = 0

    def nfree(self):
        return sum(self.free) * self.G * 4

    def alloc(self, nbytes, soft=False):
        n = -(-nbytes // (self.G * 4))
        run = 0
        for i in range(self.ng):
            run = run + 1 if self.free[i] else 0
            if run == n:
                g0 = i - n + 1
                for j in range(g0, i + 1):
                    self.free[j] = False
                self.peak = max(self.peak, self.ng - sum(self.free))
                return Buf(self, g0, n)
        if soft:
            return None
        raise RuntimeError(f"arena full: want {nbytes}, free {self.nfree()}")

    def release(self, b):
        for j in range(b.g0, b.g0 + b.n):
            assert not self.free[j]
            self.free[j] = True


class Buf:
    def __init__(self, arena, g0, n):
        self.a, self.g0, self.n = arena, g0, n

    def free(self):
        self.a.release(self)

    def v(self, dt, w0, w1, pattern=None, parts=128, p0=0, **kw):
        G = self.a.G
        base = self.g0 * G
        if dt == BF16:
            ap = self.a.t[p0:p0 + parts, base:base + self.n * G].bitcast(BF16)[:, w0:w1]
            f0, f1 = w0 // 2, (w1 + 1) // 2
        else:
            ap = self.a.t[p0:p0 + parts, base + w0:base + w1]
            f0, f1 = w0, w1
        if pattern:
            ap = ap.rearrange(pattern, **kw)
        ts = self.a.gr[self.g0 + f0 // G: self.g0 + (f1 - 1) // G + 1]
        return View(ap, ts)


class Chunked:
    def __init__(self, arena, dt, nch, n):
        self.dt, self.nch, self.n = dt, nch, n
        self.b = arena.alloc(nch * n * (2 if dt == BF16 else 4))

    def ch(self, c, t0=0, t1=None, parts=128, pattern=None, p0=0, **kw):
        t1 = self.n if t1 is None else t1
        return self.b.v(self.dt, c * self.n + t0, c * self.n + t1, pattern=pattern, parts=parts, p0=p0, **kw)

    def all(self, c0=0, c1=None, parts=128, p0=0):
        c1 = self.nch if c1 is None else c1
        return self.b.v(self.dt, c0 * self.n, c1 * self.n, pattern="p (c n) -> p c n", parts=parts, p0=p0, c=c1 - c0)

    def free(self):
        self.b.free()


class Psum:
    def __init__(self, nc, es):
        self.t = es.enter_context(nc.psum_tensor("psum", [128, 8 * 512], F32))
        self.banks = [TObj(excl=True) for _ in range(8)]
        self.pos = 0

    def fixed(self, b0, nb=1):
        return PB(self, b0, nb)

    def alloc(self, nb=1):
        if self.pos + nb > 8:
            self.pos = 0
        b0 = self.pos
        self.pos = (self.pos + nb) % 8
        return PB(self, b0, nb)


class PB:
    def __init__(self, ps, b0, nb):
        self.ps, self.b0, self.nb = ps, b0, nb

    def v(self, n0=0, n1=512, parts=128, p0=0, dt=F32, pattern=None, **kw):
        base = self.b0 * 512
        if dt == BF16:
            ap = self.ps.t[p0:p0 + parts, base:base + self.nb * 512].bitcast(BF16)[:, n0:n1]
        else:
            ap = self.ps.t[p0:p0 + parts, base + n0:base + n1]
        if pattern:
            ap = ap.rearrange(pattern, **kw)
        return View(ap, self.ps.banks[self.b0:self.b0 + self.nb])


class Small:
    def __init__(self, nc, es, name, shape, dt):
        self.t = es.enter_context(nc.sbuf_tensor(name, shape, dt))
        self.o = TObj()

    def v(self, *idx):
        ap = self.t[idx] if idx else self.t[:]
        return View(ap, [self.o])


class Stats:
    def __init__(self, nc, es, n=256):
        self.t = es.enter_context(nc.sbuf_tensor("stats", [128, n], F32))
        self.o = [TObj() for _ in range(n)]
        self.i = 0

    def get(self):
        i = self.i
        self.i = (self.i + 1) % len(self.o)
        return View(self.t[:, i:i + 1], [self.o[i]])


class WStream:
    def __init__(self, cx, arena, reserve, spec_of, plan=None):
        self.cx, self.arena, self.reserve, self.spec_of, self.plan = cx, arena, reserve, spec_of, plan
        self.rec = []
        self.issued = []
        self.pos = 0

    def _issue(self, key, soft):
        sap, kch, n = self.spec_of(key)
        nbytes = kch * n * 2
        if soft and self.arena.nfree() - nbytes < self.reserve:
            return False
        b = self.arena.alloc(nbytes, soft=soft)
        if b is None:
            return False
        v = b.v(BF16, 0, kch * n, pattern="p (k n) -> p k n", k=kch)
        self.cx.dma(self.cx.qg, v, sap)
        self.issued.append((b, v))
        return True

    def next(self, key, look=6):
        self.rec.append(key)
        if self.plan is None:
            self._issue(key, soft=False)
        else:
            assert self.plan[self.pos] == key, (self.plan[self.pos], key)
            while len(self.issued) <= self.pos:
                self._issue(self.plan[len(self.issued)], soft=False)
        b, v = self.issued[self.pos]
        self.pos += 1
        if self.plan is not None:
            while len(self.issued) < len(self.plan) and len(self.issued) < self.pos + look:
                if not self._issue(self.plan[len(self.issued)], soft=True):
                    break
        return b, v


def sub(v, ap):
    return View(ap, v.t)


class StopBuild(Exception):
    pass


def build_program(dbg=None, stop=None, plan=None):
    nc = bass.Bass("TRN2", target_bir_lowering=False)
    di = lambda n, s: nc.dram_tensor(n, s, F32, kind="ExternalInput").ap()
    do = lambda n, s: nc.dram_tensor(n, s, F32, kind="ExternalOutput").ap()
    TG = (1024, 512)
    xin = [di("x0", [TG[0], D]), di("x1", [TG[1], D])]
    cache = di("cache", [2, 256, 320])
    condT = di("condT", [128, 32])
    w_ada = di("w_ada", [2, D, 3 * D])
    b_ada = di("b_ada", [2, 3 * D])
    w_in = di("w_in", [2, D, IN_COLS])
    w_qb = di("w_qb", [2, 512, 1536])
    w_kvb = di("w_kvb", [2, 256, 2048])
    pool_w = di("pool_w", [2, 4, 256, 256])
    w_bproj = di("w_bproj", [2, 4, W, D])
    w_out = di("w_out", [2, D, D])
    pvec = di("pvec", [2, 128, NPV])
    fnw = di("fnw", [1, D])
    consts = di("consts", [128, 384])
    rope = di("rope", [4, 64, T])
    invcs = [di("invc0", [4, TG[0]]), di("invc1", [4, TG[1]])]
    kmask = di("kmask", [4, 1280])
    flag = di("flag", [128, 1])
    selrow_h = di("selrow_h", [2, 256])
    qsel = di("qsel", [4, 1024])
    youts = [do("y0", [TG[0], D]), do("y1", [TG[1], D])]
    kv_os = [do("kv_o0", [4, 2, 256, 320]), do("kv_o1", [2, 2, 256, 320])]
    scr = [[nc.dram_tensor(f"scr{l}_{g}", [TG[g], D], F32, kind="Internal").ap() for g in range(2)] for l in range(2)]
    dbg_out = {}

    with contextlib.ExitStack() as es:
        cx = Ctx(nc, es)
        ps = Psum(nc, es)
        st = Stats(nc, es)
        cst = Small(nc, es, "cst", [128, 384], F32)
        identb = Small(nc, es, "identb", [128, 128], BF16)
        pvs = [Small(nc, es, f"pv{l}", [128, NPV], F32) for l in range(2)]
        cond_f = Small(nc, es, "cond_f", [128, 32], F32)
        cond_b = Small(nc, es, "cond_b", [128, 32], BF16)
        modTs = [Small(nc, es, f"modT{l}", [128, 64], F32) for l in range(2)]
        amods = [Small(nc, es, f"amod{l}", [128, 32], F32) for l in range(2)]
        shiftTs = [Small(nc, es, f"shiftT{l}", [128, 32], F32) for l in range(2)]
        gate_d = [nc.dram_tensor(f"gate_scr{l}", [2, D], F32, kind="Internal").ap() for l in range(2)]
        gate_t = [TObj() for _ in range(2)]
        epsT = Small(nc, es, "epsT", [128, 1], F32)
        flagT = Small(nc, es, "flagT", [128, 1], F32)
        nfree = nc.sbuf_bytes_remaining
        arena = Arena(nc, es, (nfree - 2048) // 4)
        def spec_of(key):
            kind, l = key[0], key[1]
            std = lambda src, c0, n: src.rearrange("(k p) n -> p k n", p=128)[:, :, c0:c0 + n]
            if kind == "ada":
                return std(w_ada[l], key[2] * 512, 512), 16, 512
            if kind == "win":
                name = key[3]
                if name == "kpe":
                    return std(w_in[l], SEG["kpe"], 64), 16, 64
                if name == "mg":
                    j, i = key[4], key[5]
                    return std(w_in[l], SEG["mg"] + i * D + j * 128, 128), 16, 128
                base = {"glua": SEG["glu"], "glub": SEG["glu"] + 1024}.get(name)
                base = SEG[name] if base is None else base
                return std(w_in[l], base + key[4] * 128, 128), 16, 128
            if kind == "wqb":
                return std(w_qb[l], 0, 1536), 4, 1536
            if kind == "wkvb":
                return std(w_kvb[l], 0, 2048), 2, 2048
            if kind == "poolw":
                return pool_w[l].rearrange("g (k p) d -> p (g k) d", p=128), 8, 256
            if kind == "bp":
                j, i = key[3], key[4]
                return std(w_bproj[l, i], j * 128, 128), 8, 128
            if kind == "wout":
                return std(w_out[l], key[3] * 512, 512), 16, 512
            raise KeyError(key)

        ws = WStream(cx, arena, 32 * 1024, spec_of, plan)

        ident = sub(cst.v(), cst.t[:, 0:128])
        ones = sub(cst.v(), cst.t[:, 128:256])
        ident2 = sub(cst.v(), cst.t[0:2, 0:2])

        cx.dma(cx.qs, cst.v(), consts)
        for l in range(2):
            cx.dma(cx.qs, pvs[l].v(), pvec[l])
        cx.dma(cx.qs, cond_f.v(), condT)
        cx.memset(epsT.v(), EPS)
        cx.dma(cx.qs, flagT.v(), flag)
        cx.op(cx.dve, [cst.v()], [identb.v()], lambda: nc.vector.tensor_copy(out=identb.t[:], in_=cst.t[:, 0:128]))
        cx.actf(cond_b.v(), cond_f.v(), AF.Silu)
        selrow = Small(nc, es, "selrow", [2, 256], F32)
        cx.dma(cx.qs, selrow.v(), selrow_h)

        dbgl = []

        def dump(name, view, shape):
            if dbg is not None and name in dbg and view is not None:
                o = nc.dram_tensor("dbg_" + name, shape, view.ap.dtype, kind="ExternalOutput").ap()
                cx.dma(cx.qs, o, view)
                dbg_out[name] = shape
                dbgl.append(name)
            if stop == name:
                raise StopBuild()

        def run_ada(l):
            modT, amod, shiftT = modTs[l], amods[l], shiftTs[l]

            def finish_mod():
                mv = modT.t[:].rearrange("p (c j) -> p j c", j=2)
                for cnd in range(2):
                    tmp = st.get()
                    cx.ts(sub(amod.v(), amod.t[:, cnd * 16:(cnd + 1) * 16]), sub(modT.v(), mv[:, cnd, 16:32]), 1.0, None, ALU.add)
                    cx.tt(sub(amod.v(), amod.t[:, cnd * 16:(cnd + 1) * 16]), sub(amod.v(), amod.t[:, cnd * 16:(cnd + 1) * 16]),
                          sub(pvs[l].v(), pvs[l].t[:, PV["normw"]:PV["normw"] + 16]), ALU.mult)
                    cx.op(cx.dve, [modT.v()], [shiftT.v()],
                          lambda cnd=cnd: nc.vector.tensor_copy(out=shiftT.t[:, cnd * 16:(cnd + 1) * 16], in_=mv[:, cnd, 0:16]))


            for nb in range(12):
                wb, wv = ws.next(("ada", l, nb), look=2)
                pb = ps.alloc()
                cbv = cond_b.v()
                cx.mm(pb.v(0, 512, parts=2),
                      [(sub(cbv, cond_b.t[:, k * 2:k * 2 + 2]), sub(wv, wv.ap[:, k, :])) for k in range(16)])
                wb.free()
                br = arena.alloc(2048)
                brv = br.v(F32, 0, 512, parts=2)
                cx.dma(cx.qs, brv, bass.AP(b_ada.tensor, l * 3 * D + nb * 512, [[0, 2], [1, 512]]))
                if nb < 8:
                    rw = arena.alloc(2048)
                    rwv = rw.v(F32, 0, 512, parts=2)
                    cx.tt(rwv, pb.v(0, 512, parts=2), brv, ALU.add)
                    pt = ps.alloc()
                    for j in range(4):
                        cx.mm(pt.v(j * 2, j * 2 + 2), [(sub(rwv, rwv.ap[:, j * 128:(j + 1) * 128]), ident2)])
                    cx.copy(sub(modT.v(), modT.t[:, nb * 8:nb * 8 + 8]), pt.v(0, 8))
                    rw.free()
                else:
                    cx.tt(brv, pb.v(0, 512, parts=2), brv, ALU.add)
                    cx.dma(cx.qs, gate_d[l][:, (nb - 8) * 512:(nb - 7) * 512], brv, writes=[gate_t[l]])
                br.free()
                if nb == 7:
                    finish_mod()
                if nb < 11:
                    yield
        def phaseN(l, g, xsrc, xsrc_t):
            T = TG[g]
            NT = T // 128
            cnd = g
            amod, shiftT = amods[l], shiftTs[l]
            hT = Chunked(arena, BF16, 16, T)
            xbs = [arena.alloc(8192) for _ in range(3)]
            jb = arena.alloc(4096)
            yield hT

            def load(tt):
                cx.dma(cx.qs, xbs[tt % 3].v(F32, 0, D), xsrc[tt * 128:(tt + 1) * 128, :], reads=xsrc_t[tt])

            def front(tt):
                xv = xbs[tt % 3].v(F32, 0, D)
                ss, rs = st.get(), st.get()
                cx.actf(jb.v(BF16, 0, D), xv, AF.Square, accum=ss)
                cx.actf(rs, ss, AF.Sqrt, bias=epsT.v(), scale=1.0 / D)
                cx.recip(rs, rs)
                cx.actf(xv, xv, AF.Copy, scale=rs)

            def back(tt):
                xv = xbs[tt % 3].v(F32, 0, D)
                for kq in range(4):
                    pb = ps.alloc()
                    cx.trs([(pb.v(j * 128, j * 128 + 128), sub(xv, xv.ap[:, (kq * 4 + j) * 128:(kq * 4 + j + 1) * 128]))
                            for j in range(4)], ident)
                    for j in range(4):
                        k = kq * 4 + j
                        a_ = sub(amod.v(), amod.t[:, cnd * 16 + k:cnd * 16 + k + 1])
                        s_ = sub(shiftT.v(), shiftT.t[:, cnd * 16 + k:cnd * 16 + k + 1])
                        if j % 2 == 0:
                            cx.ts(hT.ch(k, tt * 128, tt * 128 + 128), pb.v(j * 128, j * 128 + 128), a_, s_, ALU.mult, ALU.add)
                        else:
                            cx.actf(hT.ch(k, tt * 128, tt * 128 + 128), pb.v(j * 128, j * 128 + 128), AF.Identity, bias=s_, scale=a_)

            load(0)
            if NT > 1:
                load(1)
            front(0)
            yield None
            for tt in range(NT):
                if tt + 2 < NT:
                    load(tt + 2)
                if tt + 1 < NT:
                    front(tt + 1)
                back(tt)
                yield None
            for b_ in xbs + [jb]:
                b_.free()
        def run_group(l, g, hT, xsrc, xsrc_t, ydst, ydst_t, kvdst, tick, pre_o, tick_o):
            amod, shiftT = amods[l], shiftTs[l]
            GEN = (g == 0)
            T = TG[g]
            NH, NT = T // 512, T // 128
            cnd = g
            nseq, L = T // 256, 256
            LK = 1280 if GEN else T
            KOFF = 256 if GEN else 0
            pv = pvs[l]
            pcol = lambda name, i: sub(pv.v(), pv.t[:, PV[name] + i:PV[name] + i + 1])

            def proj(wv, M, th, hT):
                pb = ps.alloc()
                cx.mm(pb.v(0, 512, parts=M),
                      [(sub(wv, wv.ap[:, k, 0:M]), hT.ch(k, th * 512, th * 512 + 512)) for k in range(16)])
                return pb

            if l == 0 and g == 0:
                dump("hT", hT.all(), [128, 16, T])
            def rms_feat(src_f32, nch, nfeat, wname, dsts):
                for th in range(NH):
                    pbs = ps.alloc()
                    sqs = []
                    for c in range(nch):
                        sq = arena.alloc(2048)
                        cx.actf(sq.v(F32, 0, 512), src_f32.ch(c, th * 512, th * 512 + 512), AF.Square)
                        sqs.append(sq)
                    cx.mm(pbs.v(), [(ones, sq.v(F32, 0, 512)) for sq in sqs])
                    for sq in sqs:
                        sq.free()
                    rb = arena.alloc(2048)
                    rv = rb.v(F32, 0, 512)
                    cx.actf(rv, pbs.v(), AF.Sqrt, bias=epsT.v(), scale=1.0 / nfeat)
                    cx.recip(rv, rv)
                    for c in range(nch):
                        for d in dsts:
                            d(c, th, src_f32.ch(c, th * 512, th * 512 + 512), rv)
                    rb.free()

            qa = Chunked(arena, F32, 4, T)
            for c in range(4):
                wb, wv = ws.next(("win", l, g, "qa", c))
                for th in range(NH):
                    pb = proj(wv, 128, th, hT)
                    cx.copy(qa.ch(c, th * 512, th * 512 + 512), pb.v())
                wb.free()
            qn = Chunked(arena, BF16, 4, T)
            rms_feat(qa, 4, 512, "qnw", [lambda c, th, a, r: cx.stt(qn.ch(c, th * 512, th * 512 + 512), a, pcol("qnw", c), r, ALU.mult, ALU.mult)])
            qa.free()
            if l == 0 and g == 0:
                dump("qn", qn.all(), [128, 4, T])

            wqb_b, wq = ws.next(("wqb", l, g))
            qTn = Chunked(arena, BF16, 8, T)
            qTp = Chunked(arena, BF16, 8, T)
            cx.memset(qTp.all(p0=64, parts=64), 0.0)
            if GEN:
                cx.dma(cx.qg, qTp.all(p0=64, parts=4), bass.AP(qsel.tensor, 0, [[T, 4], [0, 8], [1, T]]))
            SC = 192.0 ** -0.5
            if GEN:
                rp = arena.alloc(4 * T * 4)
                rpv = [rp.v(F32, i * T, (i + 1) * T, parts=64) for i in range(4)]
                for i in range(4):
                    cx.dma(cx.qs, rpv[i], rope[i])
                wsw_b = arena.alloc(4 * 8 * 64 * 2)
                wsw = wsw_b.v(BF16, 0, 4 * 512, pattern="p (c h d) -> p c h d", c=4, h=8)
                wq4 = wq.ap.rearrange("p c (h d) -> p c h d", h=8)
                for blk, srcb in ((0, 16), (16, 0), (32, 48), (48, 32)):
                    cx.op(cx.dve, [wq], [wsw], lambda blk=blk, srcb=srcb: nc.vector.tensor_copy(
                        out=wsw.ap[:, :, :, blk:blk + 16], in_=wq4[:, :, :, 128 + srcb:128 + srcb + 16]))
            if l == 0 and g == 0:
                dump("wsw", sub(wsw, wsw_b.v(BF16, 0, 2048).ap), [128, 2048])
            for h in range(8):
                if l == 0 and g == 0 and h >= 1:
                    dump("q%dn" % (h - 1), qTn.ch(h - 1), [128, T])
                    dump("q%dp" % (h - 1), qTp.ch(h - 1, parts=64), [64, T])
                for th in range(NH):
                    sl = slice(th * 512, th * 512 + 512)
                    pb = ps.alloc()
                    cx.mm(pb.v(), [(sub(wq, wq.ap[:, c, h * 192:h * 192 + 128]), qn.ch(c, th * 512, th * 512 + 512)) for c in range(4)])
                    cx.copy(qTn.ch(h, th * 512, th * 512 + 512), pb.v(), scale=SC)
                    pb2 = ps.alloc()
                    cx.mm(pb2.v(0, 512, parts=64), [(sub(wq, wq.ap[:, c, h * 192 + 128:h * 192 + 192]), qn.ch(c, th * 512, th * 512 + 512)) for c in range(4)])
                    if not GEN:
                        cx.copy(qTp.ch(h, th * 512, th * 512 + 512, parts=64), pb2.v(0, 512, parts=64), scale=SC)
                    else:
                        pb3 = ps.alloc()
                        cx.mm(pb3.v(0, 512, parts=64), [(sub(wsw, wsw.ap[:, c, h, :]), qn.ch(c, th * 512, th * 512 + 512)) for c in range(4)])
                        t1 = arena.alloc(2048)
                        t2 = arena.alloc(2048)
                        cx.tt(t1.v(F32, 0, 512, parts=64), pb2.v(0, 512, parts=64), sub(rpv[0], rpv[0].ap[:, sl]), ALU.mult)
                        cx.tt(t2.v(F32, 0, 512, parts=64), pb3.v(0, 512, parts=64), sub(rpv[1], rpv[1].ap[:, sl]), ALU.mult)
                        cx.tt(qTp.ch(h, th * 512, th * 512 + 512, parts=64), t1.v(F32, 0, 512, parts=64), t2.v(F32, 0, 512, parts=64), ALU.add)
                        t1.free()
                        t2.free()
            qn.free()
            wqb_b.free()
            if GEN:
                wsw_b.free()
            if l == 0 and g == 0:
                dump("qTn", qTn.all(), [128, 8, T])
                dump("qTp", qTp.all(parts=64), [64, 8, T])

            ckv = Chunked(arena, F32, 2, T)
            for c in range(2):
                wb, wv = ws.next(("win", l, g, "ckv", c))
                for th in range(NH):
                    pb = proj(wv, 128, th, hT)
                    cx.copy(ckv.ch(c, th * 512, th * 512 + 512), pb.v())
                wb.free()
            ckb = Chunked(arena, BF16, 2, LK)
            dsts = [lambda c, th, a, r: cx.stt(ckb.ch(c, KOFF + th * 512, KOFF + th * 512 + 512), a, pcol("kvnw", c), r, ALU.mult, ALU.mult)]
            ckn = Chunked(arena, F32, 2, T)
            dsts.append(lambda c, th, a, r: cx.stt(ckn.ch(c, th * 512, th * 512 + 512), a, pcol("kvnw", c), r, ALU.mult, ALU.mult))
            rms_feat(ckv, 2, 256, "kvnw", dsts)
            ckv.free()
            if l == 0 and g == 0:
                dump("A2a", None, None)
            kpT = Chunked(arena, BF16, 1, LK)
            cx.memset(kpT.all(p0=64, parts=64), 0.0)
            if GEN:
                cx.dma(cx.qg, kpT.ch(0, p0=64, parts=4), kmask)
            wb, wv = ws.next(("win", l, g, "kpe"))
            kpf = Chunked(arena, F32, 1, T)
            if GEN:
                ksw_b = arena.alloc(16 * 64 * 2)
                ksw = ksw_b.v(BF16, 0, 16 * 64, pattern="p (k d) -> p k d", k=16)
                for blk, srcb in ((0, 16), (16, 0), (32, 48), (48, 32)):
                    cx.op(cx.dve, [wv], [ksw], lambda blk=blk, srcb=srcb: nc.vector.tensor_copy(
                        out=ksw.ap[:, :, blk:blk + 16], in_=wv.ap[:, :, srcb:srcb + 16]))
            for th in range(NH):
                sl = slice(th * 512, th * 512 + 512)
                pb = proj(wv, 64, th, hT)
                cx.actf(kpf.ch(0, th * 512, th * 512 + 512, parts=64), pb.v(0, 512, parts=64), AF.Copy)
                if not GEN:
                    cx.op(cx.dve, [pb.v()], [kpT.ch(0, th * 512, th * 512 + 512, parts=64)],
                          lambda th=th, pb=pb: nc.vector.tensor_copy(out=kpT.ch(0, th * 512, th * 512 + 512, parts=64).ap, in_=pb.v(0, 512, parts=64).ap))
                else:
                    pb3 = ps.alloc()
                    cx.mm(pb3.v(0, 512, parts=64), [(sub(ksw, ksw.ap[:, k, :]), hT.ch(k, th * 512, th * 512 + 512)) for k in range(16)])
                    t1 = arena.alloc(2048)
                    t2 = arena.alloc(2048)
                    cx.tt(t1.v(F32, 0, 512, parts=64), pb.v(0, 512, parts=64), sub(rpv[2], rpv[2].ap[:, sl]), ALU.mult)
                    cx.tt(t2.v(F32, 0, 512, parts=64), pb3.v(0, 512, parts=64), sub(rpv[3], rpv[3].ap[:, sl]), ALU.mult)
                    cx.tt(kpT.ch(0, KOFF + th * 512, KOFF + th * 512 + 512, parts=64), t1.v(F32, 0, 512, parts=64), t2.v(F32, 0, 512, parts=64), ALU.add)
                    t1.free()
                    t2.free()
            wb.free()
            if l == 0 and g == 0:
                dump("A2b", None, None)
            if GEN:
                ksw_b.free()
                rp.free()
                for i in range(2):
                    cb_ = arena.alloc(384 * 4)
                    cv = cb_.v(F32, 0, 384)
                    cx.memset(sub(cv, cv.ap[:, 320:384]), 0.0)
                    cx.dma(cx.qs, sub(cv, cv.ap[:, 0:320]), cache[l, i * 128:(i + 1) * 128, :])
                    pb = ps.alloc()
                    cx.trs([(pb.v(0, 128), sub(cv, cv.ap[:, 0:128])), (pb.v(128, 256), sub(cv, cv.ap[:, 128:256])),
                            (pb.v(256, 384), sub(cv, cv.ap[:, 256:384]))], ident)
                    cx.copy(ckb.ch(0, i * 128, i * 128 + 128), pb.v(0, 128))
                    cx.copy(ckb.ch(1, i * 128, i * 128 + 128), pb.v(128, 256))
                    cx.copy(kpT.ch(0, i * 128, i * 128 + 128, parts=64), pb.v(256, 384, parts=64))
                    cb_.free()
            for tt in range(NT):
                pb = ps.alloc()
                cx.trs([(pb.v(0, 128), ckn.ch(0, tt * 128, tt * 128 + 128)),
                        (pb.v(128, 256), ckn.ch(1, tt * 128, tt * 128 + 128))], ident)
                cx.tr(pb.v(256, 320), kpf.ch(0, tt * 128, tt * 128 + 128, parts=64), sub(ident, ident.ap[0:64, 0:64]))
                ob = arena.alloc(320 * 4)
                cx.copy(ob.v(F32, 0, 320), pb.v(0, 320))
                s_, r_ = tt // 2, (tt % 2) * 128
                cx.dma(cx.qs, kvdst[s_, l, r_:r_ + 128, :], ob.v(F32, 0, 320), writes=[kv_t])
                ob.free()
            ckn.free()
            kpf.free()
            if l == 0 and g == 0:
                dump("ckb", ckb.all(), [128, 2, LK])
                dump("kpT", kpT.all(parts=64), [64, 1, LK])

            wkv_b, wkv = ws.next(("wkvb", l, g))
            kTn = Chunked(arena, BF16, 8, LK)
            for h in range(8):
                k0 = 0
                while k0 < LK:
                    n = min(512, LK - k0)
                    pb = ps.alloc()
                    cx.mm(pb.v(0, n), [(sub(wkv, wkv.ap[:, c, h * 256:h * 256 + 128]), ckb.ch(c, k0, k0 + n)) for c in range(2)])
                    cx.copy(kTn.ch(h, k0, k0 + n), pb.v(0, n))
                    k0 += n
            NKT = LK // 128
            Vt = Chunked(arena, BF16, NKT, W)
            wkv4 = wkv.ap.rearrange("p c (h t d) -> p c h t d", h=8, t=2)
            for kt in range(NKT):
                for hv in range(2):
                    pb = ps.alloc()
                    cx.mm(pb.v(0, 512, pattern="p (h d) -> p h d", h=4),
                          [(ckb.ch(c, kt * 128, kt * 128 + 128), sub(wkv, wkv4[:, c, hv * 4:hv * 4 + 4, 1, :])) for c in range(2)])
                    cx.copy(Vt.ch(kt, hv * 512, hv * 512 + 512), pb.v())
            wkv_b.free()
            ckb.free()
            yA = Chunked(arena, BF16, 8, T)
            for c in range(8):
                wb, wv = ws.next(("win", l, g, "ga", c))
                for th in range(NH):
                    pb = proj(wv, 128, th, hT)
                    cx.actf(yA.ch(c, th * 512, th * 512 + 512), pb.v(), AF.Silu)
                wb.free()

            Lk = 1280 if GEN else 256
            nkt = Lk // 128
            QB = 4 if GEN else 2
            units = [(0, 8, 0)] if GEN else [(s * 256, 2, s * 256) for s in range(nseq)]
            NSB = 2
            SBK = 3 if GEN else 1
            Pb = [arena.alloc(Lk * 2) for _ in range(4)]
            PTbs = [Chunked(arena, BF16, nkt, QB * 128) for _ in range(1 if GEN else 2)]
            items = [(qbase, kb0, h, b0, qi) for (qbase, nqb, kb0) in units for h in range(8)
                     for b0 in range(0, nqb, QB) for qi in range(QB)]
            Sbuf = {}

            def stage_s(i):
                qbase, kb0, h, b0, qi = items[i]
                q0 = qbase + (b0 + qi) * 128
                seg = (b0 + qi) // 2
                S = ps.fixed((i % NSB) * SBK, SBK)
                k0 = 0
                while k0 < Lk:
                    n = min(512, Lk - k0)
                    pairs = [(qTn.ch(h, q0, q0 + 128), kTn.ch(h, kb0 + k0, kb0 + k0 + n)),
                             (qTp.ch(h, q0, q0 + 128), kpT.ch(0, kb0 + k0, kb0 + k0 + n))]
                    cx.mm(S.v(k0, k0 + n), pairs)
                    k0 += n
                Sbuf[i] = S

            Pv, Rs = {}, {}

            def stage_x1(i):
                S = Sbuf.pop(i)
                nmx, rsum = st.get(), st.get()
                cx.op(cx.dve, [S.v(0, Lk)], [nmx], lambda S=S, nmx=nmx: nc.vector.tensor_reduce(
                    out=nmx.ap, in_=S.v(0, Lk).ap, axis=AX.X, op=ALU.max, negate=True))
                P = Pb[i % 4].v(BF16, 0, Lk)
                cx.actf(P, S.v(0, Lk), AF.Exp, bias=nmx, accum=rsum)
                Pv[i], Rs[i] = P, rsum

            def stage_nt(i):
                P, rsum = Pv[i], Rs.pop(i)
                cx.recip(rsum, rsum)
                cx.ts(P, P, rsum, None, ALU.mult)
                for k8 in range(0, nkt, 8):
                    n8 = min(8, nkt - k8)
                    pt = ps.fixed(6 if k8 == 0 else 7)
                    cx.trs([(pt.v(j * 128, j * 128 + 128, dt=BF16), sub(P, P.ap[:, (k8 + j) * 128:(k8 + j + 1) * 128])) for j in range(n8)], identb.v())

            def stage_cp(i):
                qbase, kb0, h, b0, qi = items[i]
                Pv.pop(i)
                PTb = PTbs[(i // QB) % len(PTbs)]
                for k8 in range(0, nkt, 8):
                    n8 = min(8, nkt - k8)
                    pt = ps.fixed(6 if k8 == 0 else 7)
                    ptv = pt.v(0, n8 * 128, dt=BF16, pattern="p (k q) -> p k q", k=n8)
                    dstv = PTb.all(k8, k8 + n8)
                    cx.copy(sub(dstv, dstv.ap[:, :, qi * 128:(qi + 1) * 128]), ptv)
                if qi == QB - 1:
                    nq = QB * 128
                    t0 = qbase + b0 * 128
                    po = ps.fixed(7)
                    cx.mm(po.v(0, nq), [(Vt.ch((kb0 // 128) + kt, h * 128, h * 128 + 128), PTb.ch(kt, 0, nq)) for kt in range(nkt)])
                    cx.tt(yA.ch(h, t0, t0 + nq), po.v(0, nq), yA.ch(h, t0, t0 + nq), ALU.mult)

            n_it = len(items)
            stage_s(0)
            for i in range(n_it + 2):
                if i + 1 < n_it:
                    stage_s(i + 1)
                if i < n_it:
                    stage_x1(i)
                if 0 <= i - 2 < n_it:
                    stage_cp(i - 2)
                if 0 <= i - 1 < n_it:
                    stage_nt(i - 1)
            for b_ in Pb + PTbs + [qTn, qTp, kpT, kTn, Vt]:
                b_.free()
            if l == 0 and g == 0:
                dump("yA", yA.all(), [128, 8, T])

            WP = L + 2 * PAD

            def padv(buf, d, th=None):
                v = buf.v(F32, 0, nseq * WP, pattern="p (s l) -> p s l", s=nseq)
                if th is None:
                    return sub(v, v.ap[:, :, PAD + d:PAD + d + L])
                return sub(v, v.ap[:, 2 * th:2 * th + 2, PAD + d:PAD + d + L])

            def hv(view):
                return sub(view, view.ap.rearrange("p (s l) -> p s l", s=2))

            def fullv(view):
                return sub(view, view.ap.rearrange("p (s l) -> p s l", s=nseq))

            def halo(buf):
                if not GEN:
                    return
                v = buf.v(F32, 0, nseq * WP, pattern="p (s l) -> p s l", s=nseq)
                cx.ts(sub(v, v.ap[:, 0:nseq - 1, PAD + L:PAD + L + PAD]), sub(v, v.ap[:, 1:nseq, PAD:PAD + PAD]), flagT.v(), None, ALU.mult)
                cx.ts(sub(v, v.ap[:, 1:nseq, 0:PAD]), sub(v, v.ap[:, 0:nseq - 1, L:L + PAD]), flagT.v(), None, ALU.mult)

            def newpad():
                b = arena.alloc(nseq * WP * 4)
                cx.memset(b.v(F32, 0, nseq * WP), 0.0)
                return b

            yD = Chunked(arena, BF16, 8, T)
            vv = Chunked(arena, F32, 8, T)
            ups = [newpad(), newpad()]
            NPE = 24
            upbs = [arena.alloc(nseq * WP * 2) for _ in range(2)]
            dgs = [arena.alloc(NPE * 128 * 2) for _ in range(2)]

            def d_stage1(c):
                tick()
                up = ups[c % 2]
                wb, wv = ws.next(("win", l, g, "glub", c))
                sgb = arena.alloc(T * 4)
                for th in range(NH):
                    pb = proj(wv, 128, th, hT)
                    cx.actf(sgb.v(F32, th * 512, th * 512 + 512), pb.v(), AF.Sigmoid)
                wb.free()
                wb, wv = ws.next(("win", l, g, "glua", c))
                for th in range(NH):
                    pb = proj(wv, 128, th, hT)
                    cx.tt(padv(up, 0, th), hv(pb.v()), hv(sgb.v(F32, th * 512, th * 512 + 512)), ALU.mult)
                wb.free()
                sgb.free()
                halo(up)
                wb, wv = ws.next(("win", l, g, "gd", c))
                for th in range(NH):
                    pb = proj(wv, 128, th, hT)
                    cx.actf(yD.ch(c, th * 512, th * 512 + 512), pb.v(), AF.Silu)
                wb.free()
                upb = upbs[c % 2]
                cx.actf(upb.v(BF16, 0, nseq * WP), up.v(F32, 0, nseq * WP), AF.Copy)
                dgv = dgs[c % 2].v(BF16, 0, NPE * 128, pattern="p (k j) -> p k j", k=NPE)
                for k in range(NPE):
                    cx.actf(sub(dgv, dgv.ap[:, k, :]), identb.v(), AF.Copy, scale=pcol("dww", c * 31 + k))

            def d_stage2(c):
                up = ups[c % 2]
                dgv = dgs[c % 2].v(BF16, 0, NPE * 128, pattern="p (k j) -> p k j", k=NPE)
                ubv = upbs[c % 2].v(BF16, 0, nseq * WP, pattern="p (s l) -> p s l", s=nseq)
                for th in range(NH):
                    acc = ps.alloc()
                    av = hv(acc.v())
                    cx.mm(av, [(sub(dgv, dgv.ap[:, k, :]), sub(ubv, ubv.ap[:, 2 * th:2 * th + 2, PAD + k - 15:PAD + k - 15 + L]))
                               for k in range(NPE)])
                    for k in range(NPE, 31):
                        cx.stt(av, padv(up, k - 15, th), pcol("dww", c * 31 + k), av, ALU.mult, ALU.add)
                    cx.actf(vv.ch(c, th * 512, th * 512 + 512), acc.v(), AF.Identity, bias=pcol("dwb", c))

            d_stage1(0)
            for c in range(8):
                if c + 1 < 8:
                    d_stage1(c + 1)
                d_stage2(c)
            for u_ in ups + upbs + dgs:
                u_.free()
            for th in range(NH):
                sl = (th * 512, th * 512 + 512)
                p1, p2 = ps.alloc(), ps.alloc()
                cx.mm(p1.v(), [(ones, vv.ch(c, *sl)) for c in range(8)])
                sqs = []
                for c in range(8):
                    sq = arena.alloc(2048)
                    cx.actf(sq.v(F32, 0, 512), vv.ch(c, *sl), AF.Square)
                    sqs.append(sq)
                cx.mm(p2.v(), [(ones, sq.v(F32, 0, 512)) for sq in sqs])
                for sq in sqs:
                    sq.free()
                mb, rb = arena.alloc(2048), arena.alloc(2048)
                mean, rstd = mb.v(F32, 0, 512), rb.v(F32, 0, 512)
                cx.actf(mean, p1.v(), AF.Copy, scale=1.0 / W)
                cx.tt(rstd, mean, mean, ALU.mult)
                cx.stt(rstd, p2.v(), 1.0 / W, rstd, ALU.mult, ALU.subtract)
                cx.actf(rstd, rstd, AF.Sqrt, bias=epsT.v(), scale=1.0)
                cx.recip(rstd, rstd)
                for c in range(8):
                    t1 = arena.alloc(2048)
                    tv1 = t1.v(F32, 0, 512)
                    cx.tt(tv1, vv.ch(c, *sl), mean, ALU.subtract)
                    cx.tt(tv1, tv1, rstd, ALU.mult)
                    cx.actf(tv1, tv1, AF.Silu, bias=pcol("clnb", c), scale=pcol("clnw", c))
                    cx.tt(yD.ch(c, *sl), tv1, yD.ch(c, *sl), ALU.mult)
                    t1.free()
                mb.free()
                rb.free()
            vv.free()
            if l == 0 and g == 0:
                dump("yD", yD.all(), [128, 8, T])

            yB = Chunked(arena, BF16, 8, T)
            cxp = newpad()
            for c in range(8):
                tick()
                bgs = arena.alloc(T * 4)
                gbs = arena.alloc(T * 4)
                for s_ in ("bg", "gb", "cg", "xc"):
                    wb, wv = ws.next(("win", l, g, s_, c))
                    for th in range(NH):
                        pb = proj(wv, 128, th, hT)
                        sl = (th * 512, th * 512 + 512)
                        if s_ == "bg":
                            cx.actf(bgs.v(F32, *sl), pb.v(), AF.Copy)
                        elif s_ == "gb":
                            cx.actf(gbs.v(F32, *sl), pb.v(), AF.Silu)
                        elif s_ == "cg":
                            if th == 0:
                                cgs = arena.alloc(T * 4)
                            cx.actf(cgs.v(F32, *sl), pb.v(), AF.Copy)
                        else:
                            cx.tt(padv(cxp, 0, th), hv(pb.v()), hv(cgs.v(F32, *sl)), ALU.mult)
                    wb.free()
                cgs.free()
                halo(cxp)
                acc = arena.alloc(T * 4)
                av = fullv(acc.v(F32, 0, T))
                cx.ts(av, padv(cxp, -1), pcol("c3w", c * 3 + 0), pcol("c3b", c), ALU.mult, ALU.add)
                cx.stt(av, padv(cxp, 0), pcol("c3w", c * 3 + 1), av, ALU.mult, ALU.add)
                cx.stt(av, padv(cxp, 1), pcol("c3w", c * 3 + 2), av, ALU.mult, ALU.add)
                cx.tt(acc.v(F32, 0, T), acc.v(F32, 0, T), bgs.v(F32, 0, T), ALU.mult)
                cx.tt(yB.ch(c), acc.v(F32, 0, T), gbs.v(F32, 0, T), ALU.mult)
                for b_ in (acc, bgs, gbs):
                    b_.free()
            cxp.free()
            if l == 0 and g == 0:
                dump("yB", yB.all(), [128, 8, T])

            yC = Chunked(arena, BF16, 8, T)
            pw_b, pw = ws.next(("poolw", l, g))
            icb = arena.alloc(4 * T * 4)
            icv = icb.v(F32, 0, 4 * T, pattern="p (w t) -> p w t", w=4)
            cx.dma(cx.qs, icv, bass.AP(invcs[g].tensor, 0, [[0, 128], [T, 4], [1, T]]))
            bufAs = [newpad(), newpad(), newpad()]
            bufB, bufC = newpad(), newpad()
            NW = nseq * WP
            pooleds, gcss = {}, {}

            def c_stage1(c):
                gi, cc = c // 2, c % 2
                tick()
                if cc == 0:
                    pooleds[gi] = Chunked(arena, BF16, 2, T)
                    gcss[gi] = Chunked(arena, F32, 2, T)
                bufA = bufAs[c % 3]
                for s_ in ("xp", "gc"):
                    wb, wv = ws.next(("win", l, g, s_, c))
                    for th in range(NH):
                        pb = proj(wv, 128, th, hT)
                        if s_ == "xp":
                            cx.actf(padv(bufA, 0, th), hv(pb.v()), AF.Copy)
                        else:
                            cx.actf(gcss[gi].ch(cc, th * 512, th * 512 + 512), pb.v(), AF.Silu)
                    wb.free()

            def c_stage2(c):
                gi, cc = c // 2, c % 2
                bufA = bufAs[c % 3]
                halo(bufA)
                fa = lambda b, a0, a1: b.v(F32, a0, a1)
                cx.tt(fa(bufB, 1, NW), fa(bufA, 0, NW - 1), fa(bufA, 1, NW), ALU.add)
                cur, oth, half = bufB, bufC, 1
                lo = 1
                for step in range(gi):
                    lo2 = lo + half
                    cx.tt(fa(oth, lo2, NW - lo2), fa(cur, lo2 - half, NW - lo2 - half), fa(cur, lo2 + half, NW - lo2 + half), ALU.add)
                    cur, oth = oth, cur
                    lo = lo2
                    half *= 2
                tmp = arena.alloc(T * 4)
                tv = fullv(tmp.v(F32, 0, T))
                cx.tt(tv, padv(cur, 0), fullv(sub(icv, icv.ap[:, gi, :])), ALU.mult)
                cx.tt(fullv(pooleds[gi].ch(cc)), tv, padv(bufA, 0), ALU.subtract)
                tmp.free()

            def c_stage3(gi):
                pooled, gcs = pooleds.pop(gi), gcss.pop(gi)
                for dc in range(2):
                    for th in range(NH):
                        pb = ps.alloc()
                        cx.mm(pb.v(), [(sub(pw, pw.ap[:, gi * 2 + kc, dc * 128:(dc + 1) * 128]), pooled.ch(kc, th * 512, th * 512 + 512)) for kc in range(2)])
                        cx.stt(yC.ch(gi * 2 + dc, th * 512, th * 512 + 512), pb.v(), pcol("pscale", gi * 2 + dc), gcs.ch(dc, th * 512, th * 512 + 512), ALU.mult, ALU.mult)
                pooled.free()
                gcs.free()

            c_stage1(0)
            c_stage1(1)
            for c in range(8):
                c_stage2(c)
                if c + 2 < 8:
                    c_stage1(c + 2)
                if c % 2 == 1:
                    c_stage3(c // 2)
            for b_ in bufAs + [bufB, bufC, icb, pw_b]:
                b_.free()
            if l == 0 and g == 0:
                dump("yC", yC.all(), [128, 8, T])

            ys = [yA, yB, yC, yD]
            mg = Chunked(arena, BF16, 16, T)
            for j in range(16):
                wl = []
                for i in range(4):
                    wl.append((ws.next(("win", l, g, "mg", j, i), look=8), ws.next(("bp", l, g, j, i), look=8)))
                for th in range(NH):
                    sl = (th * 512, th * 512 + 512)
                    pbufs = []
                    for i in range(4):
                        (gb_, gw), (bb_, bw) = wl[i]
                        pg = proj(gw, 128, th, hT)
                        po = ps.alloc()
                        cx.mm(po.v(), [(sub(bw, bw.ap[:, c, :]), ys[i].ch(c, *sl)) for c in range(8)])
                        sg = arena.alloc(2048)
                        cx.actf(sg.v(F32, 0, 512), pg.v(), AF.Sigmoid)
                        cx.tt(sg.v(F32, 0, 512), po.v(), sg.v(F32, 0, 512), ALU.mult)
                        pbufs.append(sg)
                    cx.tt(pbufs[0].v(F32, 0, 512), pbufs[0].v(F32, 0, 512), pbufs[1].v(F32, 0, 512), ALU.add)
                    cx.tt(pbufs[2].v(F32, 0, 512), pbufs[2].v(F32, 0, 512), pbufs[3].v(F32, 0, 512), ALU.add)
                    cx.tt(mg.ch(j, *sl), pbufs[0].v(F32, 0, 512), pbufs[2].v(F32, 0, 512), ALU.add)
                    for b_ in pbufs:
                        b_.free()
                for (gb_, _), (bb_, _) in wl:
                    gb_.free()
                    bb_.free()
            for b_ in (hT, yA, yB, yC, yD):
                b_.free()
            if l == 0 and g == 0:
                dump("mg", mg.all(), [128, 16, T])

            pre_o()
            gbc = arena.alloc(D * 4)
            cx.dma(cx.qs, gbc.v(F32, 0, D), bass.AP(gate_d[l].tensor, cnd * D, [[0, 128], [1, D]]), reads=[gate_t[l]])
            xrs = [[arena.alloc(2048) for _ in range(NT)] for _ in range(2)]
            yos = [arena.alloc(2048) for _ in range(6)]
            n_need, n_done = 10, [0]

            def load_res(cb):
                for tt in range(NT):
                    cx.dma(cx.qs, xrs[cb % 2][tt].v(F32, 0, 512), xsrc[tt * 128:(tt + 1) * 128, cb * 512:(cb + 1) * 512], reads=xsrc_t[tt])

            load_res(0)
            for cb in range(4):
                wb, wv = ws.next(("wout", l, g, cb), look=3)
                if cb + 1 < 4:
                    load_res(cb + 1)
                for tt in range(NT):
                    pb = ps.alloc()
                    cx.mm(pb.v(), [(mg.ch(j, tt * 128, tt * 128 + 128), sub(wv, wv.ap[:, j, :])) for j in range(16)])
                    yo = yos[(cb * NT + tt) % len(yos)]
                    xr = xrs[cb % 2][tt]
                    cx.tt(yo.v(F32, 0, 512), pb.v(), gbc.v(F32, cb * 512, cb * 512 + 512), ALU.mult)
                    cx.tt(yo.v(F32, 0, 512), yo.v(F32, 0, 512), xr.v(F32, 0, 512), ALU.add)
                    cx.dma(cx.qs, ydst[tt * 128:(tt + 1) * 128, cb * 512:(cb + 1) * 512], yo.v(F32, 0, 512), writes=[ydst_t[tt][cb]])
                    step_o = cb * NT + tt + 1
                    while n_done[0] * (4 * NT) < step_o * n_need:
                        tick_o()
                        n_done[0] += 1
                wb.free()
            for r_ in xrs[0] + xrs[1] + yos:
                r_.free()
            gbc.free()
            mg.free()

        def run_final(g, src, src_t, dst, out_t):
            NT = TG[g] // 128
            fb = arena.alloc(D * 4)
            fv = fb.v(F32, 0, D)
            cx.dma(cx.qs, fv, bass.AP(fnw.tensor, 0, [[0, 128], [1, D]]))
            xbs = [arena.alloc(8192) for _ in range(3)]

            def load(tt):
                cx.dma(cx.qs, xbs[tt % 3].v(F32, 0, D), src[tt * 128:(tt + 1) * 128, :], reads=src_t[tt])

            load(0)
            for tt in range(NT):
                if tt + 1 < NT:
                    load(tt + 1)
                xv = xbs[tt % 3].v(F32, 0, D)
                jb = arena.alloc(4096)
                ss = st.get()
                cx.actf(jb.v(BF16, 0, D), xv, AF.Square, accum=ss)
                jb.free()
                cx.actf(ss, ss, AF.Sqrt, bias=epsT.v(), scale=1.0 / D)
                cx.recip(ss, ss)
                cx.stt(xv, xv, ss, fv, ALU.mult, ALU.mult)
                cx.dma(cx.qs, dst[tt * 128:(tt + 1) * 128, :], xv, writes=[out_t])
                yield
            for b_ in xbs + [fb]:
                b_.free()

        kv_t = TObj()
        out_t = TObj()
        scr_t = [[[[TObj() for _ in range(4)] for _ in range(8)] for _ in range(2)] for _ in range(2)]
        none_t = [[] for _ in range(8)]

        def drain(gen):
            for _ in gen:
                pass

        def srcs(l, g):
            return (xin[g], none_t) if l == 0 else (scr[0][g], scr_t[0][g])

        try:
            ada0, ada1 = run_ada(0), run_ada(1)
            for _ in range(8):
                next(ada0)
            order = [(0, 0), (0, 1), (1, 0), (1, 1)]
            gN = phaseN(0, 0, *srcs(0, 0))
            hT = next(gN)
            drain(gN)
            fin0 = None
            for idx, (l, g) in enumerate(order):
                bg_gens = {(0, 0): [ada0], (0, 1): [ada1], (1, 0): [], (1, 1): []}[(l, g)]
                if (l, g) == (1, 1):
                    fin0 = run_final(0, scr[1][0], scr_t[1][0], youts[0], out_t)
                    bg_gens = [fin0]

                def tick(bg_gens=bg_gens):
                    for gen in bg_gens:
                        if next(gen, "done") != "done":
                            return

                nxt = {}

                def pre_o(idx=idx, bg_gens=bg_gens, nxt=nxt):
                    for gen in bg_gens:
                        drain(gen)
                    if idx + 1 < len(order):
                        ln, gn = order[idx + 1]
                        nxt["gen"] = phaseN(ln, gn, *srcs(ln, gn))
                        nxt["hT"] = next(nxt["gen"])

                def tick_o(nxt=nxt):
                    if "gen" in nxt:
                        next(nxt["gen"], None)

                src, src_t = srcs(l, g)
                run_group(l, g, hT, src, src_t, scr[l][g], scr_t[l][g], kv_os[g], tick, pre_o, tick_o)
                if "gen" in nxt:
                    drain(nxt["gen"])
                    hT = nxt["hT"]
            drain(run_final(1, scr[1][1], scr_t[1][1], youts[1], out_t))
        except StopBuild:
            pass
        for t in [kv_t, out_t]:
            for ev in t.w.values():
                cx.sp.wait(ev)
        for q in (cx.qs, cx.qg):
            for ev in q.ev:
                if ev is not None:
                    cx.sp.wait(ev)
        build_program.peak = arena.peak * Arena.G * 4
        build_program.plan = list(ws.rec)
        build_program.counts = dict(pe=cx.pe.count, act=cx.act.count, dve=cx.dve.count, pool=cx.pool.count, qs=cx.qs.n, qg=cx.qg.n, qs_cnt=cx.qs.cnt, qg_cnt=cx.qg.cnt)
    return nc, dbg_out


def _host_consts():
    consts = np.zeros((128, 384), np.float32)
    consts[:, 0:128] = np.eye(128, dtype=np.float32)
    consts[:, 128:256] = 1.0
    return consts


def _rope_tables(identity):
    n = 1024
    sc = np.float32(192.0 ** -0.5)
    if identity:
        one, zero = np.ones((64, n), np.float32), np.zeros((64, n), np.float32)
        return np.stack([one * sc, zero, one, zero]).astype(np.float32)
    pos = np.arange(n)
    r = (pos // 64).astype(np.float32)
    c = (pos % 64).astype(np.float32)
    inv = (np.float32(10000.0) ** (-np.arange(0, 32, 2, dtype=np.float32) / np.float32(32))).astype(np.float32)
    ang = np.stack([r[:, None] * inv, c[:, None] * inv], axis=1).astype(np.float32)
    cos, sin = np.cos(ang).astype(np.float32), np.sin(ang).astype(np.float32)
    cosT = np.zeros((64, n), np.float32)
    sinT = np.zeros((64, n), np.float32)
    for a in range(2):
        for hf in range(2):
            rows = slice(a * 32 + hf * 16, a * 32 + hf * 16 + 16)
            cosT[rows] = cos[:, a, :].T
            sinT[rows] = (-sin[:, a, :].T) if hf == 0 else sin[:, a, :].T
    return np.stack([cosT * sc, sinT * sc, cosT, sinT]).astype(np.float32)


def _invc(L, n):
    out = np.zeros((4, n), np.float32)
    t = np.arange(L)
    for wi, w in enumerate((2, 4, 8, 16)):
        lo = np.clip(t - w // 2, 0, L)
        hi = np.clip(t - w // 2 + w, 0, L)
        v = (1.0 / (hi - lo).astype(np.float32)).astype(np.float32)
        out[wi] = np.tile(v, n // L)
    return out


def _kmask(separate):
    m = np.zeros((4, 1280), np.float32)
    if separate:
        m[:] = -30000.0
        for s in range(4):
            m[s, 256 + s * 256:256 + (s + 1) * 256] = 0.0
    return m


def _pvec(norm_w, q_norm_w, kv_norm_w, conv3_w, conv3_b, pool_scale, dw_w, dw_b, cln_w, cln_b):
    out = np.zeros((2, 128, NPV), np.float32)
    fm = lambda v: np.ascontiguousarray(v.reshape(-1, 128).T)
    for l in range(2):
        o = out[l]
        o[:, PV["normw"]:PV["normw"] + 16] = fm(norm_w[l])
        o[:, PV["qnw"]:PV["qnw"] + 4] = fm(q_norm_w[l])
        o[:, PV["kvnw"]:PV["kvnw"] + 2] = fm(kv_norm_w[l])
        o[:, PV["c3w"]:PV["c3w"] + 24] = conv3_w[l].reshape(3, 8, 128).transpose(2, 1, 0).reshape(128, 24)
        o[:, PV["c3b"]:PV["c3b"] + 8] = fm(conv3_b[l])
        o[:, PV["pscale"]:PV["pscale"] + 8] = fm(pool_scale[l])
        o[:, PV["dww"]:PV["dww"] + 248] = dw_w[l].reshape(31, 8, 128).transpose(2, 1, 0).reshape(128, 248)
        o[:, PV["dwb"]:PV["dwb"] + 8] = fm(dw_b[l])
        o[:, PV["clnw"]:PV["clnw"] + 8] = fm(cln_w[l])
        o[:, PV["clnb"]:PV["clnb"] + 8] = fm(cln_b[l])
    return out


_CACHE = {}


def kernel(x_prompt, x_sample, cache_kv, c, c_ctx, w_ada, b_ada, norm_w, w_in, q_norm_w, w_qb, kv_norm_w, w_kvb,
           conv3_w, conv3_b, pool_w, pool_scale, dw_w, dw_b, cln_w, cln_b, w_bproj, w_out, final_norm_w, _dbg=None, _stop=None, _ncores=8):
    A = lambda v: np.ascontiguousarray(np.asarray(v, dtype=np.float32))
    x_prompt, x_sample, cache_kv, c, c_ctx = map(A, (x_prompt, x_sample, cache_kv, c, c_ctx))
    key = (tuple(sorted(_dbg)) if _dbg else None, _stop)
    if key not in _CACHE:
        build_program(_dbg, _stop)
        _CACHE[key] = build_program(_dbg, _stop, build_program.plan)
    nc, dbg_out = _CACHE[key]
    consts = _host_consts()
    sel = np.zeros((2, 256), np.float32)
    sel[0, 0:128] = 1.0
    sel[1, 128:256] = 1.0
    qsel = np.zeros((4, 1024), np.float32)
    for r_ in range(4):
        qsel[r_, r_ * 256:(r_ + 1) * 256] = 1.0
    shared = dict(w_ada=A(w_ada), b_ada=A(b_ada), w_in=A(w_in), w_qb=A(w_qb), w_kvb=A(w_kvb), pool_w=A(pool_w),
                  w_bproj=A(w_bproj), w_out=A(w_out),
                  pvec=_pvec(*map(A, (norm_w, q_norm_w, kv_norm_w, conv3_w, conv3_b, pool_scale, dw_w, dw_b, cln_w, cln_b))),
                  fnw=A(final_norm_w).reshape(1, D), consts=consts, selrow_h=sel, invc1=_invc(256, 512), qsel=qsel)
    rope_s, rope_i = _rope_tables(False), _rope_tables(True)
    invc_s, invc_p = _invc(1024, 1024), _invc(256, 1024)
    km_s, km_p = _kmask(False), _kmask(True)
    g0_prompts = {i: list(range(8 + 4 * (i - 4), 12 + 4 * (i - 4))) for i in range(4, 8)}
    g1_prompts = {i: ([2 * i, 2 * i + 1] if i < 4 else [24 + 2 * (i - 4), 25 + 2 * (i - 4)]) for i in range(8)}
    in_maps = []
    for i in range(_ncores):
        m = dict(shared)
        if i < 4:
            cond0 = c[i]
            m["x0"] = np.ascontiguousarray(x_sample[i])
            m["cache"] = np.ascontiguousarray(cache_kv[i])
            m["rope"], m["invc0"], m["kmask"] = rope_s, invc_s, km_s
            m["flag"] = np.ones((128, 1), np.float32)
        else:
            cond0 = c_ctx
            m["x0"] = np.ascontiguousarray(x_prompt[g0_prompts[i]].reshape(1024, D))
            m["cache"] = np.zeros((2, 256, 320), np.float32)
            m["rope"], m["invc0"], m["kmask"] = rope_i, invc_p, km_p
            m["flag"] = np.zeros((128, 1), np.float32)
        m["x1"] = np.ascontiguousarray(x_prompt[g1_prompts[i]].reshape(512, D))
        cond = np.stack([cond0, c_ctx], axis=1)
        m["condT"] = np.ascontiguousarray(cond.reshape(16, 128, 2).transpose(1, 0, 2).reshape(128, 32))
        in_maps.append(m)
    res = run_bass_kernel_spmd(nc, in_maps, core_ids=list(range(_ncores)))
    R = res.results
    if _dbg:
        kernel.dbg = {n: [R[i]["dbg_" + n] for i in range(_ncores)] for n in dbg_out}
        if _stop:
            return None
    if _ncores < 8:
        return R
    y_prompt = np.zeros((32, 256, D), np.float32)
    y_sample = np.zeros((4, 1024, D), np.float32)
    new_kv = np.zeros((32, 2, 256, 320), np.float32)
    for i in range(8):
        y_prompt[g1_prompts[i]] = R[i]["y1"].reshape(2, 256, D)
        new_kv[g1_prompts[i]] = R[i]["kv_o1"]
        if i < 4:
            y_sample[i] = R[i]["y0"]
        else:
            y_prompt[g0_prompts[i]] = R[i]["y0"].reshape(4, 256, D)
            new_kv[g0_prompts[i]] = R[i]["kv_o0"]
    return (y_prompt.astype(np.float32), y_sample.astype(np.float32), new_kv.astype(np.float32))
```

```python
import contextlib
import numpy as np
import concourse.bass as bass
import concourse.mybir as mybir
from concourse.bass_utils import run_bass_kernel_spmd

F32, BF16 = mybir.dt.float32, mybir.dt.bfloat16
AF = mybir.ActivationFunctionType
ALU = mybir.AluOpType
AX = mybir.AxisListType

D = 2048
W = 1024
T = 1024
IN_COLS = 19264
EPS = 1e-6
SEG = dict(qa=0, ckv=512, kpe=768, ga=832, bg=1856, cg=2880, xc=3904, gb=4928, xp=5952, gc=6976,
           glu=8000, gd=10048, mg=11072)
PAD = 16
PV = {}
_o = 0
for _n, _w in (("normw", 16), ("qnw", 4), ("kvnw", 2), ("c3w", 24), ("c3b", 8), ("pscale", 8),
               ("dww", 248), ("dwb", 8), ("clnw", 8), ("clnb", 8)):
    PV[_n] = _o
    _o += _w
NPV = _o


class Ev:
    __slots__ = ("sem", "val")

    def __init__(self, sem, val):
        self.sem, self.val = sem, val


class TObj:
    __slots__ = ("w", "r", "excl")

    def __init__(self, excl=False):
        self.w = {}
        self.r = {}
        self.excl = excl


class View:
    __slots__ = ("ap", "t")

    def __init__(self, ap, t):
        self.ap, self.t = ap, t


class Eng:
    def __init__(self, h, sem, skip_self=False):
        self.h, self.sem, self.count, self.waited, self.skip_self = h, sem, 0, {}, skip_self

    def wait(self, ev):
        if ev.sem is self.sem and self.skip_self:
            return
        k = id(ev.sem)
        if self.waited.get(k, 0) >= ev.val:
            return
        self.h.wait_ge(ev.sem, ev.val)
        self.waited[k] = ev.val


class Queue:
    def __init__(self, eng, sems):
        self.eng, self.sems = eng, sems
        self.cnt = [0] * len(sems)
        self.ev = [None] * len(sems)
        self.n = 0


def _tl(xs):
    out = []
    for x in xs:
        if x is None:
            continue
        if isinstance(x, TObj):
            out.append(x)
        elif isinstance(x, View):
            out.extend(x.t)
        else:
            out.extend(_tl(x))
    return out


class Ctx:
    def __init__(self, nc, es):
        self.nc, self.es = nc, es
        sem = lambda n: es.enter_context(nc.semaphore(n))
        self.pe = Eng(nc.tensor, sem("s_pe"), skip_self=True)
        self.act = Eng(nc.scalar, sem("s_act"))
        self.dve = Eng(nc.vector, sem("s_dve"))
        self.pool = Eng(nc.gpsimd, sem("s_pool"))
        self.sp = Eng(nc.sync, sem("s_sp"))
        self.qs = Queue(self.sp, [sem(f"qs{i}") for i in range(16)])
        self.qg = Queue(self.pool, [sem(f"qg{i}") for i in range(16)])
        self.alt = 0

    def _deps(self, eng, reads, writes):
        need = {}

        def add(ev):
            k = id(ev.sem)
            c = need.get(k)
            if c is None or c.val < ev.val:
                need[k] = ev

        for t in reads:
            for ev in t.w.values():
                add(ev)
            if t.excl:
                for ev in t.r.values():
                    if ev.sem is not eng.sem:
                        add(ev)
        for t in writes:
            for ev in t.w.values():
                add(ev)
            for ev in t.r.values():
                add(ev)
        for ev in need.values():
            eng.wait(ev)

    def _commit(self, ev, reads, writes):
        k = id(ev.sem)
        for t in reads:
            t.r[k] = ev
        for t in writes:
            t.w = {k: ev}
            t.r = {}

    def op(self, eng, reads, writes, emit):
        reads, writes = _tl(reads), _tl(writes)
        self._deps(eng, reads, writes)
        ins = emit()
        eng.count += 1
        ins.then_inc(eng.sem, 1)
        self._commit(Ev(eng.sem, eng.count), reads, writes)

    def dma(self, q, out, in_, reads=(), writes=()):
        reads, writes = _tl(reads), _tl(writes)
        oap, iap = out, in_
        if isinstance(out, View):
            writes = writes + out.t
            oap = out.ap
        if isinstance(in_, View):
            reads = reads + in_.t
            iap = in_.ap
        self._deps(q.eng, reads, writes)
        s = q.n % len(q.sems)
        if q.ev[s] is not None:
            q.eng.wait(q.ev[s])
        q.cnt[s] += 16
        q.eng.h.dma_start(out=oap, in_=iap).then_inc(q.sems[s], 16)
        ev = Ev(q.sems[s], q.cnt[s])
        q.ev[s] = ev
        q.n += 1
        k = id(ev.sem)
        for t in reads:
            t.r[k] = ev
        for t in writes:
            t.w[k] = ev
            t.r = {}

    def mm(self, out, pairs, extra=None):
        reads = [p[0] for p in pairs] + [p[1] for p in pairs]
        n = len(pairs)

        def emit():
            ins = None
            for i, (l, r) in enumerate(pairs):
                ins = self.nc.tensor.matmul(out.ap, lhsT=l.ap, rhs=r.ap, start=(i == 0), stop=(i == n - 1))
            return ins

        self.op(self.pe, reads, [out], emit)

    def tr(self, out, in_, ident):
        self.op(self.pe, [in_, ident], [out], lambda: self.nc.tensor.transpose(out.ap, in_.ap, ident.ap))

    def trs(self, outs_ins, ident):
        def emit():
            ins = None
            for o, i in outs_ins:
                ins = self.nc.tensor.transpose(o.ap, i.ap, ident.ap)
            return ins
        self.op(self.pe, [i for _, i in outs_ins] + [ident], [o for o, _ in outs_ins], emit)

    def actf(self, out, in_, func, bias=None, scale=1.0, accum=None):
        reads = [in_]
        kw = {}
        if isinstance(bias, View):
            reads.append(bias)
            kw["bias"] = bias.ap
        elif bias is not None:
            kw["bias"] = bias
        if isinstance(scale, View):
            reads.append(scale)
            kw["scale"] = scale.ap
        else:
            kw["scale"] = scale
        writes = [out]
        if accum is not None:
            writes.append(accum)
            kw["accum_out"] = accum.ap
        self.op(self.act, reads, writes, lambda: self.nc.scalar.activation(out=out.ap, in_=in_.ap, func=func, **kw))

    def tt(self, out, a, b, op, eng=None):
        eng = eng or self.dve
        self.op(eng, [a, b], [out], lambda: eng.h.tensor_tensor(out=out.ap, in0=a.ap, in1=b.ap, op=op))

    def ts(self, out, a, s1, s2, op0, op1=None, eng=None):
        eng = eng or self.dve
        reads = [a]
        v1 = s1.ap if isinstance(s1, View) else s1
        v2 = s2.ap if isinstance(s2, View) else s2
        if isinstance(s1, View):
            reads.append(s1)
        if isinstance(s2, View):
            reads.append(s2)
        if op1 is None:
            self.op(eng, reads, [out], lambda: eng.h.tensor_scalar(out=out.ap, in0=a.ap, scalar1=v1, scalar2=None, op0=op0))
        else:
            self.op(eng, reads, [out], lambda: eng.h.tensor_scalar(out=out.ap, in0=a.ap, scalar1=v1, scalar2=v2, op0=op0, op1=op1))

    def stt(self, out, a, s, b, op0, op1):
        reads = [a, b]
        sv = s
        if isinstance(s, View):
            reads.append(s)
            sv = s.ap
        self.op(self.dve, reads, [out], lambda: self.nc.vector.scalar_tensor_tensor(out=out.ap, in0=a.ap, scalar=sv, in1=b.ap, op0=op0, op1=op1))

    def copy(self, out, in_, scale=None):
        self.alt ^= 1
        if self.alt:
            self.actf(out, in_, AF.Copy, scale=(1.0 if scale is None else scale))
        elif scale is None:
            self.op(self.dve, [in_], [out], lambda: self.nc.vector.tensor_copy(out=out.ap, in_=in_.ap))
        else:
            self.ts(out, in_, scale, None, ALU.mult)

    def recip(self, out, in_):
        self.op(self.dve, [in_], [out], lambda: self.nc.vector.reciprocal(out=out.ap, in_=in_.ap))

    def memset(self, out, val):
        self.op(self.dve, [], [out], lambda: self.nc.vector.memset(out.ap, val))


class Arena:
    G = 128

    def __init__(self, nc, es, nwords):
        self.ng = nwords // self.G
        self.t = es.enter_context(nc.sbuf_tensor("arena", [128, self.ng * self.G], F32))
        self.gr = [TObj() for _ in range(self.ng)]
        self.free = [True] * self.ng
        self.peak = 0

    def nfree(self):
        return sum(self.free) * self.G * 4

    def alloc(self, nbytes, soft=False):
        n = -(-nbytes // (self.G * 4))
        run = 0
        for i in range(self.ng):
            run = run + 1 if self.free[i] else 0
            if run == n:
                g0 = i - n + 1
                for j in range(g0, i + 1):
                    self.free[j] = False
                self.peak = max(self.peak, self.ng - sum(self.free))
                return Buf(self, g0, n)
        if soft:
            return None
        raise RuntimeError(f"arena full: want {nbytes}, free {self.nfree()}")

    def release(self, b):
        for j in range(b.g0, b.g0 + b.n):
            assert not self.free[j]
            self.free[j] = True


class Buf:
    def __init__(self, arena, g0, n):
        self.a, self.g0, self.n = arena, g0, n

    def free(self):
        self.a.release(self)

    def v(self, dt, w0, w1, pattern=None, parts=128, p0=0, **kw):
        G = self.a.G
        base = self.g0 * G
        if dt == BF16:
            ap = self.a.t[p0:p0 + parts, base:base + self.n * G].bitcast(BF16)[:, w0:w1]
            f0, f1 = w0 // 2, (w1 + 1) // 2
        else:
            ap = self.a.t[p0:p0 + parts, base + w0:base + w1]
            f0, f1 = w0, w1
        if pattern:
            ap = ap.rearrange(pattern, **kw)
        ts = self.a.gr[self.g0 + f0 // G: self.g0 + (f1 - 1) // G + 1]
        return View(ap, ts)


class Chunked:
    def __init__(self, arena, dt, nch, n):
        self.dt, self.nch, self.n = dt, nch, n
        self.b = arena.alloc(nch * n * (2 if dt == BF16 else 4))

    def ch(self, c, t0=0, t1=None, parts=128, pattern=None, p0=0, **kw):
        t1 = self.n if t1 is None else t1
        return self.b.v(self.dt, c * self.n + t0, c * self.n + t1, pattern=pattern, parts=parts, p0=p0, **kw)

    def all(self, c0=0, c1=None, parts=128, p0=0):
        c1 = self.nch if c1 is None else c1
        return self.b.v(self.dt, c0 * self.n, c1 * self.n, pattern="p (c n) -> p c n", parts=parts, p0=p0, c=c1 - c0)

    def free(self):
        self.b.free()


class Psum:
    def __init__(self, nc, es):
        self.t = es.enter_context(nc.psum_tensor("psum", [128, 8 * 512], F32))
        self.banks = [TObj(excl=True) for _ in range(8)]
        self.pos = 0

    def fixed(self, b0, nb=1):
        return PB(self, b0, nb)

    def alloc(self, nb=1):
        if self.pos + nb > 8:
            self.pos = 0
        b0 = self.pos
        self.pos = (self.pos + nb) % 8
        return PB(self, b0, nb)


class PB:
    def __init__(self, ps, b0, nb):
        self.ps, self.b0, self.nb = ps, b0, nb

    def v(self, n0=0, n1=512, parts=128, p0=0, dt=F32, pattern=None, **kw):
        base = self.b0 * 512
        if dt == BF16:
            ap = self.ps.t[p0:p0 + parts, base:base + self.nb * 512].bitcast(BF16)[:, n0:n1]
        else:
            ap = self.ps.t[p0:p0 + parts, base + n0:base + n1]
        if pattern:
            ap = ap.rearrange(pattern, **kw)
        return View(ap, self.ps.banks[self.b0:self.b0 + self.nb])


class Small:
    def __init__(self, nc, es, name, shape, dt):
        self.t = es.enter_context(nc.sbuf_tensor(name, shape, dt))
        self.o = TObj()

    def v(self, *idx):
        ap = self.t[idx] if idx else self.t[:]
        return View(ap, [self.o])


class Stats:
    def __init__(self, nc, es, n=256):
        self.t = es.enter_context(nc.sbuf_tensor("stats", [128, n], F32))
        self.o = [TObj() for _ in range(n)]
        self.i = 0

    def get(self):
        i = self.i
        self.i = (self.i + 1) % len(self.o)
        return View(self.t[:, i:i + 1], [self.o[i]])


class WStream:
    def __init__(self, cx, arena, reserve, spec_of, plan=None):
        self.cx, self.arena, self.reserve, self.spec_of, self.plan = cx, arena, reserve, spec_of, plan
        self.rec = []
        self.issued = []
        self.pos = 0

    def _issue(self, key, soft):
        sap, kch, n = self.spec_of(key)
        nbytes = kch * n * 2
        if soft and self.arena.nfree() - nbytes < self.reserve:
            return False
        b = self.arena.alloc(nbytes, soft=soft)
        if b is None:
            return False
        v = b.v(BF16, 0, kch * n, pattern="p (k n) -> p k n", k=kch)
        self.cx.dma(self.cx.qg, v, sap)
        self.issued.append((b, v))
        return True

    def next(self, key, look=6):
        self.rec.append(key)
        if self.plan is None:
            self._issue(key, soft=False)
        else:
            assert self.plan[self.pos] == key, (self.plan[self.pos], key)
            while len(self.issued) <= self.pos:
                self._issue(self.plan[len(self.issued)], soft=False)
        b, v = self.issued[self.pos]
        self.pos += 1
        if self.plan is not None:
            while len(self.issued) < len(self.plan) and len(self.issued) < self.pos + look:
                if not self._issue(self.plan[len(self.issued)], soft=True):
                    break
        return b, v


def sub(v, ap):
    return View(ap, v.t)


class StopBuild(Exception):
    pass


def build_program(dbg=None, stop=None, plan=None):
    nc = bass.Bass("TRN2", target_bir_lowering=False)
    di = lambda n, s: nc.dram_tensor(n, s, F32, kind="ExternalInput").ap()
    do = lambda n, s: nc.dram_tensor(n, s, F32, kind="ExternalOutput").ap()
    TG = (1024, 512)
    xin = [di("x0", [TG[0], D]), di("x1", [TG[1], D])]
    cache = di("cache", [2, 256, 320])
    condT = di("condT", [128, 32])
    w_ada = di("w_ada", [2, D, 3 * D])
    b_ada = di("b_ada", [2, 3 * D])
    w_in = di("w_in", [2, D, IN_COLS])
    w_qb = di("w_qb", [2, 512, 1536])
    w_kvb = di("w_kvb", [2, 256, 2048])
    pool_w = di("pool_w", [2, 4, 256, 256])
    w_bproj = di("w_bproj", [2, 4, W, D])
    w_out = di("w_out", [2, D, D])
    pvec = di("pvec", [2, 128, NPV])
    fnw = di("fnw", [1, D])
    consts = di("consts", [128, 384])
    rope = di("rope", [4, 64, T])
    invcs = [di("invc0", [4, TG[0]]), di("invc1", [4, TG[1]])]
    kmask = di("kmask", [4, 1280])
    flag = di("flag", [128, 1])
    selrow_h = di("selrow_h", [2, 256])
    qsel = di("qsel", [4, 1024])
    youts = [do("y0", [TG[0], D]), do("y1", [TG[1], D])]
    kv_os = [do("kv_o0", [4, 2, 256, 320]), do("kv_o1", [2, 2, 256, 320])]
    scr = [[nc.dram_tensor(f"scr{l}_{g}", [TG[g], D], F32, kind="Internal").ap() for g in range(2)] for l in range(2)]
    dbg_out = {}

    with contextlib.ExitStack() as es:
        cx = Ctx(nc, es)
        ps = Psum(nc, es)
        st = Stats(nc, es)
        cst = Small(nc, es, "cst", [128, 384], F32)
        identb = Small(nc, es, "identb", [128, 128], BF16)
        pvs = [Small(nc, es, f"pv{l}", [128, NPV], F32) for l in range(2)]
        cond_f = Small(nc, es, "cond_f", [128, 32], F32)
        cond_b = Small(nc, es, "cond_b", [128, 32], BF16)
        modTs = [Small(nc, es, f"modT{l}", [128, 64], F32) for l in range(2)]
        amods = [Small(nc, es, f"amod{l}", [128, 32], F32) for l in range(2)]
        shiftTs = [Small(nc, es, f"shiftT{l}", [128, 32], F32) for l in range(2)]
        gate_d = [nc.dram_tensor(f"gate_scr{l}", [2, D], F32, kind="Internal").ap() for l in range(2)]
        gate_t = [TObj() for _ in range(2)]
        epsT = Small(nc, es, "epsT", [128, 1], F32)
        flagT = Small(nc, es, "flagT", [128, 1], F32)
        nfree = nc.sbuf_bytes_remaining
        arena = Arena(nc, es, (nfree - 2048) // 4)
        def spec_of(key):
            kind, l = key[0], key[1]
            std = lambda src, c0, n: src.rearrange("(k p) n -> p k n", p=128)[:, :, c0:c0 + n]
            if kind == "ada":
                return std(w_ada[l], key[2] * 512, 512), 16, 512
            if kind == "win":
                name = key[3]
                if name == "kpe":
                    return std(w_in[l], SEG["kpe"], 64), 16, 64
                if name == "mg":
                    j, i = key[4], key[5]
                    return std(w_in[l], SEG["mg"] + i * D + j * 128, 128), 16, 128
                base = {"glua": SEG["glu"], "glub": SEG["glu"] + 1024}.get(name)
                base = SEG[name] if base is None else base
                return std(w_in[l], base + key[4] * 128, 128), 16, 128
            if kind == "wqb":
                return std(w_qb[l], 0, 1536), 4, 1536
            if kind == "wkvb":
                return std(w_kvb[l], 0, 2048), 2, 2048
            if kind == "poolw":
                return pool_w[l].rearrange("g (k p) d -> p (g k) d", p=128), 8, 256
            if kind == "bp":
                j, i = key[3], key[4]
                return std(w_bproj[l, i], j * 128, 128), 8, 128
            if kind == "wout":
                return std(w_out[l], key[3] * 512, 512), 16, 512
            raise KeyError(key)

        ws = WStream(cx, arena, 32 * 1024, spec_of, plan)

        ident = sub(cst.v(), cst.t[:, 0:128])
        ones = sub(cst.v(), cst.t[:, 128:256])
        ident2 = sub(cst.v(), cst.t[0:2, 0:2])

        cx.dma(cx.qs, cst.v(), consts)
        for l in range(2):
            cx.dma(cx.qs, pvs[l].v(), pvec[l])
        cx.dma(cx.qs, cond_f.v(), condT)
        cx.memset(epsT.v(), EPS)
        cx.dma(cx.qs, flagT.v(), flag)
        cx.op(cx.dve, [cst.v()], [identb.v()], lambda: nc.vector.tensor_copy(out=identb.t[:], in_=cst.t[:, 0:128]))
        cx.actf(cond_b.v(), cond_f.v(), AF.Silu)
        selrow = Small(nc, es, "selrow", [2, 256], F32)
        cx.dma(cx.qs, selrow.v(), selrow_h)

        dbgl = []

        def dump(name, view, shape):
            if dbg is not None and name in dbg and view is not None:
                o = nc.dram_tensor("dbg_" + name, shape, view.ap.dtype, kind="ExternalOutput").ap()
                cx.dma(cx.qs, o, view)
                dbg_out[name] = shape
                dbgl.append(name)
            if stop == name:
                raise StopBuild()

        def run_ada(l):
            modT, amod, shiftT = modTs[l], amods[l], shiftTs[l]

            def finish_mod():
                mv = modT.t[:].rearrange("p (c j) -> p j c", j=2)
                for cnd in range(2):
                    tmp = st.get()
                    cx.ts(sub(amod.v(), amod.t[:, cnd * 16:(cnd + 1) * 16]), sub(modT.v(), mv[:, cnd, 16:32]), 1.0, None, ALU.add)
                    cx.tt(sub(amod.v(), amod.t[:, cnd * 16:(cnd + 1) * 16]), sub(amod.v(), amod.t[:, cnd * 16:(cnd + 1) * 16]),
                          sub(pvs[l].v(), pvs[l].t[:, PV["normw"]:PV["normw"] + 16]), ALU.mult)
                    cx.op(cx.dve, [modT.v()], [shiftT.v()],
                          lambda cnd=cnd: nc.vector.tensor_copy(out=shiftT.t[:, cnd * 16:(cnd + 1) * 16], in_=mv[:, cnd, 0:16]))


            for nb in range(12):
                wb, wv = ws.next(("ada", l, nb), look=2)
                pb = ps.alloc()
                cbv = cond_b.v()
                cx.mm(pb.v(0, 512, parts=2),
                      [(sub(cbv, cond_b.t[:, k * 2:k * 2 + 2]), sub(wv, wv.ap[:, k, :])) for k in range(16)])
                wb.free()
                br = arena.alloc(2048)
                brv = br.v(F32, 0, 512, parts=2)
                cx.dma(cx.qs, brv, bass.AP(b_ada.tensor, l * 3 * D + nb * 512, [[0, 2], [1, 512]]))
                if nb < 8:
                    rw = arena.alloc(2048)
                    rwv = rw.v(F32, 0, 512, parts=2)
                    cx.tt(rwv, pb.v(0, 512, parts=2), brv, ALU.add)
                    pt = ps.alloc()
                    for j in range(4):
                        cx.mm(pt.v(j * 2, j * 2 + 2), [(sub(rwv, rwv.ap[:, j * 128:(j + 1) * 128]), ident2)])
                    cx.copy(sub(modT.v(), modT.t[:, nb * 8:nb * 8 + 8]), pt.v(0, 8))
                    rw.free()
                else:
                    cx.tt(brv, pb.v(0, 512, parts=2), brv, ALU.add)
                    cx.dma(cx.qs, gate_d[l][:, (nb - 8) * 512:(nb - 7) * 512], brv, writes=[gate_t[l]])
                br.free()
                if nb == 7:
                    finish_mod()
                if nb < 11:
                    yield
        def phaseN(l, g, xsrc, xsrc_t):
            T = TG[g]
            NT = T // 128
            cnd = g
            amod, shiftT = amods[l], shiftTs[l]
            hT = Chunked(arena, BF16, 16, T)
            xbs = [arena.alloc(8192) for _ in range(3)]
            jb = arena.alloc(4096)
            yield hT

            def load(tt):
                cx.dma(cx.qs, xbs[tt % 3].v(F32, 0, D), xsrc[tt * 128:(tt + 1) * 128, :], reads=xsrc_t[tt])

            def front(tt):
                xv = xbs[tt % 3].v(F32, 0, D)
                ss, rs = st.get(), st.get()
                cx.actf(jb.v(BF16, 0, D), xv, AF.Square, accum=ss)
                cx.actf(rs, ss, AF.Sqrt, bias=epsT.v(), scale=1.0 / D)
                cx.recip(rs, rs)
                cx.actf(xv, xv, AF.Copy, scale=rs)

            def back(tt):
                xv = xbs[tt % 3].v(F32, 0, D)
                for kq in range(4):
                    pb = ps.alloc()
                    cx.trs([(pb.v(j * 128, j * 128 + 128), sub(xv, xv.ap[:, (kq * 4 + j) * 128:(kq * 4 + j + 1) * 128]))
                            for j in range(4)], ident)
                    for j in range(4):
                        k = kq * 4 + j
                        a_ = sub(amod.v(), amod.t[:, cnd * 16 + k:cnd * 16 + k + 1])
                        s_ = sub(shiftT.v(), shiftT.t[:, cnd * 16 + k:cnd * 16 + k + 1])
                        if j % 2 == 0:
                            cx.ts(hT.ch(k, tt * 128, tt * 128 + 128), pb.v(j * 128, j * 128 + 128), a_, s_, ALU.mult, ALU.add)
                        else:
                            cx.actf(hT.ch(k, tt * 128, tt * 128 + 128), pb.v(j * 128, j * 128 + 128), AF.Identity, bias=s_, scale=a_)

            load(0)
            if NT > 1:
                load(1)
            front(0)
            yield None
            for tt in range(NT):
                if tt + 2 < NT:
                    load(tt + 2)
                if tt + 1 < NT:
                    front(tt + 1)
                back(tt)
                yield None
            for b_ in xbs + [jb]:
                b_.free()
        def run_group(l, g, hT, xsrc, xsrc_t, ydst, ydst_t, kvdst, tick, pre_o, tick_o):
            amod, shiftT = amods[l], shiftTs[l]
            GEN = (g == 0)
            T = TG[g]
            NH, NT = T // 512, T // 128
            cnd = g
            nseq, L = T // 256, 256
            LK = 1280 if GEN else T
            KOFF = 256 if GEN else 0
            pv = pvs[l]
            pcol = lambda name, i: sub(pv.v(), pv.t[:, PV[name] + i:PV[name] + i + 1])

            def proj(wv, M, th, hT):
                pb = ps.alloc()
                cx.mm(pb.v(0, 512, parts=M),
                      [(sub(wv, wv.ap[:, k, 0:M]), hT.ch(k, th * 512, th * 512 + 512)) for k in range(16)])
                return pb

            if l == 0 and g == 0:
                dump("hT", hT.all(), [128, 16, T])
            def rms_feat(src_f32, nch, nfeat, wname, dsts):
                for th in range(NH):
                    pbs = ps.alloc()
                    sqs = []
                    for c in range(nch):
                        sq = arena.alloc(2048)
                        cx.actf(sq.v(F32, 0, 512), src_f32.ch(c, th * 512, th * 512 + 512), AF.Square)
                        sqs.append(sq)
                    cx.mm(pbs.v(), [(ones, sq.v(F32, 0, 512)) for sq in sqs])
                    for sq in sqs:
                        sq.free()
                    rb = arena.alloc(2048)
                    rv = rb.v(F32, 0, 512)
                    cx.actf(rv, pbs.v(), AF.Sqrt, bias=epsT.v(), scale=1.0 / nfeat)
                    cx.recip(rv, rv)
                    for c in range(nch):
                        for d in dsts:
                            d(c, th, src_f32.ch(c, th * 512, th * 512 + 512), rv)
                    rb.free()

            qa = Chunked(arena, F32, 4, T)
            for c in range(4):
                wb, wv = ws.next(("win", l, g, "qa", c))
                for th in range(NH):
                    pb = proj(wv, 128, th, hT)
                    cx.copy(qa.ch(c, th * 512, th * 512 + 512), pb.v())
                wb.free()
            qn = Chunked(arena, BF16, 4, T)
            rms_feat(qa, 4, 512, "qnw", [lambda c, th, a, r: cx.stt(qn.ch(c, th * 512, th * 512 + 512), a, pcol("qnw", c), r, ALU.mult, ALU.mult)])
            qa.free()
            if l == 0 and g == 0:
                dump("qn", qn.all(), [128, 4, T])

            wqb_b, wq = ws.next(("wqb", l, g))
            qTn = Chunked(arena, BF16, 8, T)
            qTp = Chunked(arena, BF16, 8, T)
            cx.memset(qTp.all(p0=64, parts=64), 0.0)
            if GEN:
                cx.dma(cx.qg, qTp.all(p0=64, parts=4), bass.AP(qsel.tensor, 0, [[T, 4], [0, 8], [1, T]]))
            SC = 192.0 ** -0.5
            if GEN:
                rp = arena.alloc(4 * T * 4)
                rpv = [rp.v(F32, i * T, (i + 1) * T, parts=64) for i in range(4)]
                for i in range(4):
                    cx.dma(cx.qs, rpv[i], rope[i])
                wsw_b = arena.alloc(4 * 8 * 64 * 2)
                wsw = wsw_b.v(BF16, 0, 4 * 512, pattern="p (c h d) -> p c h d", c=4, h=8)
                wq4 = wq.ap.rearrange("p c (h d) -> p c h d", h=8)
                for blk, srcb in ((0, 16), (16, 0), (32, 48), (48, 32)):
                    cx.op(cx.dve, [wq], [wsw], lambda blk=blk, srcb=srcb: nc.vector.tensor_copy(
                        out=wsw.ap[:, :, :, blk:blk + 16], in_=wq4[:, :, :, 128 + srcb:128 + srcb + 16]))
            if l == 0 and g == 0:
                dump("wsw", sub(wsw, wsw_b.v(BF16, 0, 2048).ap), [128, 2048])
            for h in range(8):
                if l == 0 and g == 0 and h >= 1:
                    dump("q%dn" % (h - 1), qTn.ch(h - 1), [128, T])
                    dump("q%dp" % (h - 1), qTp.ch(h - 1, parts=64), [64, T])
                for th in range(NH):
                    sl = slice(th * 512, th * 512 + 512)
                    pb = ps.alloc()
                    cx.mm(pb.v(), [(sub(wq, wq.ap[:, c, h * 192:h * 192 + 128]), qn.ch(c, th * 512, th * 512 + 512)) for c in range(4)])
                    cx.copy(qTn.ch(h, th * 512, th * 512 + 512), pb.v(), scale=SC)
                    pb2 = ps.alloc()
                    cx.mm(pb2.v(0, 512, parts=64), [(sub(wq, wq.ap[:, c, h * 192 + 128:h * 192 + 192]), qn.ch(c, th * 512, th * 512 + 512)) for c in range(4)])
                    if not GEN:
                        cx.copy(qTp.ch(h, th * 512, th * 512 + 512, parts=64), pb2.v(0, 512, parts=64), scale=SC)
                    else:
                        pb3 = ps.alloc()
                        cx.mm(pb3.v(0, 512, parts=64), [(sub(wsw, wsw.ap[:, c, h, :]), qn.ch(c, th * 512, th * 512 + 512)) for c in range(4)])
                        t1 = arena.alloc(2048)
                        t2 = arena.alloc(2048)
                        cx.tt(t1.v(F32, 0, 512, parts=64), pb2.v(0, 512, parts=64), sub(rpv[0], rpv[0].ap[:, sl]), ALU.mult)
                        cx.tt(t2.v(F32, 0, 512, parts=64), pb3.v(0, 512, parts=64), sub(rpv[1], rpv[1].ap[:, sl]), ALU.mult)
                        cx.tt(qTp.ch(h, th * 512, th * 512 + 512, parts=64), t1.v(F32, 0, 512, parts=64), t2.v(F32, 0, 512, parts=64), ALU.add)
                        t1.free()
                        t2.free()
            qn.free()
            wqb_b.free()
            if GEN:
                wsw_b.free()
            if l == 0 and g == 0:
                dump("qTn", qTn.all(), [128, 8, T])
                dump("qTp", qTp.all(parts=64), [64, 8, T])

            ckv = Chunked(arena, F32, 2, T)
            for c in range(2):
                wb, wv = ws.next(("win", l, g, "ckv", c))
                for th in range(NH):
                    pb = proj(wv, 128, th, hT)
                    cx.copy(ckv.ch(c, th * 512, th * 512 + 512), pb.v())
                wb.free()
            ckb = Chunked(arena, BF16, 2, LK)
            dsts = [lambda c, th, a, r: cx.stt(ckb.ch(c, KOFF + th * 512, KOFF + th * 512 + 512), a, pcol("kvnw", c), r, ALU.mult, ALU.mult)]
            ckn = Chunked(arena, F32, 2, T)
            dsts.append(lambda c, th, a, r: cx.stt(ckn.ch(c, th * 512, th * 512 + 512), a, pcol("kvnw", c), r, ALU.mult, ALU.mult))
            rms_feat(ckv, 2, 256, "kvnw", dsts)
            ckv.free()
            if l == 0 and g == 0:
                dump("A2a", None, None)
            kpT = Chunked(arena, BF16, 1, LK)
            cx.memset(kpT.all(p0=64, parts=64), 0.0)
            if GEN:
                cx.dma(cx.qg, kpT.ch(0, p0=64, parts=4), kmask)
            wb, wv = ws.next(("win", l, g, "kpe"))
            kpf = Chunked(arena, F32, 1, T)
            if GEN:
                ksw_b = arena.alloc(16 * 64 * 2)
                ksw = ksw_b.v(BF16, 0, 16 * 64, pattern="p (k d) -> p k d", k=16)
                for blk, srcb in ((0, 16), (16, 0), (32, 48), (48, 32)):
                    cx.op(cx.dve, [wv], [ksw], lambda blk=blk, srcb=srcb: nc.vector.tensor_copy(
                        out=ksw.ap[:, :, blk:blk + 16], in_=wv.ap[:, :, srcb:srcb + 16]))
            for th in range(NH):
                sl = slice(th * 512, th * 512 + 512)
                pb = proj(wv, 64, th, hT)
                cx.actf(kpf.ch(0, th * 512, th * 512 + 512, parts=64), pb.v(0, 512, parts=64), AF.Copy)
                if not GEN:
                    cx.op(cx.dve, [pb.v()], [kpT.ch(0, th * 512, th * 512 + 512, parts=64)],
                          lambda th=th, pb=pb: nc.vector.tensor_copy(out=kpT.ch(0, th * 512, th * 512 + 512, parts=64).ap, in_=pb.v(0, 512, parts=64).ap))
                else:
                    pb3 = ps.alloc()
                    cx.mm(pb3.v(0, 512, parts=64), [(sub(ksw, ksw.ap[:, k, :]), hT.ch(k, th * 512, th * 512 + 512)) for k in range(16)])
                    t1 = arena.alloc(2048)
                    t2 = arena.alloc(2048)
                    cx.tt(t1.v(F32, 0, 512, parts=64), pb.v(0, 512, parts=64), sub(rpv[2], rpv[2].ap[:, sl]), ALU.mult)
                    cx.tt(t2.v(F32, 0, 512, parts=64), pb3.v(0, 512, parts=64), sub(rpv[3], rpv[3].ap[:, sl]), ALU.mult)
                    cx.tt(kpT.ch(0, KOFF + th * 512, KOFF + th * 512 + 512, parts=64), t1.v(F32, 0, 512, parts=64), t2.v(F32, 0, 512, parts=64), ALU.add)
                    t1.free()
                    t2.free()
            wb.free()
            if l == 0 and g == 0:
                dump("A2b", None, None)
            if GEN:
                ksw_b.free()
                rp.free()
                for i in range(2):
                    cb_ = arena.alloc(384 * 4)
                    cv = cb_.v(F32, 0, 384)
                    cx.memset(sub(cv, cv.ap[:, 320:384]), 0.0)
                    cx.dma(cx.qs, sub(cv, cv.ap[:, 0:320]), cache[l, i * 128:(i + 1) * 128, :])
                    pb = ps.alloc()
                    cx.trs([(pb.v(0, 128), sub(cv, cv.ap[:, 0:128])), (pb.v(128, 256), sub(cv, cv.ap[:, 128:256])),
                            (pb.v(256, 384), sub(cv, cv.ap[:, 256:384]))], ident)
                    cx.copy(ckb.ch(0, i * 128, i * 128 + 128), pb.v(0, 128))
                    cx.copy(ckb.ch(1, i * 128, i * 128 + 128), pb.v(128, 256))
                    cx.copy(kpT.ch(0, i * 128, i * 128 + 128, parts=64), pb.v(256, 384, parts=64))
                    cb_.free()
            for tt in range(NT):
                pb = ps.alloc()
                cx.trs([(pb.v(0, 128), ckn.ch(0, tt * 128, tt * 128 + 128)),
                        (pb.v(128, 256), ckn.ch(1, tt * 128, tt * 128 + 128))], ident)
                cx.tr(pb.v(256, 320), kpf.ch(0, tt * 128, tt * 128 + 128, parts=64), sub(ident, ident.ap[0:64, 0:64]))
                ob = arena.alloc(320 * 4)
                cx.copy(ob.v(F32, 0, 320), pb.v(0, 320))
                s_, r_ = tt // 2, (tt % 2) * 128
                cx.dma(cx.qs, kvdst[s_, l, r_:r_ + 128, :], ob.v(F32, 0, 320), writes=[kv_t])
                ob.free()
            ckn.free()
            kpf.free()
            if l == 0 and g == 0:
                dump("ckb", ckb.all(), [128, 2, LK])
                dump("kpT", kpT.all(parts=64), [64, 1, LK])

            wkv_b, wkv = ws.next(("wkvb", l, g))
            kTn = Chunked(arena, BF16, 8, LK)
            for h in range(8):
                k0 = 0
                while k0 < LK:
                    n = min(512, LK - k0)
                    pb = ps.alloc()
                    cx.mm(pb.v(0, n), [(sub(wkv, wkv.ap[:, c, h * 256:h * 256 + 128]), ckb.ch(c, k0, k0 + n)) for c in range(2)])
                    cx.copy(kTn.ch(h, k0, k0 + n), pb.v(0, n))
                    k0 += n
            NKT = LK // 128
            Vt = Chunked(arena, BF16, NKT, W)
            wkv4 = wkv.ap.rearrange("p c (h t d) -> p c h t d", h=8, t=2)
            for kt in range(NKT):
                for hv in range(2):
                    pb = ps.alloc()
                    cx.mm(pb.v(0, 512, pattern="p (h d) -> p h d", h=4),
                          [(ckb.ch(c, kt * 128, kt * 128 + 128), sub(wkv, wkv4[:, c, hv * 4:hv * 4 + 4, 1, :])) for c in range(2)])
                    cx.copy(Vt.ch(kt, hv * 512, hv * 512 + 512), pb.v())
            wkv_b.free()
            ckb.free()
            yA = Chunked(arena, BF16, 8, T)
            for c in range(8):
                wb, wv = ws.next(("win", l, g, "ga", c))
                for th in range(NH):
                    pb = proj(wv, 128, th, hT)
                    cx.actf(yA.ch(c, th * 512, th * 512 + 512), pb.v(), AF.Silu)
                wb.free()

            Lk = 1280 if GEN else 256
            nkt = Lk // 128
            QB = 4 if GEN else 2
            units = [(0, 8, 0)] if GEN else [(s * 256, 2, s * 256) for s in range(nseq)]
            NSB = 2
            SBK = 3 if GEN else 1
            Pb = [arena.alloc(Lk * 2) for _ in range(4)]
            PTbs = [Chunked(arena, BF16, nkt, QB * 128) for _ in range(1 if GEN else 2)]
            items = [(qbase, kb0, h, b0, qi) for (qbase, nqb, kb0) in units for h in range(8)
                     for b0 in range(0, nqb, QB) for qi in range(QB)]
            Sbuf = {}

            def stage_s(i):
                qbase, kb0, h, b0, qi = items[i]
                q0 = qbase + (b0 + qi) * 128
                seg = (b0 + qi) // 2
                S = ps.fixed((i % NSB) * SBK, SBK)
                k0 = 0
                while k0 < Lk:
                    n = min(512, Lk - k0)
                    pairs = [(qTn.ch(h, q0, q0 + 128), kTn.ch(h, kb0 + k0, kb0 + k0 + n)),
                             (qTp.ch(h, q0, q0 + 128), kpT.ch(0, kb0 + k0, kb0 + k0 + n))]
                    cx.mm(S.v(k0, k0 + n), pairs)
                    k0 += n
                Sbuf[i] = S

            Pv, Rs = {}, {}

            def stage_x1(i):
                S = Sbuf.pop(i)
                nmx, rsum = st.get(), st.get()
                cx.op(cx.dve, [S.v(0, Lk)], [nmx], lambda S=S, nmx=nmx: nc.vector.tensor_reduce(
                    out=nmx.ap, in_=S.v(0, Lk).ap, axis=AX.X, op=ALU.max, negate=True))
                P = Pb[i % 4].v(BF16, 0, Lk)
                cx.actf(P, S.v(0, Lk), AF.Exp, bias=nmx, accum=rsum)
                Pv[i], Rs[i] = P, rsum

            def stage_nt(i):
                P, rsum = Pv[i], Rs.pop(i)
                cx.recip(rsum, rsum)
                cx.ts(P, P, rsum, None, ALU.mult)
                for k8 in range(0, nkt, 8):
                    n8 = min(8, nkt - k8)
                    pt = ps.fixed(6 if k8 == 0 else 7)
                    cx.trs([(pt.v(j * 128, j * 128 + 128, dt=BF16), sub(P, P.ap[:, (k8 + j) * 128:(k8 + j + 1) * 128])) for j in range(n8)], identb.v())

            def stage_cp(i):
                qbase, kb0, h, b0, qi = items[i]
                Pv.pop(i)
                PTb = PTbs[(i // QB) % len(PTbs)]
                for k8 in range(0, nkt, 8):
                    n8 = min(8, nkt - k8)
                    pt = ps.fixed(6 if k8 == 0 else 7)
                    ptv = pt.v(0, n8 * 128, dt=BF16, pattern="p (k q) -> p k q", k=n8)
                    dstv = PTb.all(k8, k8 + n8)
                    cx.copy(sub(dstv, dstv.ap[:, :, qi * 128:(qi + 1) * 128]), ptv)
                if qi == QB - 1:
                    nq = QB * 128
                    t0 = qbase + b0 * 128
                    po = ps.fixed(7)
                    cx.mm(po.v(0, nq), [(Vt.ch((kb0 // 128) + kt, h * 128, h * 128 + 128), PTb.ch(kt, 0, nq)) for kt in range(nkt)])
                    cx.tt(yA.ch(h, t0, t0 + nq), po.v(0, nq), yA.ch(h, t0, t0 + nq), ALU.mult)

            n_it = len(items)
            stage_s(0)
            for i in range(n_it + 2):
                if i + 1 < n_it:
                    stage_s(i + 1)
                if i < n_it:
                    stage_x1(i)
                if 0 <= i - 2 < n_it:
                    stage_cp(i - 2)
                if 0 <= i - 1 < n_it:
                    stage_nt(i - 1)
            for b_ in Pb + PTbs + [qTn, qTp, kpT, kTn, Vt]:
                b_.free()
            if l == 0 and g == 0:
                dump("yA", yA.all(), [128, 8, T])

            WP = L + 2 * PAD

            def padv(buf, d, th=None):
                v = buf.v(F32, 0, nseq * WP, pattern="p (s l) -> p s l", s=nseq)
                if th is None:
                    return sub(v, v.ap[:, :, PAD + d:PAD + d + L])
                return sub(v, v.ap[:, 2 * th:2 * th + 2, PAD + d:PAD + d + L])

            def hv(view):
                return sub(view, view.ap.rearrange("p (s l) -> p s l", s=2))

            def fullv(view):
                return sub(view, view.ap.rearrange("p (s l) -> p s l", s=nseq))

            def halo(buf):
                if not GEN:
                    return
                v = buf.v(F32, 0, nseq * WP, pattern="p (s l) -> p s l", s=nseq)
                cx.ts(sub(v, v.ap[:, 0:nseq - 1, PAD + L:PAD + L + PAD]), sub(v, v.ap[:, 1:nseq, PAD:PAD + PAD]), flagT.v(), None, ALU.mult)
                cx.ts(sub(v, v.ap[:, 1:nseq, 0:PAD]), sub(v, v.ap[:, 0:nseq - 1, L:L + PAD]), flagT.v(), None, ALU.mult)

            def newpad():
                b = arena.alloc(nseq * WP * 4)
                cx.memset(b.v(F32, 0, nseq * WP), 0.0)
                return b

            yD = Chunked(arena, BF16, 8, T)
            vv = Chunked(arena, F32, 8, T)
            ups = [newpad(), newpad()]
            NPE = 24
            upbs = [arena.alloc(nseq * WP * 2) for _ in range(2)]
            dgs = [arena.alloc(NPE * 128 * 2) for _ in range(2)]

            def d_stage1(c):
                tick()
                up = ups[c % 2]
                wb, wv = ws.next(("win", l, g, "glub", c))
                sgb = arena.alloc(T * 4)
                for th in range(NH):
                    pb = proj(wv, 128, th, hT)
                    cx.actf(sgb.v(F32, th * 512, th * 512 + 512), pb.v(), AF.Sigmoid)
                wb.free()
                wb, wv = ws.next(("win", l, g, "glua", c))
                for th in range(NH):
                    pb = proj(wv, 128, th, hT)
                    cx.tt(padv(up, 0, th), hv(pb.v()), hv(sgb.v(F32, th * 512, th * 512 + 512)), ALU.mult)
                wb.free()
                sgb.free()
                halo(up)
                wb, wv = ws.next(("win", l, g, "gd", c))
                for th in range(NH):
                    pb = proj(wv, 128, th, hT)
                    cx.actf(yD.ch(c, th * 512, th * 512 + 512), pb.v(), AF.Silu)
                wb.free()
                upb = upbs[c % 2]
                cx.actf(upb.v(BF16, 0, nseq * WP), up.v(F32, 0, nseq * WP), AF.Copy)
                dgv = dgs[c % 2].v(BF16, 0, NPE * 128, pattern="p (k j) -> p k j", k=NPE)
                for k in range(NPE):
                    cx.actf(sub(dgv, dgv.ap[:, k, :]), identb.v(), AF.Copy, scale=pcol("dww", c * 31 + k))

            def d_stage2(c):
                up = ups[c % 2]
                dgv = dgs[c % 2].v(BF16, 0, NPE * 128, pattern="p (k j) -> p k j", k=NPE)
                ubv = upbs[c % 2].v(BF16, 0, nseq * WP, pattern="p (s l) -> p s l", s=nseq)
                for th in range(NH):
                    acc = ps.alloc()
                    av = hv(acc.v())
                    cx.mm(av, [(sub(dgv, dgv.ap[:, k, :]), sub(ubv, ubv.ap[:, 2 * th:2 * th + 2, PAD + k - 15:PAD + k - 15 + L]))
                               for k in range(NPE)])
                    for k in range(NPE, 31):
                        cx.stt(av, padv(up, k - 15, th), pcol("dww", c * 31 + k), av, ALU.mult, ALU.add)
                    cx.actf(vv.ch(c, th * 512, th * 512 + 512), acc.v(), AF.Identity, bias=pcol("dwb", c))

            d_stage1(0)
            for c in range(8):
                if c + 1 < 8:
                    d_stage1(c + 1)
                d_stage2(c)
            for u_ in ups + upbs + dgs:
                u_.free()
            ln_state = []
            for th in range(NH):
                sl = (th * 512, th * 512 + 512)
                p1, p2 = ps.alloc(), ps.alloc()
                cx.mm(p1.v(), [(ones, vv.ch(c, *sl)) for c in range(8)])
                sqs = []
                for c in range(8):
                    sq = arena.alloc(2048)
                    cx.actf(sq.v(F32, 0, 512), vv.ch(c, *sl), AF.Square)
                    sqs.append(sq)
                cx.mm(p2.v(), [(ones, sq.v(F32, 0, 512)) for sq in sqs])
                for sq in sqs:
                    sq.free()
                mb, rb = arena.alloc(2048), arena.alloc(2048)
                mean, rstd = mb.v(F32, 0, 512), rb.v(F32, 0, 512)
                cx.actf(mean, p1.v(), AF.Copy, scale=1.0 / W)
                cx.tt(rstd, mean, mean, ALU.mult)
                cx.stt(rstd, p2.v(), 1.0 / W, rstd, ALU.mult, ALU.subtract)
                cx.actf(rstd, rstd, AF.Sqrt, bias=epsT.v(), scale=1.0)
                cx.recip(rstd, rstd)
                ln_state.append((sl, mb, rb, mean, rstd))

            def ln_apply():
                for c in range(8):
                    for (sl, mb, rb, mean, rstd) in ln_state:
                        t1 = arena.alloc(2048)
                        tv1 = t1.v(F32, 0, 512)
                        cx.tt(tv1, vv.ch(c, *sl), mean, ALU.subtract)
                        cx.tt(tv1, tv1, rstd, ALU.mult)
                        cx.actf(tv1, tv1, AF.Silu, bias=pcol("clnb", c), scale=pcol("clnw", c))
                        cx.tt(yD.ch(c, *sl), tv1, yD.ch(c, *sl), ALU.mult)
                        t1.free()
                    yield
                for (sl, mb, rb, _, _) in ln_state:
                    mb.free()
                    rb.free()
                vv.free()

            ln_gen = ln_apply()

            yB = Chunked(arena, BF16, 8, T)
            cxp = newpad()
            for c in range(8):
                tick()
                next(ln_gen, None)
                bgs = arena.alloc(T * 4)
                gbs = arena.alloc(T * 4)
                for s_ in ("bg", "gb", "cg", "xc"):
                    wb, wv = ws.next(("win", l, g, s_, c))
                    for th in range(NH):
                        pb = proj(wv, 128, th, hT)
                        sl = (th * 512, th * 512 + 512)
                        if s_ == "bg":
                            cx.actf(bgs.v(F32, *sl), pb.v(), AF.Copy)
                        elif s_ == "gb":
                            cx.actf(gbs.v(F32, *sl), pb.v(), AF.Silu)
                        elif s_ == "cg":
                            if th == 0:
                                cgs = arena.alloc(T * 4)
                            cx.actf(cgs.v(F32, *sl), pb.v(), AF.Copy)
                        else:
                            cx.tt(padv(cxp, 0, th), hv(pb.v()), hv(cgs.v(F32, *sl)), ALU.mult)
                    wb.free()
                cgs.free()
                halo(cxp)
                acc = arena.alloc(T * 4)
                av = fullv(acc.v(F32, 0, T))
                cx.ts(av, padv(cxp, -1), pcol("c3w", c * 3 + 0), pcol("c3b", c), ALU.mult, ALU.add)
                cx.stt(av, padv(cxp, 0), pcol("c3w", c * 3 + 1), av, ALU.mult, ALU.add)
                cx.stt(av, padv(cxp, 1), pcol("c3w", c * 3 + 2), av, ALU.mult, ALU.add)
                cx.tt(acc.v(F32, 0, T), acc.v(F32, 0, T), bgs.v(F32, 0, T), ALU.mult)
                cx.tt(yB.ch(c), acc.v(F32, 0, T), gbs.v(F32, 0, T), ALU.mult)
                for b_ in (acc, bgs, gbs):
                    b_.free()
            cxp.free()
            for _ in ln_gen:
                pass
            if l == 0 and g == 0:
                dump("yD", yD.all(), [128, 8, T])
                dump("yB", yB.all(), [128, 8, T])

            yC = Chunked(arena, BF16, 8, T)
            pw_b, pw = ws.next(("poolw", l, g))
            icb = arena.alloc(4 * T * 4)
            icv = icb.v(F32, 0, 4 * T, pattern="p (w t) -> p w t", w=4)
            cx.dma(cx.qs, icv, bass.AP(invcs[g].tensor, 0, [[0, 128], [T, 4], [1, T]]))
            bufAs = [newpad(), newpad(), newpad()]
            bufB, bufC = newpad(), newpad()
            NW = nseq * WP
            pooleds, gcss = {}, {}

            def c_stage1(c):
                gi, cc = c // 2, c % 2
                tick()
                if cc == 0:
                    pooleds[gi] = Chunked(arena, BF16, 2, T)
                    gcss[gi] = Chunked(arena, F32, 2, T)
                bufA = bufAs[c % 3]
                for s_ in ("xp", "gc"):
                    wb, wv = ws.next(("win", l, g, s_, c))
                    for th in range(NH):
                        pb = proj(wv, 128, th, hT)
                        if s_ == "xp":
                            cx.actf(padv(bufA, 0, th), hv(pb.v()), AF.Copy)
                        else:
                            cx.actf(gcss[gi].ch(cc, th * 512, th * 512 + 512), pb.v(), AF.Silu)
                    wb.free()

            def c_stage2(c):
                gi, cc = c // 2, c % 2
                bufA = bufAs[c % 3]
                halo(bufA)
                fa = lambda b, a0, a1: b.v(F32, a0, a1)
                cx.tt(fa(bufB, 1, NW), fa(bufA, 0, NW - 1), fa(bufA, 1, NW), ALU.add)
                cur, oth, half = bufB, bufC, 1
                lo = 1
                for step in range(gi):
                    lo2 = lo + half
                    cx.tt(fa(oth, lo2, NW - lo2), fa(cur, lo2 - half, NW - lo2 - half), fa(cur, lo2 + half, NW - lo2 + half), ALU.add)
                    cur, oth = oth, cur
                    lo = lo2
                    half *= 2
                tmp = arena.alloc(T * 4)
                tv = fullv(tmp.v(F32, 0, T))
                cx.tt(tv, padv(cur, 0), fullv(sub(icv, icv.ap[:, gi, :])), ALU.mult)
                cx.tt(fullv(pooleds[gi].ch(cc)), tv, padv(bufA, 0), ALU.subtract)
                tmp.free()

            def c_stage3(gi):
                pooled, gcs = pooleds.pop(gi), gcss.pop(gi)
                for dc in range(2):
                    for th in range(NH):
                        pb = ps.alloc()
                        cx.mm(pb.v(), [(sub(pw, pw.ap[:, gi * 2 + kc, dc * 128:(dc + 1) * 128]), pooled.ch(kc, th * 512, th * 512 + 512)) for kc in range(2)])
                        cx.stt(yC.ch(gi * 2 + dc, th * 512, th * 512 + 512), pb.v(), pcol("pscale", gi * 2 + dc), gcs.ch(dc, th * 512, th * 512 + 512), ALU.mult, ALU.mult)
                pooled.free()
                gcs.free()

            c_stage1(0)
            c_stage1(1)
            for c in range(8):
                c_stage2(c)
                if c + 2 < 8:
                    c_stage1(c + 2)
                if c % 2 == 1:
                    c_stage3(c // 2)
            for b_ in bufAs + [bufB, bufC, icb, pw_b]:
                b_.free()
            if l == 0 and g == 0:
                dump("yC", yC.all(), [128, 8, T])

            ys = [yA, yB, yC, yD]
            mg = Chunked(arena, BF16, 16, T)
            for j in range(16):
                wl = []
                for i in range(4):
                    wl.append((ws.next(("win", l, g, "mg", j, i), look=8), ws.next(("bp", l, g, j, i), look=8)))
                for th in range(NH):
                    sl = (th * 512, th * 512 + 512)
                    pbufs = []
                    for i in range(4):
                        (gb_, gw), (bb_, bw) = wl[i]
                        pg = proj(gw, 128, th, hT)
                        po = ps.alloc()
                        cx.mm(po.v(), [(sub(bw, bw.ap[:, c, :]), ys[i].ch(c, *sl)) for c in range(8)])
                        sg = arena.alloc(2048)
                        cx.actf(sg.v(F32, 0, 512), pg.v(), AF.Sigmoid)
                        cx.tt(sg.v(F32, 0, 512), po.v(), sg.v(F32, 0, 512), ALU.mult)
                        pbufs.append(sg)
                    cx.tt(pbufs[0].v(F32, 0, 512), pbufs[0].v(F32, 0, 512), pbufs[1].v(F32, 0, 512), ALU.add)
                    cx.tt(pbufs[2].v(F32, 0, 512), pbufs[2].v(F32, 0, 512), pbufs[3].v(F32, 0, 512), ALU.add)
                    cx.tt(mg.ch(j, *sl), pbufs[0].v(F32, 0, 512), pbufs[2].v(F32, 0, 512), ALU.add)
                    for b_ in pbufs:
                        b_.free()
                for (gb_, _), (bb_, _) in wl:
                    gb_.free()
                    bb_.free()
            for b_ in (hT, yA, yB, yC, yD):
                b_.free()
            if l == 0 and g == 0:
                dump("mg", mg.all(), [128, 16, T])

            pre_o()
            gbc = arena.alloc(D * 4)
            cx.dma(cx.qs, gbc.v(F32, 0, D), bass.AP(gate_d[l].tensor, cnd * D, [[0, 128], [1, D]]), reads=[gate_t[l]])
            xrs = [[arena.alloc(2048) for _ in range(NT)] for _ in range(2)]
            yos = [arena.alloc(2048) for _ in range(6)]
            n_need, n_done = 10, [0]

            def load_res(cb):
                for tt in range(NT):
                    cx.dma(cx.qs, xrs[cb % 2][tt].v(F32, 0, 512), xsrc[tt * 128:(tt + 1) * 128, cb * 512:(cb + 1) * 512], reads=xsrc_t[tt])

            load_res(0)
            for cb in range(4):
                wb, wv = ws.next(("wout", l, g, cb), look=3)
                if cb + 1 < 4:
                    load_res(cb + 1)
                for tt in range(NT):
                    pb = ps.alloc()
                    cx.mm(pb.v(), [(mg.ch(j, tt * 128, tt * 128 + 128), sub(wv, wv.ap[:, j, :])) for j in range(16)])
                    yo = yos[(cb * NT + tt) % len(yos)]
                    xr = xrs[cb % 2][tt]
                    cx.tt(yo.v(F32, 0, 512), pb.v(), gbc.v(F32, cb * 512, cb * 512 + 512), ALU.mult)
                    cx.tt(yo.v(F32, 0, 512), yo.v(F32, 0, 512), xr.v(F32, 0, 512), ALU.add)
                    cx.dma(cx.qs, ydst[tt * 128:(tt + 1) * 128, cb * 512:(cb + 1) * 512], yo.v(F32, 0, 512), writes=[ydst_t[tt][cb]])
                    step_o = cb * NT + tt + 1
                    while n_done[0] * (4 * NT) < step_o * n_need:
                        tick_o()
                        n_done[0] += 1
                wb.free()
            for r_ in xrs[0] + xrs[1] + yos:
                r_.free()
            gbc.free()
            mg.free()

        def run_final(g, src, src_t, dst, out_t):
            NT = TG[g] // 128
            fb = arena.alloc(D * 4)
            fv = fb.v(F32, 0, D)
            cx.dma(cx.qs, fv, bass.AP(fnw.tensor, 0, [[0, 128], [1, D]]))
            xbs = [arena.alloc(8192) for _ in range(3)]

            def load(tt):
                cx.dma(cx.qs, xbs[tt % 3].v(F32, 0, D), src[tt * 128:(tt + 1) * 128, :], reads=src_t[tt])

            load(0)
            for tt in range(NT):
                if tt + 1 < NT:
                    load(tt + 1)
                xv = xbs[tt % 3].v(F32, 0, D)
                jb = arena.alloc(4096)
                ss = st.get()
                cx.actf(jb.v(BF16, 0, D), xv, AF.Square, accum=ss)
                jb.free()
                cx.actf(ss, ss, AF.Sqrt, bias=epsT.v(), scale=1.0 / D)
                cx.recip(ss, ss)
                cx.stt(xv, xv, ss, fv, ALU.mult, ALU.mult)
                cx.dma(cx.qs, dst[tt * 128:(tt + 1) * 128, :], xv, writes=[out_t])
                yield
            for b_ in xbs + [fb]:
                b_.free()

        kv_t = TObj()
        out_t = TObj()
        scr_t = [[[[TObj() for _ in range(4)] for _ in range(8)] for _ in range(2)] for _ in range(2)]
        none_t = [[] for _ in range(8)]

        def drain(gen):
            for _ in gen:
                pass

        def srcs(l, g):
            return (xin[g], none_t) if l == 0 else (scr[0][g], scr_t[0][g])

        try:
            ada0, ada1 = run_ada(0), run_ada(1)
            for _ in range(8):
                next(ada0)
            order = [(0, 0), (0, 1), (1, 0), (1, 1)]
            gN = phaseN(0, 0, *srcs(0, 0))
            hT = next(gN)
            drain(gN)
            fin0 = None
            for idx, (l, g) in enumerate(order):
                bg_gens = {(0, 0): [ada0], (0, 1): [ada1], (1, 0): [], (1, 1): []}[(l, g)]
                if (l, g) == (1, 1):
                    fin0 = run_final(0, scr[1][0], scr_t[1][0], youts[0], out_t)
                    bg_gens = [fin0]

                def tick(bg_gens=bg_gens):
                    for gen in bg_gens:
                        if next(gen, "done") != "done":
                            return

                nxt = {}

                def pre_o(idx=idx, bg_gens=bg_gens, nxt=nxt):
                    for gen in bg_gens:
                        drain(gen)
                    if idx + 1 < len(order):
                        ln, gn = order[idx + 1]
                        nxt["gen"] = phaseN(ln, gn, *srcs(ln, gn))
                        nxt["hT"] = next(nxt["gen"])

                def tick_o(nxt=nxt):
                    if "gen" in nxt:
                        next(nxt["gen"], None)

                src, src_t = srcs(l, g)
                run_group(l, g, hT, src, src_t, scr[l][g], scr_t[l][g], kv_os[g], tick, pre_o, tick_o)
                if "gen" in nxt:
                    drain(nxt["gen"])
                    hT = nxt["hT"]
            drain(run_final(1, scr[1][1], scr_t[1][1], youts[1], out_t))
        except StopBuild:
            pass
        for t in [kv_t, out_t]:
            for ev in t.w.values():
                cx.sp.wait(ev)
        for q in (cx.qs, cx.qg):
            for ev in q.ev:
                if ev is not None:
                    cx.sp.wait(ev)
        build_program.peak = arena.peak * Arena.G * 4
        build_program.plan = list(ws.rec)
        build_program.counts = dict(pe=cx.pe.count, act=cx.act.count, dve=cx.dve.count, pool=cx.pool.count, qs=cx.qs.n, qg=cx.qg.n, qs_cnt=cx.qs.cnt, qg_cnt=cx.qg.cnt)
    return nc, dbg_out


def _host_consts():
    consts = np.zeros((128, 384), np.float32)
    consts[:, 0:128] = np.eye(128, dtype=np.float32)
    consts[:, 128:256] = 1.0
    return consts


def _rope_tables(identity):
    n = 1024
    sc = np.float32(192.0 ** -0.5)
    if identity:
        one, zero = np.ones((64, n), np.float32), np.zeros((64, n), np.float32)
        return np.stack([one * sc, zero, one, zero]).astype(np.float32)
    pos = np.arange(n)
    r = (pos // 64).astype(np.float32)
    c = (pos % 64).astype(np.float32)
    inv = (np.float32(10000.0) ** (-np.arange(0, 32, 2, dtype=np.float32) / np.float32(32))).astype(np.float32)
    ang = np.stack([r[:, None] * inv, c[:, None] * inv], axis=1).astype(np.float32)
    cos, sin = np.cos(ang).astype(np.float32), np.sin(ang).astype(np.float32)
    cosT = np.zeros((64, n), np.float32)
    sinT = np.zeros((64, n), np.float32)
    for a in range(2):
        for hf in range(2):
            rows = slice(a * 32 + hf * 16, a * 32 + hf * 16 + 16)
            cosT[rows] = cos[:, a, :].T
            sinT[rows] = (-sin[:, a, :].T) if hf == 0 else sin[:, a, :].T
    return np.stack([cosT * sc, sinT * sc, cosT, sinT]).astype(np.float32)


def _invc(L, n):
    out = np.zeros((4, n), np.float32)
    t = np.arange(L)
    for wi, w in enumerate((2, 4, 8, 16)):
        lo = np.clip(t - w // 2, 0, L)
        hi = np.clip(t - w // 2 + w, 0, L)
        v = (1.0 / (hi - lo).astype(np.float32)).astype(np.float32)
        out[wi] = np.tile(v, n // L)
    return out


def _kmask(separate):
    m = np.zeros((4, 1280), np.float32)
    if separate:
        m[:] = -30000.0
        for s in range(4):
            m[s, 256 + s * 256:256 + (s + 1) * 256] = 0.0
    return m


def _pvec(norm_w, q_norm_w, kv_norm_w, conv3_w, conv3_b, pool_scale, dw_w, dw_b, cln_w, cln_b):
    out = np.zeros((2, 128, NPV), np.float32)
    fm = lambda v: np.ascontiguousarray(v.reshape(-1, 128).T)
    for l in range(2):
        o = out[l]
        o[:, PV["normw"]:PV["normw"] + 16] = fm(norm_w[l])
        o[:, PV["qnw"]:PV["qnw"] + 4] = fm(q_norm_w[l])
        o[:, PV["kvnw"]:PV["kvnw"] + 2] = fm(kv_norm_w[l])
        o[:, PV["c3w"]:PV["c3w"] + 24] = conv3_w[l].reshape(3, 8, 128).transpose(2, 1, 0).reshape(128, 24)
        o[:, PV["c3b"]:PV["c3b"] + 8] = fm(conv3_b[l])
        o[:, PV["pscale"]:PV["pscale"] + 8] = fm(pool_scale[l])
        o[:, PV["dww"]:PV["dww"] + 248] = dw_w[l].reshape(31, 8, 128).transpose(2, 1, 0).reshape(128, 248)
        o[:, PV["dwb"]:PV["dwb"] + 8] = fm(dw_b[l])
        o[:, PV["clnw"]:PV["clnw"] + 8] = fm(cln_w[l])
        o[:, PV["clnb"]:PV["clnb"] + 8] = fm(cln_b[l])
    return out


_CACHE = {}


def kernel(x_prompt, x_sample, cache_kv, c, c_ctx, w_ada, b_ada, norm_w, w_in, q_norm_w, w_qb, kv_norm_w, w_kvb,
           conv3_w, conv3_b, pool_w, pool_scale, dw_w, dw_b, cln_w, cln_b, w_bproj, w_out, final_norm_w, _dbg=None, _stop=None, _ncores=8):
    A = lambda v: np.ascontiguousarray(np.asarray(v, dtype=np.float32))
    x_prompt, x_sample, cache_kv, c, c_ctx = map(A, (x_prompt, x_sample, cache_kv, c, c_ctx))
    key = (tuple(sorted(_dbg)) if _dbg else None, _stop)
    if key not in _CACHE:
        build_program(_dbg, _stop)
        _CACHE[key] = build_program(_dbg, _stop, build_program.plan)
    nc, dbg_out = _CACHE[key]
    consts = _host_consts()
    sel = np.zeros((2, 256), np.float32)
    sel[0, 0:128] = 1.0
    sel[1, 128:256] = 1.0
    qsel = np.zeros((4, 1024), np.float32)
    for r_ in range(4):
        qsel[r_, r_ * 256:(r_ + 1) * 256] = 1.0
    shared = dict(w_ada=A(w_ada), b_ada=A(b_ada), w_in=A(w_in), w_qb=A(w_qb), w_kvb=A(w_kvb), pool_w=A(pool_w),
                  w_bproj=A(w_bproj), w_out=A(w_out),
                  pvec=_pvec(*map(A, (norm_w, q_norm_w, kv_norm_w, conv3_w, conv3_b, pool_scale, dw_w, dw_b, cln_w, cln_b))),
                  fnw=A(final_norm_w).reshape(1, D), consts=consts, selrow_h=sel, invc1=_invc(256, 512), qsel=qsel)
    rope_s, rope_i = _rope_tables(False), _rope_tables(True)
    invc_s, invc_p = _invc(1024, 1024), _invc(256, 1024)
    km_s, km_p = _kmask(False), _kmask(True)
    g0_prompts = {i: list(range(8 + 4 * (i - 4), 12 + 4 * (i - 4))) for i in range(4, 8)}
    g1_prompts = {i: ([2 * i, 2 * i + 1] if i < 4 else [24 + 2 * (i - 4), 25 + 2 * (i - 4)]) for i in range(8)}
    in_maps = []
    for i in range(_ncores):
        m = dict(shared)
        if i < 4:
            cond0 = c[i]
            m["x0"] = np.ascontiguousarray(x_sample[i])
            m["cache"] = np.ascontiguousarray(cache_kv[i])
            m["rope"], m["invc0"], m["kmask"] = rope_s, invc_s, km_s
            m["flag"] = np.ones((128, 1), np.float32)
        else:
            cond0 = c_ctx
            m["x0"] = np.ascontiguousarray(x_prompt[g0_prompts[i]].reshape(1024, D))
            m["cache"] = np.zeros((2, 256, 320), np.float32)
            m["rope"], m["invc0"], m["kmask"] = rope_i, invc_p, km_p
            m["flag"] = np.zeros((128, 1), np.float32)
        m["x1"] = np.ascontiguousarray(x_prompt[g1_prompts[i]].reshape(512, D))
        cond = np.stack([cond0, c_ctx], axis=1)
        m["condT"] = np.ascontiguousarray(cond.reshape(16, 128, 2).transpose(1, 0, 2).reshape(128, 32))
        in_maps.append(m)
    res = run_bass_kernel_spmd(nc, in_maps, core_ids=list(range(_ncores)))
    R = res.results
    if _dbg:
        kernel.dbg = {n: [R[i]["dbg_" + n] for i in range(_ncores)] for n in dbg_out}
        if _stop:
            return None
    if _ncores < 8:
        return R
    y_prompt = np.zeros((32, 256, D), np.float32)
    y_sample = np.zeros((4, 1024, D), np.float32)
    new_kv = np.zeros((32, 2, 256, 320), np.float32)
    for i in range(8):
        y_prompt[g1_prompts[i]] = R[i]["y1"].reshape(2, 256, D)
        new_kv[g1_prompts[i]] = R[i]["kv_o1"]
        if i < 4:
            y_sample[i] = R[i]["y0"]
        else:
            y_prompt[g0_prompts[i]] = R[i]["y0"].reshape(4, 256, D)
            new_kv[g0_prompts[i]] = R[i]["kv_o0"]
    return (y_prompt.astype(np.float32), y_sample.astype(np.float32), new_kv.astype(np.float32))
```
